# Optimizing a Trainium2 kernel written in Bass

```python
import jax, jax.numpy as jnp
from jax import lax
import numpy as np

D_MODEL = 2048
BATCH = 2
SEQ = 8192
DEPTH = 2
DEC_BATCH = 32
DEC_SEQ = 64
PAST_LEN = 4096

CHUNK = 64
N_BRANCH = 4
BRANCH_WIDTH = D_MODEL // 4
N_HEADS = 4
HEAD_DV = BRANCH_WIDTH // N_HEADS
GLA_DK = HEAD_DV // 2
GLA_LOW_RANK = 16
GLA_GATE_NORM = 16.0
HG_DK = HEAD_DV
GDN_DK = HEAD_DV
GDN_CONV = 4
GDN_QKV = 3 * N_HEADS * GDN_DK
RET_DK = HEAD_DV
ROPE_BASE = 10000.0
D_FF = 128 * ((8 * D_MODEL // 3 + 127) // 128)
FFN_CONV = 3
EPS = 1e-6

IN_SIZES = (
    N_HEADS * GLA_DK, N_HEADS * GLA_DK, BRANCH_WIDTH, BRANCH_WIDTH, GLA_LOW_RANK,
    N_HEADS * HG_DK, N_HEADS * HG_DK, BRANCH_WIDTH, BRANCH_WIDTH,
    GDN_QKV, BRANCH_WIDTH, N_HEADS, N_HEADS,
    N_HEADS * RET_DK, N_HEADS * RET_DK, BRANCH_WIDTH, BRANCH_WIDTH,
    N_BRANCH * D_MODEL,
)
N_IN = sum(IN_SIZES)
IN_SPLITS = tuple(int(s) for s in np.cumsum(IN_SIZES)[:-1])

kernel_name = "hybrid_gated_linear_stream_encoder_step"


def rms_norm(x, g):
    xf = x.astype(jnp.float32)
    y = xf * lax.rsqrt(jnp.mean(xf * xf, axis=-1, keepdims=True) + EPS)
    return (y * g.astype(jnp.float32)).astype(x.dtype)


def head_layer_norm(x, g, b):
    xf = x.astype(jnp.float32)
    mu = jnp.mean(xf, axis=-1, keepdims=True)
    xc = xf - mu
    y = xc * lax.rsqrt(jnp.mean(xc * xc, axis=-1, keepdims=True) + EPS)
    return (y * g.astype(jnp.float32) + b.astype(jnp.float32)).astype(x.dtype)


def l2_normalize(x):
    xf = x.astype(jnp.float32)
    return (xf * lax.rsqrt(jnp.sum(xf * xf, axis=-1, keepdims=True) + EPS)).astype(x.dtype)


def split_heads(x, d):
    return x.reshape(x.shape[:-1] + (x.shape[-1] // d, d))


def merge_heads(x):
    return x.reshape(x.shape[:-2] + (-1,))


def to_chunks(x, L):
    B, T = x.shape[:2]
    x = x.astype(jnp.float32).reshape((B, T // L, L) + x.shape[2:])
    return jnp.swapaxes(jnp.swapaxes(x, 0, 1), 2, 3)


def from_chunks(o):
    o = jnp.swapaxes(jnp.swapaxes(o, 2, 3), 0, 1)
    return o.reshape((o.shape[0], -1) + o.shape[3:])


def rotary(x, pos0):
    T, K = x.shape[1], x.shape[-1]
    pos = (pos0 + jnp.arange(T)).astype(jnp.float32)
    inv = 1.0 / (ROPE_BASE ** (jnp.arange(0, K, 2, dtype=jnp.float32) / K))
    ang = pos[:, None] * inv[None, :]
    cos, sin = jnp.cos(ang)[None, :, None, :], jnp.sin(ang)[None, :, None, :]
    xf = x.astype(jnp.float32)
    x1, x2 = xf[..., : K // 2], xf[..., K // 2:]
    return jnp.concatenate([x1 * cos - x2 * sin, x1 * sin + x2 * cos], axis=-1).astype(x.dtype)


def causal_dwconv(x, hist, w):
    W, T = w.shape[0], x.shape[1]
    xe = jnp.concatenate([hist.astype(x.dtype), x], axis=1)
    y = sum(xe[:, j:j + T] * w[j] for j in range(W))
    return y, xe[:, -(W - 1):]


def chunk_gla(q, k, v, log_f, s0):
    T = q.shape[1]
    L = min(CHUNK, T)
    incl = jnp.tril(jnp.ones((L, L), dtype=bool))

    def step(s, blk):
        qb, kb, vb, gb = blk
        b = jnp.cumsum(gb, axis=2)
        dec = jnp.exp(jnp.where(incl[:, :, None], b[:, :, :, None, :] - b[:, :, None, :, :], -jnp.inf))
        att = jnp.einsum("bhtk,bhsk,bhtsk->bhts", qb, kb, dec)
        o = jnp.einsum("bhts,bhsv->bhtv", att, vb) + jnp.einsum("bhtk,bhkv->bhtv", qb * jnp.exp(b), s)
        b_end = b[:, :, -1:, :]
        s = jnp.exp(b_end[:, :, 0, :])[..., None] * s + jnp.einsum("bhsk,bhsv->bhkv", kb * jnp.exp(b_end - b), vb)
        return s, o

    s, o = lax.scan(step, s0.astype(jnp.float32),
                    (to_chunks(q, L), to_chunks(k, L), to_chunks(v, L), to_chunks(log_f, L)))
    return from_chunks(o).astype(v.dtype), s.astype(s0.dtype)


def chunk_retention(q, k, v, log_g, s0):
    T = q.shape[1]
    L = min(CHUNK, T)
    incl = jnp.tril(jnp.ones((L, L), dtype=bool))

    def step(s, blk):
        qb, kb, vb, gb = blk
        b = jnp.cumsum(gb, axis=-1)
        dec = jnp.exp(jnp.where(incl, b[..., :, None] - b[..., None, :], -jnp.inf))
        att = jnp.einsum("bhtk,bhsk->bhts", qb, kb) * dec
        o = jnp.einsum("bhts,bhsv->bhtv", att, vb) + jnp.exp(b)[..., None] * jnp.einsum("bhtk,bhkv->bhtv", qb, s)
        b_end = b[..., -1:]
        s = jnp.exp(b_end)[..., None] * s + jnp.einsum("bhsk,bhsv->bhkv", kb * jnp.exp(b_end - b)[..., None], vb)
        return s, o

    s, o = lax.scan(step, s0.astype(jnp.float32),
                    (to_chunks(q, L), to_chunks(k, L), to_chunks(v, L), to_chunks(log_g, L)))
    return from_chunks(o).astype(v.dtype), s.astype(s0.dtype)


def chunk_gated_delta(q, k, v, beta, log_a, s0):
    T = q.shape[1]
    L = min(CHUNK, T)
    incl = jnp.tril(jnp.ones((L, L), dtype=bool))
    strict = jnp.tril(jnp.ones((L, L), dtype=bool), k=-1)

    def step(s, blk):
        qb, kb, vb, bb, gb = blk
        b = jnp.cumsum(gb, axis=-1)
        dec = jnp.exp(jnp.where(incl, b[..., :, None] - b[..., None, :], -jnp.inf))
        m = jnp.where(strict, jnp.einsum("bhtk,bhsk->bhts", kb, kb) * dec, 0.0) * bb[..., None]
        eb = jnp.exp(b)[..., None]
        rhs = bb[..., None] * (vb - eb * jnp.einsum("bhtk,bhkv->bhtv", kb, s))
        u = lax.linalg.triangular_solve(m, rhs, left_side=True, lower=True, unit_diagonal=True)
        att = jnp.einsum("bhtk,bhsk->bhts", qb, kb) * dec
        o = jnp.einsum("bhts,bhsv->bhtv", att, u) + eb * jnp.einsum("bhtk,bhkv->bhtv", qb, s)
        b_end = b[..., -1:]
        s = jnp.exp(b_end)[..., None] * s + jnp.einsum("bhsk,bhsv->bhkv", kb * jnp.exp(b_end - b)[..., None], u)
        return s, o

    s, o = lax.scan(step, s0.astype(jnp.float32),
                    (to_chunks(q, L), to_chunks(k, L), to_chunks(v, L), to_chunks(beta, L), to_chunks(log_a, L)))
    return from_chunks(o).astype(v.dtype), s.astype(s0.dtype)


def trunk_layer(x, states, p, lb, pos0):
    s_gla, s_hg, s_gdn, s_ret, c_gdn, c_ffn = states
    B, T, _ = x.shape
    f32 = jnp.float32
    h = rms_norm(x, p["norm_mix_g"])
    (gla_q, gla_k, gla_v, gla_g, gla_lr, hg_q, hg_f, hg_i, hg_g,
     gdn_qkv, gdn_z, gdn_b, gdn_a, ret_q, ret_k, ret_v, ret_g, merge) = jnp.split(h @ p["w_in"], IN_SPLITS, axis=-1)

    log_gk = jax.nn.log_sigmoid((gla_lr @ p["gla_w_gk"] + p["gla_b_gk"]).astype(f32)) / GLA_GATE_NORM
    o, s_gla = chunk_gla(split_heads(gla_q, GLA_DK) * GLA_DK ** -0.5, split_heads(gla_k, GLA_DK),
                         split_heads(gla_v, HEAD_DV), split_heads(log_gk, GLA_DK), s_gla)
    o_gla = merge_heads(rms_norm(o, p["gla_norm_g"]) * jax.nn.silu(split_heads(gla_g, HEAD_DV)))

    f = lb + (1.0 - lb) * jax.nn.sigmoid(hg_f.astype(f32))
    o, s_hg = chunk_gla(split_heads(jax.nn.silu(hg_q), HG_DK), split_heads((1.0 - f).astype(x.dtype), HG_DK),
                        split_heads(hg_i, HEAD_DV), split_heads(jnp.log(f), HG_DK), s_hg)
    o_hg = merge_heads(rms_norm(o, p["hgrn_norm_g"]) * jax.nn.silu(split_heads(hg_g, HEAD_DV)))

    qkv, c_gdn = causal_dwconv(gdn_qkv, c_gdn, p["gdn_conv_w"])
    dq, dk, dv = jnp.split(jax.nn.silu(qkv), 3, axis=-1)
    beta = jax.nn.sigmoid(gdn_b.astype(f32))
    log_a = -jnp.exp(p["gdn_a_log"].astype(f32)) * jax.nn.softplus((gdn_a + p["gdn_dt_bias"]).astype(f32))
    o, s_gdn = chunk_gated_delta(l2_normalize(split_heads(dq, GDN_DK)) * GDN_DK ** -0.5,
                                 l2_normalize(split_heads(dk, GDN_DK)), split_heads(dv, HEAD_DV), beta, log_a, s_gdn)
    o_gdn = merge_heads(rms_norm(o, p["gdn_norm_g"]) * jax.nn.silu(split_heads(gdn_z, HEAD_DV)))

    log_gamma = jnp.broadcast_to(jnp.log(1.0 - 2.0 ** (-5.0 - jnp.arange(N_HEADS, dtype=f32))), (B, T, N_HEADS))
    o, s_ret = chunk_retention(rotary(split_heads(ret_q, RET_DK), pos0),
                               rotary(split_heads(ret_k, RET_DK), pos0) * RET_DK ** -0.5,
                               split_heads(ret_v, HEAD_DV), log_gamma, s_ret)
    o_ret = merge_heads(head_layer_norm(o, p["ret_norm_g"], p["ret_norm_b"]) * jax.nn.silu(split_heads(ret_g, HEAD_DV)))

    branches = jnp.einsum("btnc,ncd->btnd", jnp.stack([o_gla, o_hg, o_gdn, o_ret], axis=2), p["w_branch"])
    gates = jax.nn.sigmoid(merge.reshape(B, T, N_BRANCH, D_MODEL))
    x = x + jnp.sum(gates * branches, axis=2) @ p["w_out"]

    h2 = rms_norm(x, p["norm_ffn_g"])
    a, u = jnp.split(h2 @ p["w_ffn_in"], 2, axis=-1)
    a, c_ffn = causal_dwconv(a, c_ffn, p["ffn_conv_w"])
    x = x + (jax.nn.silu(a + p["ffn_conv_b"]) * u) @ p["w_ffn_out"]
    return x, (s_gla, s_hg, s_gdn, s_ret, c_gdn, c_ffn)


def setup_inputs(seed: int = 0) -> dict:
    key = jax.random.key(seed)
    ks = jax.random.split(key, 32)
    nrm = jax.random.normal
    f32 = jnp.float32
    H = N_HEADS
    return {
        "x_prompt": nrm(ks[0], (BATCH, SEQ, D_MODEL), f32),
        "x_sample": nrm(ks[1], (DEC_BATCH, DEC_SEQ, D_MODEL), f32),
        "state_gla": 0.1 * nrm(ks[2], (DEPTH, DEC_BATCH, H, GLA_DK, HEAD_DV), f32),
        "state_hgrn": 0.1 * nrm(ks[3], (DEPTH, DEC_BATCH, H, HG_DK, HEAD_DV), f32),
        "state_gdn": 0.1 * nrm(ks[4], (DEPTH, DEC_BATCH, H, GDN_DK, HEAD_DV), f32),
        "state_ret": 0.1 * nrm(ks[5], (DEPTH, DEC_BATCH, H, RET_DK, HEAD_DV), f32),
        "cache_gdn_conv": nrm(ks[6], (DEPTH, DEC_BATCH, GDN_CONV - 1, GDN_QKV), f32),
        "cache_ffn_conv": nrm(ks[7], (DEPTH, DEC_BATCH, FFN_CONV - 1, D_FF), f32),
        "norm_mix_g": 1.0 + 0.02 * nrm(ks[8], (DEPTH, D_MODEL), f32),
        "w_in": nrm(ks[9], (DEPTH, D_MODEL, N_IN), f32) * D_MODEL ** -0.5,
        "gla_w_gk": nrm(ks[10], (DEPTH, GLA_LOW_RANK, H * GLA_DK), f32) * GLA_LOW_RANK ** -0.5,
        "gla_b_gk": 0.1 * nrm(ks[11], (DEPTH, H * GLA_DK), f32),
        "gla_norm_g": 1.0 + 0.02 * nrm(ks[12], (DEPTH, HEAD_DV), f32),
        "hgrn_lb_logits": 0.1 * nrm(ks[13], (DEPTH, H * HG_DK), f32),
        "hgrn_norm_g": 1.0 + 0.02 * nrm(ks[14], (DEPTH, HEAD_DV), f32),
        "gdn_conv_w": nrm(ks[15], (DEPTH, GDN_CONV, GDN_QKV), f32) * GDN_CONV ** -0.5,
        "gdn_a_log": jnp.log(jax.random.uniform(ks[16], (DEPTH, H), f32, 1.0, 16.0)),
        "gdn_dt_bias": jnp.log(jnp.expm1(jax.random.uniform(ks[17], (DEPTH, H), f32, 1e-3, 0.1))),
        "gdn_norm_g": 1.0 + 0.02 * nrm(ks[18], (DEPTH, HEAD_DV), f32),
        "ret_norm_g": 1.0 + 0.02 * nrm(ks[19], (DEPTH, HEAD_DV), f32),
        "ret_norm_b": 0.02 * nrm(ks[20], (DEPTH, HEAD_DV), f32),
        "w_branch": nrm(ks[21], (DEPTH, N_BRANCH, BRANCH_WIDTH, D_MODEL), f32) * BRANCH_WIDTH ** -0.5,
        "w_out": nrm(ks[22], (DEPTH, D_MODEL, D_MODEL), f32) * D_MODEL ** -0.5,
        "norm_ffn_g": 1.0 + 0.02 * nrm(ks[23], (DEPTH, D_MODEL), f32),
        "w_ffn_in": nrm(ks[24], (DEPTH, D_MODEL, 2 * D_FF), f32) * D_MODEL ** -0.5,
        "ffn_conv_w": nrm(ks[25], (DEPTH, FFN_CONV, D_FF), f32) * FFN_CONV ** -0.5,
        "ffn_conv_b": 0.02 * nrm(ks[26], (DEPTH, D_FF), f32),
        "w_ffn_out": nrm(ks[27], (DEPTH, D_FF, D_MODEL), f32) * D_FF ** -0.5,
        "norm_final_g": 1.0 + 0.02 * nrm(ks[28], (D_MODEL,), f32),
    }


def reference(x_prompt, x_sample, state_gla, state_hgrn, state_gdn, state_ret, cache_gdn_conv, cache_ffn_conv,
              norm_mix_g, w_in, gla_w_gk, gla_b_gk, gla_norm_g, hgrn_lb_logits, hgrn_norm_g,
              gdn_conv_w, gdn_a_log, gdn_dt_bias, gdn_norm_g, ret_norm_g, ret_norm_b,
              w_branch, w_out, norm_ffn_g, w_ffn_in, ffn_conv_w, ffn_conv_b, w_ffn_out, norm_final_g):
    lb_soft = jax.nn.softmax(hgrn_lb_logits.astype(jnp.float32), axis=0)
    lb_all = jnp.cumsum(lb_soft, axis=0) - lb_soft[0]

    dt = x_prompt.dtype
    prompt_states = (
        jnp.zeros((BATCH, N_HEADS, GLA_DK, HEAD_DV), dt),
        jnp.zeros((BATCH, N_HEADS, HG_DK, HEAD_DV), dt),
        jnp.zeros((BATCH, N_HEADS, GDN_DK, HEAD_DV), dt),
        jnp.zeros((BATCH, N_HEADS, RET_DK, HEAD_DV), dt),
        jnp.zeros((BATCH, GDN_CONV - 1, GDN_QKV), dt),
        jnp.zeros((BATCH, FFN_CONV - 1, D_FF), dt),
    )
    xp, xs = x_prompt, x_sample
    new_p, new_s = [], []
    for l in range(DEPTH):
        p = {
            "norm_mix_g": norm_mix_g[l], "w_in": w_in[l], "gla_w_gk": gla_w_gk[l], "gla_b_gk": gla_b_gk[l],
            "gla_norm_g": gla_norm_g[l], "hgrn_norm_g": hgrn_norm_g[l], "gdn_conv_w": gdn_conv_w[l],
            "gdn_a_log": gdn_a_log[l], "gdn_dt_bias": gdn_dt_bias[l], "gdn_norm_g": gdn_norm_g[l],
            "ret_norm_g": ret_norm_g[l], "ret_norm_b": ret_norm_b[l], "w_branch": w_branch[l], "w_out": w_out[l],
            "norm_ffn_g": norm_ffn_g[l], "w_ffn_in": w_ffn_in[l], "ffn_conv_w": ffn_conv_w[l],
            "ffn_conv_b": ffn_conv_b[l], "w_ffn_out": w_ffn_out[l],
        }
        xp, st_p = trunk_layer(xp, prompt_states, p, lb_all[l], 0)
        xs, st_s = trunk_layer(xs, (state_gla[l], state_hgrn[l], state_gdn[l], state_ret[l],
                                    cache_gdn_conv[l], cache_ffn_conv[l]), p, lb_all[l], PAST_LEN)
        new_p.append(st_p)
        new_s.append(st_s)
    y_prompt = rms_norm(xp, norm_final_g)
    y_sample = rms_norm(xs, norm_final_g)
    p_gla, p_hgrn, p_gdn, p_ret, p_gdn_conv, p_ffn_conv = (jnp.stack([st[i] for st in new_p]) for i in range(6))
    s_gla, s_hgrn, s_gdn, s_ret, s_gdn_conv, s_ffn_conv = (jnp.stack([st[i] for st in new_s]) for i in range(6))
    return (y_prompt, y_sample, p_gla, p_hgrn, p_gdn, p_ret, p_gdn_conv, p_ffn_conv,
            s_gla, s_hgrn, s_gdn, s_ret, s_gdn_conv, s_ffn_conv)
```

```python
import math
from contextlib import ExitStack

import numpy as np
import concourse.bass as bass
import concourse.mybir as mybir
from concourse.bass_utils import run_bass_kernel_spmd

F32 = mybir.dt.float32
BF16 = mybir.dt.bfloat16
AF = mybir.ActivationFunctionType
ALU = mybir.AluOpType

ENGS = ("pe", "act", "dve", "pool", "sp")

D = 2048
KT = 16
NIN = 15896
DFF = 5504
FKT = 43
H = 4
EPS = 1e-6
C_GLA_Q, C_GLA_K, C_GLA_V, C_GLA_G, C_GLA_LR = 0, 256, 512, 1024, 1536
C_HG_Q, C_HG_F, C_HG_I, C_HG_G = 1552, 2064, 2576, 3088
C_GDN_QKV, C_GDN_Z, C_GDN_B = 3600, 5136, 5648
C_RET_Q, C_RET_K, C_RET_V, C_RET_G = 5656, 6168, 6680, 7192
C_MERGE = 7704
WSLOT = 5632


class Op:
    __slots__ = ("eng", "emit", "waits", "sig", "is_dma")


class Prog:
    def __init__(self, nc, stack, n_dma_sems=(("sp", 24), ("pool", 24), ("act", 8))):
        self.nc = nc
        self.ops = {e: [] for e in ENGS}
        self.res = {}
        self.esem = {e: stack.enter_context(nc.semaphore("s_" + e)) for e in ENGS}
        self.ecount = {e: 0 for e in ENGS}
        self.dsem, self.dstate, self.drr = {}, {}, {}
        for e, n in n_dma_sems:
            self.dsem[e] = [stack.enter_context(nc.semaphore("d_%s%d" % (e, i))) for i in range(n)]
            self.dstate[e] = [[0, None] for _ in range(n)]
            self.drr[e] = 0
        self.waited = {e: {} for e in ENGS}
        self.frontier = {}
        self.groups = {}
        self.nops = 0

    def _need(self, op, src):
        if src is None or src is op:
            return
        if src.eng == op.eng and not src.is_dma:
            return
        sem, val = src.sig
        w = self.waited[op.eng]
        k = id(sem)
        if w.get(k, -1) >= val:
            return
        w[k] = val
        op.waits.append((sem, val))

    def _get(self, k):
        r = self.res.get(k)
        if r is None:
            r = [None, []]
            if isinstance(k, tuple) and k and k[0] in self.frontier:
                r[1] = list(self.frontier[k[0]])
            self.res[k] = r
            if isinstance(k, tuple) and k:
                self.groups.setdefault(k[0], set()).add(k)
        return r

    def fence(self, groups):
        if isinstance(groups, str):
            groups = (groups,)
        fr = {}
        for group in groups:
            for k in self.groups.get(group, ()):
                r = self.res.pop(k)
                if r[0] is not None:
                    fr[id(r[0])] = r[0]
                for o in r[1]:
                    fr[id(o)] = o
            for o in self.frontier.get(group, ()):
                fr[id(o)] = o
        best = {}
        for o in fr.values():
            sem, val = o.sig
            b = best.get(id(sem))
            if b is None or b.sig[1] < val:
                best[id(sem)] = o
        for group in groups:
            self.frontier[group] = list(best.values())
            self.groups[group] = set()

    def add(self, eng, emit, reads=(), writes=(), is_dma=False):
        op = Op()
        op.eng, op.emit, op.waits, op.is_dma = eng, emit, [], is_dma
        self.nops += 1
        if is_dma:
            pool = self.dstate[eng]
            i = self.drr[eng]
            self.drr[eng] = (i + 1) % len(pool)
            st = pool[i]
            if st[1] is not None:
                sem, val = st[1].sig
                w = self.waited[eng]
                if w.get(id(sem), -1) < val:
                    w[id(sem)] = val
                    op.waits.append((sem, val))
            st[0] += 16
            st[1] = op
            op.sig = (self.dsem[eng][i], st[0])
        else:
            self.ecount[eng] += 1
            op.sig = (self.esem[eng], self.ecount[eng])
        pr = [k for k in reads if isinstance(k, tuple) and k[0] == "ps"]
        if pr:
            reads = [k for k in reads if not (isinstance(k, tuple) and k[0] == "ps")]
            writes = list(writes) + pr
        for k in reads:
            r = self._get(k)
            self._need(op, r[0])
            r[1].append(op)
        for k in writes:
            r = self._get(k)
            self._need(op, r[0])
            for rd in r[1]:
                self._need(op, rd)
            r[0] = op
            r[1] = []
        self.ops[eng].append(op)
        return op

    def emit_all(self, final_eng="sp"):
        nc = self.nc
        finals = []
        for e in ENGS:
            if self.ecount[e] > 0 and e != final_eng:
                finals.append((self.esem[e], self.ecount[e]))
        for e in self.dsem:
            for i, st in enumerate(self.dstate[e]):
                if st[0] > 0:
                    finals.append((self.dsem[e][i], st[0]))
        with nc.Block() as block:
            def mk(e):
                def body(eng):
                    for o in self.ops[e]:
                        for (sem, val) in o.waits:
                            eng.wait_ge(sem, val)
                        ins = o.emit(eng)
                        ins.then_inc(o.sig[0], 16 if o.is_dma else 1)
                    if e == final_eng:
                        for (sem, val) in finals:
                            eng.wait_ge(sem, val)
                return body
            block.tensor(mk("pe"))
            block.scalar(mk("act"))
            block.vector(mk("dve"))
            block.gpsimd(mk("pool"))
            block.sync(mk("sp"))


def const_layout(NH):
    lay, off = {}, 0
    for name, w in (("ident", 128), ("ones", 128), ("mask2", 128), ("nml", 128), ("nmu", 128),
                    ("sel", 8), ("onesel", 1024), ("rmask", NH), ("m47", 1),
                    ("reteb", 256), ("retenb", 256), ("retkd", 256)):
        lay[name] = (off, w)
        off += w
    return lay, off


def make_consts(NH):
    lay, cw = const_layout(NH)
    c = np.zeros((128, cw), np.float32)
    p = np.arange(128)[:, None]
    f = np.arange(128)[None, :]
    same = (p // 64) == (f // 64)
    c[:, lay["ident"][0]:lay["ident"][0] + 128] = np.eye(128)
    c[:, lay["ones"][0]:lay["ones"][0] + 128] = 1.0
    c[:, lay["mask2"][0]:lay["mask2"][0] + 128] = (same & (p <= f)).astype(np.float32)
    c[:, lay["nml"][0]:lay["nml"][0] + 128] = -(same & (f < p)).astype(np.float32)
    c[:, lay["nmu"][0]:lay["nmu"][0] + 128] = -(same & (p < f)).astype(np.float32)
    so = lay["sel"][0]
    for h in range(4):
        c[4 + h, so + h] = 1.0
        c[h, so + 4 + h] = 1.0
    oo = lay["onesel"][0]
    for h in range(4):
        c[4 + h, oo + h * 128:oo + (h + 1) * 128] = 1.0
        c[h, oo + (4 + h) * 128:oo + (5 + h) * 128] = 1.0
    ro = lay["rmask"][0]
    rm = np.ones(NH, np.float32)
    rm[::64] = 0.0
    c[:, ro:ro + NH] = rm[None, :]
    c[4:8, lay["m47"][0]] = 1.0
    t = np.arange(64, dtype=np.float64)
    for h in range(4):
        lg = math.log(1.0 - 2.0 ** (-5.0 - h))
        c[:, lay["reteb"][0] + h * 64:lay["reteb"][0] + (h + 1) * 64] = np.exp(lg * (t + 1))[None, :]
        c[:, lay["retenb"][0] + h * 64:lay["retenb"][0] + (h + 1) * 64] = (np.exp(-lg * (t + 1)) * 128 ** -0.5)[None, :]
        c[:, lay["retkd"][0] + h * 64:lay["retkd"][0] + (h + 1) * 64] = (np.exp(lg * (63 - t)) * 128 ** -0.5)[None, :]
    return c


def pcol_layout():
    lay, off = {}, 0
    def add(name, w):
        nonlocal off
        lay[name] = (off, w)
        off += w
    for l in range(2):
        add("nmix%d" % l, 16)
        add("nffn%d" % l, 16)
        add("bgk%d" % l, 2)
        add("gla_ng%d" % l, 1)
        add("hg_ng%d" % l, 1)
        add("gdn_ng%d" % l, 1)
        add("ret_ng%d" % l, 1)
        add("ret_nb%d" % l, 1)
        add("lbz%d" % l, 4)
        add("gcw%d" % l, 48)
        add("fcw%d" % l, 129)
        add("fcb%d" % l, 43)
        add("dtb%d" % l, 1)
        add("alog%d" % l, 1)
    add("nfin", 16)
    return lay, off


def cols(v):
    return np.ascontiguousarray(np.asarray(v, np.float32).reshape(-1, 128).T)


def make_pcols(inp):
    lay, n = pcol_layout()
    t = np.zeros((128, n), np.float32)
    def put(name, arr):
        o, w = lay[name]
        assert arr.shape == (128, w), (name, arr.shape, w)
        t[:, o:o + w] = arr
    for l in range(2):
        put("nmix%d" % l, cols(inp["norm_mix_g"][l]))
        put("nffn%d" % l, cols(inp["norm_ffn_g"][l]))
        put("bgk%d" % l, cols(inp["gla_b_gk"][l]))
        put("gla_ng%d" % l, cols(inp["gla_norm_g"][l]))
        put("hg_ng%d" % l, cols(inp["hgrn_norm_g"][l]))
        put("gdn_ng%d" % l, cols(inp["gdn_norm_g"][l]))
        put("ret_ng%d" % l, cols(inp["ret_norm_g"][l]))
        put("ret_nb%d" % l, cols(inp["ret_norm_b"][l]))
        put("lbz%d" % l, cols(inp["hgrn_lb_logits"][l]))
        put("gcw%d" % l, np.concatenate([cols(inp["gdn_conv_w"][l][j]) for j in range(4)], axis=1))
        put("fcw%d" % l, np.concatenate([cols(inp["ffn_conv_w"][l][j]) for j in range(3)], axis=1))
        put("fcb%d" % l, cols(inp["ffn_conv_b"][l]))
        a = np.zeros((128, 1), np.float32)
        a[4:8, 0] = np.asarray(inp["gdn_dt_bias"][l], np.float32)
        put("dtb%d" % l, a)
        a = np.zeros((128, 1), np.float32)
        a[4:8, 0] = np.asarray(inp["gdn_a_log"][l], np.float32)
        put("alog%d" % l, a)
    put("nfin", cols(inp["norm_final_g"]))
    return t


def rot_tables(TP, past_len):
    pos = np.concatenate([np.arange(TP), past_len + np.arange(64)]).astype(np.float32)
    inv = (1.0 / (np.float32(10000.0) ** (np.arange(0, 128, 2, dtype=np.float32) / np.float32(128)))).astype(np.float32)
    ang = (pos[None, :] * inv[:, None]).astype(np.float32)
    cos, sin = np.cos(ang).astype(np.float32), np.sin(ang).astype(np.float32)
    rc = np.concatenate([cos, cos], axis=0)
    rs = np.concatenate([-sin, sin], axis=0)
    return np.ascontiguousarray(rc), np.ascontiguousarray(rs)


class Builder:
    def __init__(self, cfg):
        self.cfg = cfg
        self.TP, self.NS, self.NT = cfg["TP"], cfg["NS"], cfg["NT"]
        self.DEPTH = cfg.get("DEPTH", 2)
        self.NH = min(512, self.NT)
        self.NTOK = self.TP + self.NS * 64
        self.parts = cfg.get("parts", ("mix", "ffn"))
        self.nc = bass.Bass("TRN2", target_bir_lowering=False)
        self.dbg = {}

    def din(self, name, shape, dt=F32):
        return self.nc.dram_tensor(name, list(shape), dt, kind="ExternalInput").ap()

    def dout(self, name, shape):
        return self.nc.dram_tensor(name, list(shape), F32, kind="ExternalOutput").ap()

    def dscr(self, name, shape, dt):
        return self.nc.dram_tensor(name, list(shape), dt).ap()

    def sb(self, name, shape, dt=F32):
        return self.st.enter_context(self.nc.sbuf_tensor(name, list(shape), dt))

    def op(self, eng, fn, r=(), w=()):
        return self.P.add(eng, fn, r, w)

    def dma(self, eng, out, in_, r=(), w=(), **kw):
        return self.P.add(eng, lambda e: e.dma_start(out=out, in_=in_, **kw), r, w, is_dma=True)

    def mm(self, out, lhsT, rhs, start, stop, r, w):
        return self.op("pe", lambda e: e.matmul(out, lhsT=lhsT, rhs=rhs, start=start, stop=stop), r, w)

    def tr(self, out, in_, ident, r, w):
        return self.op("pe", lambda e: e.transpose(out, in_, ident), r, w)

    def act(self, out, in_, func, r, w, bias=None, scale=None):
        kw = {}
        if bias is not None:
            kw["bias"] = bias
        if scale is not None:
            kw["scale"] = scale
        return self.op("act", lambda e: e.activation(out=out, in_=in_, func=func, **kw), r, w)

    def tt(self, eng, out, in0, in1, alu, r, w):
        return self.op(eng, lambda e: e.tensor_tensor(out=out, in0=in0, in1=in1, op=alu), r, w)

    def ts(self, eng, out, in0, s1, s2, op0, op1, r, w):
        return self.op(eng, lambda e: e.tensor_scalar(out=out, in0=in0, scalar1=s1, scalar2=s2, op0=op0, op1=op1), r, w)

    def ts1(self, eng, out, in0, s1, op0, r, w):
        return self.op(eng, lambda e: e.tensor_single_scalar(out=out, in_=in0, scalar=s1, op=op0), r, w)

    def stt(self, out, in0, scalar, in1, op0, op1, r, w):
        return self.op("dve", lambda e: e.scalar_tensor_tensor(out=out, in0=in0, scalar=scalar, in1=in1, op0=op0, op1=op1), r, w)

    def cp(self, eng, out, in_, r, w):
        if eng == "act":
            return self.act(out, in_, AF.Copy, r, w)
        return self.op(eng, lambda e: e.tensor_copy(out=out, in_=in_), r, w)

    def memset(self, eng, ap, val, w):
        return self.op(eng, lambda e: e.memset(ap, val), (), w)

    def psk(self, bank, q0=0, q1=4):
        return [("ps", bank)]

    def rview(self, off, n, dt):
        assert off % 4 == 0
        if dt is F32:
            assert off + 4 * n <= self.RBYTES, (off, n)
            return self.R[:, off // 4: off // 4 + n]
        assert n % 2 == 0 and off + 2 * n <= self.RBYTES, (off, n)
        return self.R[:, off // 4: off // 4 + n // 2].bitcast(BF16)

    def declare(self):
        TP, NS, DEPTH, NTOK = self.TP, self.NS, self.DEPTH, self.NTOK
        self.clay, self.CW = const_layout(self.NH)
        self.play, self.NPC = pcol_layout()
        i = {}
        i["x_p"] = self.din("x_p", [TP, D])
        i["x_s"] = self.din("x_s", [NS * 64, D])
        i["st_gla"] = self.din("st_gla", [DEPTH, NS, 2, 128, 128])
        for n in ("st_hg", "st_gdn", "st_ret"):
            i[n] = self.din(n, [DEPTH, NS, 4, 128, 128])
        i["c_gdnT"] = self.din("c_gdnT", [DEPTH, NS, 128, 12, 3])
        i["c_ffnT"] = self.din("c_ffnT", [DEPTH, NS, 128, FKT, 2])
        i["pcols"] = self.din("pcols", [128, self.NPC])
        i["consts"] = self.din("consts", [128, self.CW])
        i["wgk"] = self.din("wgk", [DEPTH, 16, 256])
        i["rotc"] = self.din("rotc", [128, TP + 64])
        i["rots"] = self.din("rots", [128, TP + 64])
        i["w_in"] = self.din("w_in", [DEPTH, D, NIN])
        i["w_branch"] = self.din("w_branch", [DEPTH, 4 * 512, D])
        i["w_out"] = self.din("w_out", [DEPTH, D, D])
        i["w_ffn_in"] = self.din("w_ffn_in", [DEPTH, D, 2 * DFF])
        i["w_ffn_out"] = self.din("w_ffn_out", [DEPTH, DFF, D])
        self.i = i
        o = {}
        o["y_p"] = self.dout("y_p", [TP, D])
        o["y_s"] = self.dout("y_s", [NS * 64, D])
        o["o_gla_p"] = self.dout("o_gla_p", [DEPTH, 2, 128, 128])
        for n in ("hg", "gdn", "ret"):
            o["o_%s_p" % n] = self.dout("o_%s_p" % n, [DEPTH, 4, 128, 128])
        o["o_cg_p"] = self.dout("o_cg_p", [DEPTH, 128, 12, 3])
        o["o_cf_p"] = self.dout("o_cf_p", [DEPTH, 128, FKT, 2])
        o["o_gla_s"] = self.dout("o_gla_s", [DEPTH, NS, 2, 128, 128])
        for n in ("hg", "gdn", "ret"):
            o["o_%s_s" % n] = self.dout("o_%s_s" % n, [DEPTH, NS, 4, 128, 128])
        o["o_cg_s"] = self.dout("o_cg_s", [DEPTH, NS, 128, 12, 3])
        o["o_cf_s"] = self.dout("o_cf_s", [DEPTH, NS, 128, FKT, 2])
        self.o = o
        self.xT = self.dscr("xT_scr", [KT, 128, NTOK], F32)
        self.wb = {n: self.dscr("wb_" + n, list(i[n].shape), BF16) for n in ("w_in", "w_branch", "w_out", "w_ffn_in", "w_ffn_out")}

    def dbg_out(self, name, sb_ap, shape, r):
        t = self.dout("dbg_" + name, shape)
        self.dbg[name] = t
        self.dma("pool", t, sb_ap, r=r)

    def build(self):
        nc = self.nc
        self.declare()
        with ExitStack() as st:
            self.st = st
            self.P = Prog(nc, st)
            self.alloc()
            self.setup()
            self.stage0()
            tiles = self.make_tiles()
            for l in range(self.DEPTH):
                for ti, tile in enumerate(tiles):
                    if "mix" in self.parts:
                        self.mixer_stage(l, ti, tile)
                    if "ffn" in self.parts:
                        self.ffn_stage(l, ti, tile)
            for ti, tile in enumerate(tiles):
                self.final_stage(ti, tile)
            self.P.emit_all()
        return nc

    def make_tiles(self):
        tiles = []
        npt = self.TP // self.NT
        for t in range(npt):
            nch = self.NT // 64
            tiles.append(dict(tok0=t * self.NT, N=self.NT, prompt=True,
                              chunks=[dict(stream=0, first=(t == 0 and c == 0), last=(t == npt - 1 and c == nch - 1),
                                           pos=t * self.NT + c * 64) for c in range(nch)],
                              nseg=1, seglen=self.NT, first_tile=(t == 0), last_tile=(t == npt - 1)))
        tiles.append(dict(tok0=self.TP, N=self.NS * 64, prompt=False,
                          chunks=[dict(stream=1 + s, first=True, last=True, pos=self.TP) for s in range(self.NS)],
                          nseg=self.NS, seglen=64, first_tile=True, last_tile=True))
        return tiles

    def alloc(self):
        nc, st = self.nc, self.st
        NT = self.NT
        self.C32 = self.sb("C32", [128, self.CW])
        self.PC = self.sb("PC", [128, self.NPC])
        self.identb = self.sb("identb", [128, 128], BF16)
        self.drv = self.sb("drv", [128, 32])
        self.hT = self.sb("hT", [128, KT, NT], BF16)
        NP = min(512, NT)
        n_a2 = 2 * KT * NT + max(76 * 1024 * self.NH // 512, 56 * 1024)
        n_a3 = 88 * NT + 16384
        n_b = 2 * FKT * NT + 2 * 4 * (NT + 2 * max(1, self.NS)) + 4 * NT + 2 * 2 * NT + 64
        n_norm = 4 * KT * NT + 8 * NP + 4 * NT + 16384
        self.RBYTES = max(n_a2, n_a3, n_b, n_norm, 32768)
        self.R = self.sb("R", [128, self.RBYTES // 4])
        self.wslot = [self.sb("wslot%d" % k, [128, WSLOT], BF16) for k in range(3)]
        self.wrr = 0
        self.S32 = self.sb("S32", [128, 14, 128])
        self.Sbf = self.sb("Sbf", [128, 14, 128], BF16)
        self.ghist = self.sb("ghist", [128, 12, 3])
        self.fhist = self.sb("fhist", [128, FKT, 2])
        self.wgk = self.sb("wgk_sb", [128, 2, 256], BF16)
        self.PS = [st.enter_context(nc.psum_tensor("psb%d" % k, [128, 512], F32)) for k in range(8)]

    def cst(self, name):
        o, w = self.clay[name]
        return self.C32[:, o:o + w]

    def pc(self, name, j0=0, j1=None):
        o, w = self.play[name]
        if j1 is None:
            j1 = w
        return self.PC[:, o + j0:o + j1]

    def setup(self):
        i = self.i
        self.dma("sp", self.C32[:], i["consts"], w=["C32"])
        self.dma("sp", self.PC[:], i["pcols"], w=["PC"])
        self.cp("dve", self.identb[:], self.cst("ident"), r=["C32"], w=["identb"])
        for l in range(self.DEPTH):
            for name in ("w_in", "w_branch", "w_out", "w_ffn_in", "w_ffn_out"):
                src, dst = i[name], self.wb[name]
                rows = src.shape[1]
                step = 128
                cranges = [(0, 7680), (7680, 7704), (7704, NIN)] if name == "w_in" else [(0, src.shape[2])]
                for r0 in range(0, rows, step):
                    r1 = min(rows, r0 + step)
                    for (ca, cb) in cranges:
                        self.dma("pool", dst[l, r0:r1, ca:cb], src[l, r0:r1, ca:cb], w=[("wb", name, l, r0 // 128)],
                                 max_dma_last_dim=4096)
        self.P.fence(("R", "W"))
        wgk32 = self.rview(0, 512, F32).rearrange("p (l c) -> p l c", l=2)
        self.memset("pool", wgk32, 0.0, w=[("R", "wgk32")])
        for l in range(self.DEPTH):
            self.dma("sp", wgk32[112:128, l, :], i["wgk"][l], w=[("R", "wgk32")])
        self.cp("pool", self.wgk[:], wgk32, r=[("R", "wgk32")], w=["wgk"])
        dv = self.drv
        self.memset("dve", dv[:], 0.0, w=["drv"])
        for l in range(self.DEPTH):
            self.ts1("dve", dv[:, 2 * l:2 * l + 2], self.pc("bgk%d" % l), -1.0, ALU.mult, r=["PC"], w=["drv"])
        self.memset("dve", dv[:, 12:16], 1.0, w=["drv"])
        if self.DEPTH > 1:
            self.tt("dve", dv[:, 24:28], self.pc("lbz1"), self.pc("lbz0"), ALU.subtract, r=["PC"], w=["drv"])
            self.act(dv[:, 8:12], dv[:, 24:28], AF.Sigmoid, r=["drv"], w=["drv"])
            self.ts("dve", dv[:, 16:20], dv[:, 8:12], -1.0, 1.0, ALU.mult, ALU.add, r=["drv"], w=["drv"])
        for l in range(self.DEPTH):
            self.act(dv[:, 28:29], self.pc("alog%d" % l), AF.Exp, r=["PC", "drv"], w=["drv"])
            self.stt(dv[:, 20 + l:21 + l], dv[:, 28:29], -1.0, self.cst("m47"), ALU.mult, ALU.mult, r=["drv", "C32"], w=["drv"])

    def stage0(self):
        P = self.P
        P.fence(("R", "W"))
        nblk = self.NTOK // 128
        xin = [self.rview(k * 8192, 2048, F32) for k in range(2)]
        xo = [self.rview(16384 + k * 8192, 2048, F32) for k in range(2)]
        for b in range(nblk):
            t0 = b * 128
            k = b % 2
            src = self.i["x_p"][t0:t0 + 128, :] if t0 < self.TP else self.i["x_s"][t0 - self.TP:t0 - self.TP + 128, :]
            self.dma("sp", xin[k], src, w=[("R", "xin", k)])
            for g in range(4):
                bank = 4 * (b % 2) + g
                for j in range(4):
                    kt = g * 4 + j
                    self.tr(self.PS[bank][:, j * 128:(j + 1) * 128], xin[k][:, kt * 128:(kt + 1) * 128], self.cst("ident"),
                            r=[("R", "xin", k), "C32"], w=self.psk(bank, j, j + 1))
                eng = "act" if g % 2 == 0 else "dve"
                self.cp(eng, xo[k][:, g * 512:(g + 1) * 512], self.PS[bank][:, :], r=self.psk(bank), w=[("R", "xo", k)])
            self.dma("pool", self.xT[:, :, t0:t0 + 128].rearrange("kt p t -> p kt t"),
                     xo[k].rearrange("p (kt t) -> p kt t", kt=KT), r=[("R", "xo", k)], w=[("xT", (t0 // self.NT if t0 < self.TP else -1), j) for j in range(KT)])

    def xkey(self, tile, j):
        return ("xT", (tile["tok0"] // self.NT if tile["prompt"] else -1), j)

    def load_w(self, name, l, r0, nkt, c0, ncols, eng="sp"):
        assert nkt * ncols <= WSLOT
        s = self.wrr
        self.wrr = (self.wrr + 1) % len(self.wslot)
        view = self.wslot[s][:, 0:nkt * ncols].rearrange("p (k c) -> p k c", k=nkt)
        src = self.wb[name][l, r0:r0 + nkt * 128, c0:c0 + ncols].rearrange("(k p) c -> p k c", p=128)
        rk = [("wb", name, l, r0 // 128 + k) for k in range(nkt)]
        self.dma(eng, view, src, r=rk, w=[("w", s)])
        return view, ("w", s)

    def dense(self, wname, l, r0, nkt, blocks, act_fn, n0, N, epilogue, banks=(0, 1, 2, 3), wcols=None):
        pieces = [(a, min(N, a + 512)) for a in range(0, N, 512)]
        npc = len(pieces)
        nset = len(banks) // npc
        assert nset >= 1
        if wcols is None:
            wcols = max(128, min(512, (WSLOT // nkt) // 128 * 128))
        bi = 0
        cnt = getattr(self, "_dense_cnt", 0)
        while bi < len(blocks):
            grp = [blocks[bi]]
            while len(grp) * 128 < wcols and bi + len(grp) < len(blocks) and blocks[bi + len(grp)][0] == grp[-1][0] + 128:
                grp.append(blocks[bi + len(grp)])
            wv, wk = self.load_w(wname, l, r0, nkt, grp[0][0], 128 * len(grp))
            for gi, (c0, tag) in enumerate(grp):
                bset = [banks[(cnt % nset) * npc + p] for p in range(npc)]
                cnt += 1
                for kt in range(nkt):
                    for p, (a, b) in enumerate(pieces):
                        ap, ak = act_fn(kt, n0 + a, n0 + b)
                        self.mm(self.PS[bset[p]][:, 0:b - a], wv[:, kt, gi * 128:(gi + 1) * 128], ap,
                                kt == 0, kt == nkt - 1, r=[wk, ak], w=self.psk(bset[p]))
                epilogue(tag, [self.PS[bset[p]][:, 0:b - a] for p, (a, b) in enumerate(pieces)],
                         [self.psk(bset[p]) for p in range(npc)], pieces)
            bi += len(grp)
        self._dense_cnt = cnt

    def norm_stage(self, tile, gname, final=False):
        P = self.P
        P.fence(("R", "W"))
        N, tok0 = tile["N"], tile["tok0"]
        xall = self.rview(0, KT * N, F32).rearrange("p (k n) -> p k n", k=KT)
        sq = [self.rview(4 * KT * N + k * 4 * min(512, N), min(512, N), F32) for k in range(2)]
        pieces = [(a, min(N, a + 512)) for a in range(0, N, 512)]
        for kt in range(KT):
            self.dma("sp", xall[:, kt, :], self.xT[kt, :, tok0:tok0 + N], r=[self.xkey(tile, kt)], w=[("R", "xall", kt)])
        c = 0
        for kt in range(KT):
            for p, (a, b) in enumerate(pieces):
                s = sq[c % 2]
                c += 1
                self.act(s[:, 0:b - a], xall[:, kt, a:b], AF.Square, r=[("R", "xall", kt)], w=[("R", "sq", (c - 1) % 2)])
                self.mm(self.PS[p][:, 0:b - a], self.cst("ones"), s[:, 0:b - a], kt == 0, kt == KT - 1,
                        r=["C32", ("R", "sq", (c - 1) % 2)], w=self.psk(p))
        rs = self.rview(4 * KT * N + 8 * min(512, N), N, F32)
        self.norm_end = 4 * KT * N + 8 * min(512, N) + 4 * N
        for p, (a, b) in enumerate(pieces):
            self.act(rs[:, a:b], self.PS[p][:, 0:b - a], AF.Ln, r=self.psk(p), w=[("R", "rstd")], bias=EPS, scale=1.0 / D)
            self.act(rs[:, a:b], rs[:, a:b], AF.Exp, r=[], w=[("R", "rstd")], scale=-0.5)
        for kt in range(KT):
            g = self.pc(gname, kt, kt + 1)
            if final:
                self.stt(xall[:, kt, :], xall[:, kt, :], g, rs[:, :], ALU.mult, ALU.mult,
                         r=["PC", ("R", "rstd"), ("R", "xall", kt)], w=[("R", "xall", kt)])
            else:
                self.stt(self.hT[:, kt, 0:N], xall[:, kt, :], g, rs[:, :], ALU.mult, ALU.mult,
                         r=["PC", ("R", "rstd"), ("R", "xall", kt)], w=[("hT", kt)])
        return xall

    def hT_fn(self, kt, a, b):
        return self.hT[:, kt, a:b], ("hT", kt)

    def resid_epilogue(self, tile, bufs, bkeys):
        N, tok0 = tile["N"], tile["tok0"]
        state = {"c": 0}
        def ep(tag, pss, pks, pieces):
            j = tag
            k = state["c"] % 2
            state["c"] += 1
            buf, bk = bufs[k], bkeys[k]
            xk = self.xkey(tile, j)
            self.dma("sp", buf, self.xT[j, :, tok0:tok0 + N], r=[xk], w=[bk])
            for p, (a, b) in enumerate(pieces):
                self.tt("dve", buf[:, a:b], buf[:, a:b], pss[p], ALU.add, r=pks[p] + [bk], w=[bk])
            self.dma("pool", self.xT[j, :, tok0:tok0 + N], buf, r=[bk], w=[xk])
        return ep

    def final_stage(self, ti, tile):
        N, tok0 = tile["N"], tile["tok0"]
        xall = self.norm_stage(tile, "nfin", final=True)
        yb = [self.rview(self.norm_end + k * 8192, 2048, F32) for k in range(2)]
        for tb in range(N // 128):
            k = tb % 2
            for g in range(4):
                bank = 4 * (tb % 2) + g
                for j in range(4):
                    kt = g * 4 + j
                    self.tr(self.PS[bank][:, j * 128:(j + 1) * 128], xall[:, kt, tb * 128:(tb + 1) * 128], self.cst("ident"),
                            r=[("R", "xall", kt), "C32"], w=self.psk(bank, j, j + 1))
                eng = "act" if g % 2 == 0 else "dve"
                self.cp(eng, yb[k][:, g * 512:(g + 1) * 512], self.PS[bank][:, :], r=self.psk(bank), w=[("R", "yb", k)])
            t0 = tok0 + tb * 128
            dst = self.o["y_p"][t0:t0 + 128, :] if t0 < self.TP else self.o["y_s"][t0 - self.TP:t0 - self.TP + 128, :]
            self.dma("pool", dst, yb[k], r=[("R", "yb", k)])

    def ffn_stage(self, l, ti, tile):
        N, tok0, nseg, L = tile["N"], tile["tok0"], tile["nseg"], tile["seglen"]
        self.norm_stage(tile, "nffn%d" % l)
        self.P.fence(("R", "W"))
        actT = self.rview(0, FKT * N, BF16).rearrange("p (k n) -> p k n", k=FKT)
        base = 2 * FKT * N
        W2 = nseg * (L + 2)
        W2a = (W2 + 1) // 2 * 2
        ae = [self.rview(base + k * 4 * W2a, W2, F32).rearrange("p (s t) -> p s t", s=nseg) for k in range(2)]
        base += 2 * 4 * W2a
        yv = self.rview(base, N, F32)
        base += 4 * N
        sv = [self.rview(base + k * 2 * N, N, BF16) for k in range(2)]
        fcw = lambda j, jf: self.pc("fcw%d" % l, j * FKT + jf, j * FKT + jf + 1)
        if tile["prompt"] and tile["first_tile"]:
            self.memset("pool", self.fhist[:], 0.0, w=["fhist"])

        def ep(tag, pss, pks, pieces):
            kind, jf = tag
            k = jf % 2
            if kind == "a":
                a_k, ak = ae[k], ("R", "ae", k)
                if tile["prompt"]:
                    self.cp("pool", a_k[:, 0, 0:2], self.fhist[:, jf, :], r=["fhist"], w=[ak])
                    for p, (a, b) in enumerate(pieces):
                        self.cp("act", a_k[:, 0, 2 + a:2 + b], pss[p], r=pks[p], w=[ak])
                else:
                    assert len(pieces) == 1
                    self.dma("sp", a_k[:, :, 0:2], self.i["c_ffnT"][l, :, :, jf, :].rearrange("s p j -> p s j"), w=[ak])
                    self.cp("act", a_k[:, :, 2:2 + L], pss[0].rearrange("p (s t) -> p s t", s=nseg), r=pks[0], w=[ak])
                y3 = yv.rearrange("p (s t) -> p s t", s=nseg)
                yk = ("R", "y")
                self.ts1("dve", y3, a_k[:, :, 0:L], fcw(0, jf), ALU.mult, r=[ak, "PC"], w=[yk])
                self.stt(y3, a_k[:, :, 1:L + 1], fcw(1, jf), y3, ALU.mult, ALU.add, r=[ak, "PC"], w=[yk])
                self.stt(y3, a_k[:, :, 2:L + 2], fcw(2, jf), y3, ALU.mult, ALU.add, r=[ak, "PC"], w=[yk])
                if tile["prompt"]:
                    self.cp("pool", self.fhist[:, jf, :], a_k[:, 0, L:L + 2], r=[ak], w=["fhist"])
                else:
                    self.dma("pool", self.o["o_cf_s"][l, :, :, jf, :].rearrange("s p j -> p s j"), a_k[:, :, L:L + 2], r=[ak])
                self.act(sv[k], yv, AF.Silu, r=[yk, "PC"], w=[("R", "s", k)], bias=self.pc("fcb%d" % l, jf, jf + 1))
            else:
                for p, (a, b) in enumerate(pieces):
                    self.tt("dve", actT[:, jf, a:b], pss[p], sv[k][:, a:b], ALU.mult, r=pks[p] + [("R", "s", k)], w=[("R", "actT", jf)])

        blocks = []
        for jp in range(0, FKT, 2):
            js = [j for j in (jp, jp + 1) if j < FKT]
            blocks += [(j * 128, ("a", j)) for j in js]
            blocks += [(DFF + j * 128, ("u", j)) for j in js]
        self.dense("w_ffn_in", l, 0, KT, blocks, self.hT_fn, 0, N, ep, wcols=256)
        if tile["prompt"] and tile["last_tile"]:
            self.dma("pool", self.o["o_cf_p"][l], self.fhist[:], r=["fhist"])
        rb = [self.rview(2 * FKT * N + k * 4 * N, N, F32) for k in range(2)]
        rkeys = [("R", "xres", 0), ("R", "xres", 1)]
        alias_r = [("R", "ae", 0), ("R", "ae", 1), ("R", "y"), ("R", "s", 0), ("R", "s", 1)]
        self.op("pool", lambda e: e.nop(), r=alias_r, w=rkeys)
        self.op("pool", lambda e: e.nop(), r=[], w=alias_r + rkeys)
        actfn = lambda kt, a, b: (actT[:, kt, a:b], ("R", "actT", kt))
        self.dense("w_ffn_out", l, 0, FKT, [(j * 128, j) for j in range(KT)], actfn, 0, N,
                   self.resid_epilogue(tile, rb, rkeys), wcols=128)

    def walloc(self, n, dt):
        nb = n * (4 if dt is F32 else 2)
        nb = (nb + 3) // 4 * 4
        v = self.rview(self.wptr, n if dt is F32 else (n + 1) // 2 * 2, dt)
        self.wptr += nb
        return v

    def dense_T(self, l, c0, ncols, h0, nh, vtok, vkey):
        cw = 256
        cnt = 0
        for cc in range(0, ncols, cw):
            wv, wk = self.load_w("w_in", l, 0, KT, c0 + cc, cw)
            for tb in range(nh // 128):
                bank = cnt % 2
                cnt += 1
                for kt in range(KT):
                    self.mm(self.PS[bank][:, 0:cw], self.hT[:, kt, h0 + tb * 128:h0 + (tb + 1) * 128], wv[:, kt, :],
                            kt == 0, kt == KT - 1, r=[wk, ("hT", kt)], w=self.psk(bank))
                self.cp("act", vtok[:, tb, cc:cc + cw], self.PS[bank][:, 0:cw], r=self.psk(bank), w=[vkey])

    def mixer_stage(self, l, ti, tile):
        N, tok0 = tile["N"], tile["tok0"]
        self.norm_stage(tile, "nmix%d" % l)
        self.P.fence(("R", "W"))
        oT = self.rview(0, KT * N, BF16).rearrange("p (k n) -> p k n", k=KT)
        self.WOFF = 2 * KT * N
        sel = self.cfg.get("mixers", ("gla", "hg", "ret", "gdn"))
        for kt in range(KT):
            if ("gla", "hg", "gdn", "ret")[kt // 4] not in sel:
                self.memset("pool", oT[:, kt, :], 0.0, w=[("R", "oT", kt)])
        for h0 in range(0, N, self.NH):
            nh = min(self.NH, N - h0)
            chunks = tile["chunks"][h0 // 64:(h0 + nh) // 64]
            if "gla" in sel:
                self.a2_gla(l, tile, h0, nh, chunks, oT)
            if "hg" in sel:
                self.a2_hg(l, tile, h0, nh, chunks, oT)
            if "ret" in sel:
                self.a2_ret(l, tile, h0, nh, chunks, oT)
            if "gdn" in sel:
                self.a2_gdn(l, tile, h0, nh, chunks, oT)
        self.P.fence("W")
        self.a3_merge(l, tile, oT)

    def state_io(self, l, mname, chunk, sidx, uidx, when):
        src = {"gla": "st_gla", "hg": "st_hg", "gdn": "st_gdn", "ret": "st_ret"}[mname]
        if when == "load":
            if chunk["stream"] == 0:
                self.memset("pool", self.S32[:, sidx, :], 0.0, w=[("S", sidx)])
                self.memset("pool", self.Sbf[:, sidx, :], 0.0, w=[("Sb", sidx)])
            else:
                s = chunk["stream"] - 1
                self.dma("sp", self.S32[:, sidx, :], self.i[src][l, s, uidx], w=[("S", sidx)])
                self.cp("act", self.Sbf[:, sidx, :], self.S32[:, sidx, :], r=[("S", sidx)], w=[("Sb", sidx)])
        else:
            if chunk["stream"] == 0:
                self.dma("pool", self.o["o_%s_p" % mname][l, uidx], self.S32[:, sidx, :], r=[("S", sidx)])
            else:
                s = chunk["stream"] - 1
                self.dma("pool", self.o["o_%s_s" % mname][l, s, uidx], self.S32[:, sidx, :], r=[("S", sidx)])

    def gla_scan(self, l, mname, mi, nh, chunks, units, vtok, vkey, oraw):
        nblk = nh // 128
        BO, BT = 5, 4
        BAq = lambda q: 2 + q % 2
        BSu = lambda u: 6 + u % 2
        att_sb = [self.walloc(128, BF16) for _ in range(4)]
        kdpA = [self.walloc(128, BF16) for _ in units]
        kdpB = [self.walloc(128, BF16) for _ in units]
        for u in range(len(units)):
            self.memset("pool", kdpA[u], 0.0, w=[("W", "kdpA", u)])
            self.memset("pool", kdpB[u], 0.0, w=[("W", "kdpB", u)])
        psT = self.PS[BT][:, :].bitcast(BF16)
        mask2 = self.cst("mask2")
        for tb in range(nblk):
            bs = slice(tb * 128, (tb + 1) * 128)
            for ui, U in enumerate(units):
                for hd in U["heads"]:
                    q = hd["h"]
                    self.mm(self.PS[BAq(q)][:, q * 128:(q + 1) * 128], hd["ktz"][:, bs], hd["qtz"][:, bs], True, True,
                            r=[hd["kkey"], hd["qkey"]], w=self.psk(BAq(q)))
                    self.tt("dve", att_sb[q], self.PS[BAq(q)][:, q * 128:(q + 1) * 128], mask2, ALU.mult,
                            r=self.psk(BAq(q)) + ["C32"], w=[("W", "att", q)])
                self.tr(psT[:, ui * 128:(ui + 1) * 128], U["kd"][:, bs], self.identb[:], r=[U["kdkey"], "identb"],
                        w=self.psk(BT, ui // 2, ui // 2 + 1))
                self.cp("act", kdpA[ui][0:64, :], psT[0:64, ui * 128:(ui + 1) * 128], r=self.psk(BT, ui // 2, ui // 2 + 1), w=[("W", "kdpA", ui)])
                self.cp("act", kdpB[ui][64:128, :], psT[64:128, ui * 128:(ui + 1) * 128], r=self.psk(BT, ui // 2, ui // 2 + 1), w=[("W", "kdpB", ui)])
            for ci in range(2):
                cidx = 2 * tb + ci
                ch = chunks[cidx]
                cs_ = slice(cidx * 64, (cidx + 1) * 64)
                for ui, U in enumerate(units):
                    sidx = U["sidx"]
                    if ch["first"]:
                        self.state_io(l, mname, ch, sidx, U["uidx"], "load")
                    for hd in U["heads"]:
                        q = hd["h"]
                        osl = self.PS[BO][:, q * 128 + ci * 64:q * 128 + ci * 64 + 64]
                        self.mm(osl, vtok[:, tb, hd["vq"] * 128:(hd["vq"] + 1) * 128], att_sb[q][:, ci * 64:(ci + 1) * 64], True, False,
                                r=[vkey, ("W", "att", q)], w=self.psk(BO, q, q + 1))
                        self.mm(osl, self.Sbf[:, sidx, :], hd["qtz"][:, cs_], False, True,
                                r=[("Sb", sidx), hd["qkey"]], w=self.psk(BO, q, q + 1))
                    kp = (kdpA if ci == 0 else kdpB)[ui]
                    kpk = ("W", "kdpA" if ci == 0 else "kdpB", ui)
                    vw = U["vw"]
                    nq = vw // 128
                    sps = self.PS[BSu(ui)][:, 0:vw]
                    spk = self.psk(BSu(ui))
                    self.mm(sps, kp, vtok[:, tb, U["vc0"]:U["vc0"] + vw], True, True, r=[kpk, vkey], w=spk)
                    dsc = U["dS"][:, cidx:cidx + 1]
                    if nq == 1:
                        self.stt(self.S32[:, sidx, :], self.S32[:, sidx, :], dsc, sps, ALU.mult, ALU.add,
                                 r=spk + [U["dskey"]], w=[("S", sidx)])
                    else:
                        self.stt(self.S32[0:64, sidx, :], self.S32[0:64, sidx, :], U["dS"][0:64, cidx:cidx + 1], sps[0:64, 0:128],
                                 ALU.mult, ALU.add, r=spk + [U["dskey"]], w=[("S", sidx)])
                        self.stt(self.S32[64:128, sidx, :], self.S32[64:128, sidx, :], U["dS"][64:128, cidx:cidx + 1], sps[64:128, 128:256],
                                 ALU.mult, ALU.add, r=spk + [U["dskey"]], w=[("S", sidx)])
                    self.cp("act", self.Sbf[:, sidx, :], self.S32[:, sidx, :], r=[("S", sidx)], w=[("Sb", sidx)])
                    if ch["last"]:
                        self.state_io(l, mname, ch, sidx, U["uidx"], "store")
            for U in units:
                for hd in U["heads"]:
                    q = hd["h"]
                    self.cp("act", oraw[q][:, bs], self.PS[BO][:, q * 128:(q + 1) * 128], r=self.psk(BO, q, q + 1), w=[("W", "oraw", q)])

    def head_norm(self, l, mi, nh, h0, oraw, sg, oT, gname, bname=None):
        BN = 6
        sq = self.walloc(nh, F32)
        rs = self.walloc(nh, F32)
        t = self.walloc(nh, F32)
        ones = self.cst("ones")
        for h in range(4):
            ok = ("W", "oraw", h)
            src = oraw[h]
            if bname is not None:
                self.mm(self.PS[BN][:, 0:nh], ones, oraw[h], True, True, r=["C32", ok], w=self.psk(BN))
                self.stt(oraw[h], self.PS[BN][:, 0:nh], -1.0 / 128, oraw[h], ALU.mult, ALU.add, r=self.psk(BN) + [ok], w=[ok])
            self.act(sq, src, AF.Square, r=[ok], w=[("W", "hn_sq")])
            self.mm(self.PS[BN][:, 0:nh], ones, sq, True, True, r=["C32", ("W", "hn_sq")], w=self.psk(BN))
            self.act(rs, self.PS[BN][:, 0:nh], AF.Ln, r=self.psk(BN), w=[("W", "hn_rs")], bias=EPS, scale=1.0 / 128)
            self.act(rs, rs, AF.Exp, r=[], w=[("W", "hn_rs")], scale=-0.5)
            self.stt(t, src, self.pc(gname), rs, ALU.mult, ALU.mult, r=[ok, "PC", ("W", "hn_rs")], w=[("W", "hn_t")])
            okey = ("R", "oT", mi * 4 + h)
            if bname is None:
                self.tt("dve", oT[:, mi * 4 + h, h0:h0 + nh], t, sg[h], ALU.mult, r=[("W", "hn_t"), ("W", "sg", h)], w=[okey])
            else:
                self.stt(oT[:, mi * 4 + h, h0:h0 + nh], t, self.pc(bname), sg[h], ALU.add, ALU.mult,
                         r=[("W", "hn_t"), ("W", "sg", h), "PC"], w=[okey])

    def decay_ops(self, cs, cskey, nh, sb, eb, enb, kdec, dS, keytag):
        nch = nh // 64
        cs3 = cs.rearrange("p (c t) -> p c t", t=64)
        kd3 = kdec.rearrange("p (c t) -> p c t", t=64)
        if eb is not None:
            self.act(eb, cs, AF.Exp, r=[cskey], w=[("W", keytag, "eb")], scale=sb)
        self.act(enb, cs, AF.Exp, r=[cskey], w=[("W", keytag, "enb")], scale=-sb)
        self.act(dS[:, 0:nch], cs3[:, :, 63], AF.Exp, r=[cskey], w=[("W", keytag, "dS")], scale=sb)
        self.tt("dve", kd3, cs3[:, :, 63:64].broadcast_to([128, nch, 64]), cs3, ALU.subtract, r=[cskey], w=[("W", keytag, "kdec")])
        self.act(kdec, kdec, AF.Exp, r=[], w=[("W", keytag, "kdec")], scale=sb)

    def a2_gla(self, l, tile, h0, nh, chunks, oT):
        P = self.P
        P.fence("W")
        self.wptr = self.WOFF
        nch, nblk = nh // 64, nh // 128
        vtok = self.walloc(nblk * 512, BF16).rearrange("p (b c) -> p b c", b=nblk)
        ktz = [self.walloc(nh, BF16) for _ in range(4)]
        qtz = [self.walloc(nh, BF16) for _ in range(4)]
        kd = [self.walloc(nh, BF16) for _ in range(2)]
        sg = [self.walloc(nh, BF16) for _ in range(4)]
        oraw = [self.walloc(nh, F32) for _ in range(4)]
        dS = [self.walloc(max(nch, 2), F32) for _ in range(2)]
        lrT = self.walloc(nh, BF16)
        cs = self.walloc(nh, F32)
        eb = self.walloc(nh, F32)
        enb = self.walloc(nh, F32)
        kdec = self.walloc(nh, F32)
        tmp = self.walloc(nh, F32)
        rmask = self.cst("rmask")
        for h in range(4):
            self.memset("pool", ktz[h], 0.0, w=[("W", "ktz", h)])
            self.memset("pool", qtz[h], 0.0, w=[("W", "qtz", h)])
        self.dense_T(l, C_GLA_V, 512, h0, nh, vtok, ("W", "vtok"))

        def ep_lr(tag, pss, pks, pieces):
            self.cp("act", lrT, pss[0], r=pks[0], w=[("W", "lrT")])
        self.dense("w_in", l, 0, KT, [(C_GLA_LR + 16 - 128, 0)], self.hT_fn, h0, nh, ep_lr, banks=(0, 1))

        def ep_k(tag, pss, pks, pieces):
            jb = tag
            ps = pss[0]
            for hh in range(2):
                rs_ = slice(hh * 64, hh * 64 + 64)
                self.tt("dve", ktz[2 * jb + hh][rs_, :], ps[rs_, :], enb[rs_, :], ALU.mult,
                        r=pks[0] + [("W", "g", "enb")], w=[("W", "ktz", 2 * jb + hh)])
            self.tt("dve", kd[jb], ps, kdec, ALU.mult, r=pks[0] + [("W", "g", "kdec")], w=[("W", "kd", jb)])

        def ep_q(tag, pss, pks, pieces):
            jb = tag
            ps = pss[0]
            for hh in range(2):
                rs_ = slice(hh * 64, hh * 64 + 64)
                self.stt(qtz[2 * jb + hh][rs_, :], ps[rs_, :], 0.125, eb[rs_, :], ALU.mult, ALU.mult,
                         r=pks[0] + [("W", "g", "eb")], w=[("W", "qtz", 2 * jb + hh)])

        for jb in range(2):
            BG = 7
            self.mm(self.PS[BG][:, 0:nh], self.wgk[:, l, jb * 128:(jb + 1) * 128], lrT, True, True,
                    r=["wgk", ("W", "lrT")], w=self.psk(BG))
            self.act(tmp, self.PS[BG][:, 0:nh], AF.Exp, r=self.psk(BG) + ["drv"], w=[("W", "g", "tmp")],
                     scale=-1.0, bias=self.drv[:, 2 * l + jb:2 * l + jb + 1])
            self.act(tmp, tmp, AF.Ln, r=[], w=[("W", "g", "tmp")], bias=1.0)
            self.op("dve", lambda e, cs=cs, tmp=tmp: e.tensor_tensor_scan(out=cs, data0=rmask[:, 0:nh], data1=tmp, initial=0.0,
                                                                         op0=ALU.mult, op1=ALU.add),
                    r=[("W", "g", "tmp"), "C32"], w=[("W", "g", "cs")])
            self.decay_ops(cs, ("W", "g", "cs"), nh, -1.0 / 16, eb, enb, kdec, dS[jb], "g")
            self.dense("w_in", l, 0, KT, [(C_GLA_K + jb * 128, jb)], self.hT_fn, h0, nh, ep_k, banks=(0, 1))
            self.dense("w_in", l, 0, KT, [(C_GLA_Q + jb * 128, jb)], self.hT_fn, h0, nh, ep_q, banks=(0, 1))

        def ep_g(tag, pss, pks, pieces):
            self.act(sg[tag], pss[0], AF.Silu, r=pks[0], w=[("W", "sg", tag)])
        self.dense("w_in", l, 0, KT, [(C_GLA_G + h * 128, h) for h in range(4)], self.hT_fn, h0, nh, ep_g, banks=(0, 1))
        units = []
        for jb in range(2):
            units.append(dict(kd=kd[jb], kdkey=("W", "kd", jb), dS=dS[jb], dskey=("W", "g", "dS"), sidx=jb, uidx=jb,
                              vc0=jb * 256, vw=256,
                              heads=[dict(h=2 * jb + hh, ktz=ktz[2 * jb + hh], qtz=qtz[2 * jb + hh], kkey=("W", "ktz", 2 * jb + hh),
                                          qkey=("W", "qtz", 2 * jb + hh), vq=2 * jb + hh) for hh in range(2)]))
        if self.cfg.get("dbg") and l == 0 and tile["tok0"] == 0 and h0 == 0:
            self.dbg_out("cs", cs, [128, nh], [("W", "g", "cs")])
            self.dbg_out("eb", eb, [128, nh], [("W", "g", "eb")])
            self.dbg_out("kdec", kdec, [128, nh], [("W", "g", "kdec")])
            t32 = self.walloc(nh, F32)
            for nm, ap, key in (("ktz1", ktz[1], ("W", "ktz", 1)), ("qtz0", qtz[0], ("W", "qtz", 0)), ("kd0", kd[0], ("W", "kd", 0)), ("lrT", lrT, ("W", "lrT"))):
                t32 = self.walloc(nh, F32)
                self.cp("dve", t32, ap, r=[key], w=[("W", "dbg", nm)])
                self.dbg_out(nm, t32, [128, nh], [("W", "dbg", nm)])
            t33 = self.walloc(nblk * 512, F32)
            self.cp("dve", t33, vtok.rearrange("p b c -> p (b c)"), r=[("W", "vtok")], w=[("W", "dbg", "vtok")])
            self.dbg_out("vtok", t33, [128, nblk * 512], [("W", "dbg", "vtok")])
            self.dbg_out("dS0", dS[0], [128, 2], [("W", "g", "dS")])
        if self.cfg.get("stop") == "prep":
            return
        self.gla_scan(l, "gla", 0, nh, chunks, units, vtok, ("W", "vtok"), oraw)
        if self.cfg.get("dbg") and l == 0 and tile["tok0"] == 0 and h0 == 0:
            self.dbg_out("oraw0", oraw[0], [128, nh], [("W", "oraw", 0)])
            self.dbg_out("S0", self.S32[:, 0, :], [128, 128], [("S", 0)])
        if self.cfg.get("stop") == "scan":
            return
        self.head_norm(l, 0, nh, h0, oraw, sg, oT, "gla_ng%d" % l)

    def a2_hg(self, l, tile, h0, nh, chunks, oT):
        self.P.fence("W")
        self.wptr = self.WOFF
        nch, nblk = nh // 64, nh // 128
        vtok = self.walloc(nblk * 512, BF16).rearrange("p (b c) -> p b c", b=nblk)
        ktz = [self.walloc(nh, BF16) for _ in range(4)]
        qtz = [self.walloc(nh, BF16) for _ in range(4)]
        kd = [self.walloc(nh, BF16) for _ in range(4)]
        sg = [self.walloc(nh, BF16) for _ in range(4)]
        oraw = [self.walloc(nh, F32) for _ in range(4)]
        dS = [self.walloc(max(nch, 2), F32) for _ in range(4)]
        cs = self.walloc(nh, F32)
        eb = [self.walloc(nh, F32) for _ in range(2)]
        enb = self.walloc(nh, F32)
        kdec = self.walloc(nh, F32)
        tA = self.walloc(nh, F32)
        tB = self.walloc(nh, F32)
        rmask = self.cst("rmask")
        self.dense_T(l, C_HG_I, 512, h0, nh, vtok, ("W", "vtok"))

        def ep(tag, pss, pks, pieces):
            kind, h = tag
            ps = pss[0]
            if kind == "f":
                lb = self.drv[:, 4 + 4 * l + h:5 + 4 * l + h]
                oml = self.drv[:, 12 + 4 * l + h:13 + 4 * l + h]
                self.act(tA, ps, AF.Sigmoid, r=pks[0], w=[("W", "h", "tA")])
                self.ts("dve", tA, tA, oml, lb, ALU.mult, ALU.add, r=["drv"], w=[("W", "h", "tA")])
                self.act(tB, tA, AF.Ln, r=[("W", "h", "tA")], w=[("W", "h", "tB")])
                self.op("dve", lambda e: e.tensor_tensor_scan(out=cs, data0=rmask[:, 0:nh], data1=tB, initial=0.0,
                                                              op0=ALU.mult, op1=ALU.add),
                        r=[("W", "h", "tB"), "C32"], w=[("W", "h", "cs")])
                self.ts("dve", tA, tA, -1.0, 1.0, ALU.mult, ALU.add, r=[], w=[("W", "h", "tA")])
                self.decay_ops_h(cs, nh, eb[h % 2], enb, kdec, dS[h], h)
                self.tt("pool", ktz[h], tA, enb, ALU.mult, r=[("W", "h", "tA"), ("W", "h", "enb")], w=[("W", "ktz", h)])
                self.tt("pool", kd[h], tA, kdec, ALU.mult, r=[("W", "h", "tA"), ("W", "h", "kdec")], w=[("W", "kd", h)])
            elif kind == "q":
                self.act(tB, ps, AF.Silu, r=pks[0], w=[("W", "h", "tB")])
                self.tt("dve", qtz[h], tB, eb[h % 2], ALU.mult, r=[("W", "h", "tB"), ("W", "h", "eb", h % 2)], w=[("W", "qtz", h)])
            else:
                self.act(sg[h], ps, AF.Silu, r=pks[0], w=[("W", "sg", h)])
        blocks = []
        for h in range(4):
            blocks += [(C_HG_F + h * 128, ("f", h)), (C_HG_Q + h * 128, ("q", h))]
        blocks += [(C_HG_G + h * 128, ("g", h)) for h in range(4)]
        self.dense("w_in", l, 0, KT, blocks, self.hT_fn, h0, nh, ep, banks=(0, 1))
        units = [dict(kd=kd[h], kdkey=("W", "kd", h), dS=dS[h], dskey=("W", "h", "dS", h), sidx=2 + h, uidx=h, vc0=h * 128, vw=128,
                      heads=[dict(h=h, ktz=ktz[h], qtz=qtz[h], kkey=("W", "ktz", h), qkey=("W", "qtz", h), vq=h)]) for h in range(4)]
        self.gla_scan(l, "hg", 1, nh, chunks, units, vtok, ("W", "vtok"), oraw)
        self.head_norm(l, 1, nh, h0, oraw, sg, oT, "hg_ng%d" % l)

    def decay_ops_h(self, cs, nh, eb, enb, kdec, dS, h):
        nch = nh // 64
        cs3 = cs.rearrange("p (c t) -> p c t", t=64)
        kd3 = kdec.rearrange("p (c t) -> p c t", t=64)
        ck = ("W", "h", "cs")
        self.act(eb, cs, AF.Exp, r=[ck], w=[("W", "h", "eb", h % 2)])
        self.act(enb, cs, AF.Exp, r=[ck], w=[("W", "h", "enb")], scale=-1.0)
        self.act(dS[:, 0:nch], cs3[:, :, 63], AF.Exp, r=[ck], w=[("W", "h", "dS", h)])
        self.tt("dve", kd3, cs3[:, :, 63:64].broadcast_to([128, nch, 64]), cs3, ALU.subtract, r=[ck], w=[("W", "h", "kdec")])
        self.act(kdec, kdec, AF.Exp, r=[], w=[("W", "h", "kdec")])

    def load_w_swapped(self, l, c0):
        s = self.wrr
        self.wrr = (self.wrr + 1) % len(self.wslot)
        view = self.wslot[s][:, 0:KT * 128].rearrange("p (k c) -> p k c", k=KT)
        rk = [("wb", "w_in", l, k) for k in range(KT)]
        for (d0, s0) in ((0, c0 + 64), (64, c0)):
            src = self.wb["w_in"][l, :, s0:s0 + 64].rearrange("(k p) c -> p k c", p=128)
            self.dma("sp", view[:, :, d0:d0 + 64], src, r=rk, w=[("w", s)])
        return view, ("w", s)

    def a2_ret(self, l, tile, h0, nh, chunks, oT):
        self.P.fence("W")
        self.wptr = self.WOFF
        nch, nblk = nh // 64, nh // 128
        vtok = self.walloc(nblk * 512, BF16).rearrange("p (b c) -> p b c", b=nblk)
        ktz = [self.walloc(nh, BF16) for _ in range(4)]
        qtz = [self.walloc(nh, BF16) for _ in range(4)]
        kd = [self.walloc(nh, BF16) for _ in range(4)]
        sg = [self.walloc(nh, BF16) for _ in range(4)]
        oraw = [self.walloc(nh, F32) for _ in range(4)]
        dS = [self.walloc(max(nch, 2), F32) for _ in range(4)]
        rc = self.walloc(nh, F32)
        rsn = self.walloc(nh, F32)
        t1 = [self.walloc(nh, F32) for _ in range(2)]
        t2 = self.walloc(nh, F32)
        if tile["prompt"]:
            p0 = tile["tok0"] + h0
            self.dma("sp", rc, self.i["rotc"][:, p0:p0 + nh], w=[("W", "rc")])
            self.dma("sp", rsn, self.i["rots"][:, p0:p0 + nh], w=[("W", "rsn")])
        else:
            for c in range(nch):
                self.dma("sp", rc[:, c * 64:(c + 1) * 64], self.i["rotc"][:, self.TP:self.TP + 64], w=[("W", "rc")])
                self.dma("sp", rsn[:, c * 64:(c + 1) * 64], self.i["rots"][:, self.TP:self.TP + 64], w=[("W", "rsn")])
        for h in range(4):
            self.memset("pool", dS[h], math.exp(64 * math.log(1.0 - 2.0 ** (-5.0 - h))), w=[("W", "r", "dS", h)])
        self.dense_T(l, C_RET_V, 512, h0, nh, vtok, ("W", "vtok"))

        def tab(name, h):
            o, w = self.clay[name]
            return self.C32[:, o + h * 64:o + (h + 1) * 64].unsqueeze(1).broadcast_to([128, nch, 64])
        v3 = lambda ap: ap.rearrange("p (c t) -> p c t", t=64)

        def ep(tag, pss, pks, pieces):
            kind, h = tag
            ps = pss[0]
            i_ = 0 if kind[0] == "q" else 1
            if kind in ("q", "k"):
                self.tt("dve", t1[i_], ps, rc, ALU.mult, r=pks[0] + [("W", "rc")], w=[("W", "r", "t1", i_)])
            elif kind in ("qs", "ks"):
                self.tt("dve", t2, ps, rsn, ALU.mult, r=pks[0] + [("W", "rsn")], w=[("W", "r", "t2")])
                self.tt("pool", t1[i_], t1[i_], t2, ALU.add, r=[("W", "r", "t2")], w=[("W", "r", "t1", i_)])
                if kind == "qs":
                    self.tt("pool", v3(qtz[h]), v3(t1[i_]), tab("reteb", h), ALU.mult, r=["C32"], w=[("W", "qtz", h)])
                else:
                    self.tt("pool", v3(ktz[h]), v3(t1[i_]), tab("retenb", h), ALU.mult, r=["C32"], w=[("W", "ktz", h)])
                    self.tt("pool", v3(kd[h]), v3(t1[i_]), tab("retkd", h), ALU.mult, r=["C32"], w=[("W", "kd", h)])
            else:
                self.act(sg[h], ps, AF.Silu, r=pks[0], w=[("W", "sg", h)])

        for h in range(4):
            for (cbase, kind) in ((C_RET_Q, "q"), (C_RET_K, "k")):
                self.dense("w_in", l, 0, KT, [(cbase + h * 128, (kind, h))], self.hT_fn, h0, nh, ep, banks=(0, 1))
                wv, wk = self.load_w_swapped(l, cbase + h * 128)
                bank = getattr(self, "_dense_cnt", 0) % 2
                self._dense_cnt = getattr(self, "_dense_cnt", 0) + 1
                for kt in range(KT):
                    self.mm(self.PS[bank][:, 0:nh], wv[:, kt, :], self.hT[:, kt, h0:h0 + nh], kt == 0, kt == KT - 1,
                            r=[wk, ("hT", kt)], w=self.psk(bank))
                ep((kind + "s", h), [self.PS[bank][:, 0:nh]], [self.psk(bank)], [(0, nh)])
        self.dense("w_in", l, 0, KT, [(C_RET_G + h * 128, ("g", h)) for h in range(4)], self.hT_fn, h0, nh, ep, banks=(0, 1))
        units = [dict(kd=kd[h], kdkey=("W", "kd", h), dS=dS[h], dskey=("W", "r", "dS", h), sidx=10 + h, uidx=h, vc0=h * 128, vw=128,
                      heads=[dict(h=h, ktz=ktz[h], qtz=qtz[h], kkey=("W", "ktz", h), qkey=("W", "qtz", h), vq=h)]) for h in range(4)]
        self.gla_scan(l, "ret", 3, nh, chunks, units, vtok, ("W", "vtok"), oraw)
        self.head_norm(l, 3, nh, h0, oraw, sg, oT, "ret_ng%d" % l, "ret_nb%d" % l)

    def a2_gdn(self, l, tile, h0, nh, chunks, oT):
        self.P.fence("W")
        self.wptr = self.WOFF
        nch, nblk = nh // 64, nh // 128
        if tile["nseg"] == 1:
            nsg, L = 1, nh
        else:
            nsg, L = nch, 64
        v4 = lambda ap: ap.rearrange("p (h n) -> p h n", h=4)
        qhat = v4(self.walloc(4 * nh, BF16))
        khat = v4(self.walloc(4 * nh, BF16))
        vT = v4(self.walloc(4 * nh, BF16))
        sg = [self.walloc(nh, BF16) for _ in range(4)]
        oraw = [self.walloc(nh, F32) for _ in range(4)]
        XW = nsg * (L + 3)
        xe = [self.walloc(XW + (XW % 2), F32)[:, 0:XW].rearrange("p (s t) -> p s t", s=nsg) for _ in range(2)]
        yv = self.walloc(nh, F32)
        sv = self.walloc(nh, F32)
        sqv = self.walloc(nh, F32)
        BTt = self.walloc(nh, F32)
        CB = self.walloc(nh, F32)
        RS = self.walloc(nh, F32)
        EBT = self.walloc(nh, F32)
        KDT = self.walloc(nh, F32)
        dSg = self.walloc(4 * max(nch, 2), F32)
        rmask = self.cst("rmask")
        ones = self.cst("ones")
        for t_, nm in ((BTt, "BT"), (CB, "CB"), (RS, "RS"), (EBT, "EBT"), (KDT, "KDT")):
            self.memset("pool", t_, 0.0, w=[("W", nm)])
        gcw = lambda j, blk: self.pc("gcw%d" % l, j * 12 + blk, j * 12 + blk + 1)
        if tile["prompt"] and tile["first_tile"] and h0 == 0:
            self.memset("pool", self.ghist[:], 0.0, w=["ghist"])
        s_first = chunks[0]["stream"] - 1
        BN = 7

        def ep(tag, pss, pks, pieces):
            kind, idx = tag
            ps = pss[0]
            if kind == "x":
                blk = idx
                k = blk % 2
                x_k, xk = xe[k], ("W", "xe", k)
                if tile["prompt"]:
                    self.cp("pool", x_k[:, 0, 0:3], self.ghist[:, blk, :], r=["ghist"], w=[xk])
                else:
                    self.dma("sp", x_k[:, :, 0:3], self.i["c_gdnT"][l, s_first:s_first + nsg, :, blk, :].rearrange("s p j -> p s j"), w=[xk])
                self.cp("act", x_k[:, :, 3:3 + L], ps.rearrange("p (s t) -> p s t", s=nsg), r=pks[0], w=[xk])
                y3 = yv.rearrange("p (s t) -> p s t", s=nsg)
                yk = ("W", "yv")
                self.ts1("dve", y3, x_k[:, :, 0:L], gcw(0, blk), ALU.mult, r=[xk, "PC"], w=[yk])
                for j in range(1, 4):
                    self.stt(y3, x_k[:, :, j:j + L], gcw(j, blk), y3, ALU.mult, ALU.add, r=[xk, "PC"], w=[yk])
                if tile["prompt"]:
                    self.cp("pool", self.ghist[:, blk, :], x_k[:, 0, L:L + 3], r=[xk], w=["ghist"])
                else:
                    self.dma("pool", self.o["o_cg_s"][l, s_first:s_first + nsg, :, blk, :].rearrange("s p j -> p s j"), x_k[:, :, L:L + 3], r=[xk])
                h = blk % 4
                if blk < 8:
                    self.act(sv, yv, AF.Silu, r=[yk], w=[("W", "sv")])
                    self.act(sqv, sv, AF.Square, r=[], w=[("W", "sqv")])
                    self.mm(self.PS[BN][:, 0:nh], ones, sqv, True, True, r=["C32", ("W", "sqv")], w=self.psk(BN))
                    self.act(sqv, self.PS[BN][:, 0:nh], AF.Ln, r=self.psk(BN), w=[("W", "sqv")], bias=EPS)
                    self.act(sqv, sqv, AF.Exp, r=[], w=[("W", "sqv")], scale=-0.5)
                    if blk < 4:
                        self.stt(qhat[:, h, :], sv, 128 ** -0.5, sqv, ALU.mult, ALU.mult, r=[("W", "sv"), ("W", "sqv")], w=[("W", "qhat")])
                    else:
                        self.tt("dve", khat[:, h, :], sv, sqv, ALU.mult, r=[("W", "sv"), ("W", "sqv")], w=[("W", "khat")])
                else:
                    self.act(vT[:, h, :], yv, AF.Silu, r=[yk], w=[("W", "vT")])
            elif kind == "z":
                self.act(sg[idx], ps, AF.Silu, r=pks[0], w=[("W", "sg", idx)])
            else:
                R32 = slice(0, 32)
                self.act(BTt[R32, :], ps[R32, :], AF.Sigmoid, r=pks[0], w=[("W", "BT")])
                self.act(RS[R32, :], ps[R32, :], AF.Exp, r=pks[0] + ["PC"], w=[("W", "RS")], bias=self.pc("dtb%d" % l)[R32, :])
                self.act(RS[R32, :], RS[R32, :], AF.Ln, r=[], w=[("W", "RS")], bias=1.0)
                self.ts1("dve", RS[R32, :], RS[R32, :], self.drv[R32, 20 + l:21 + l], ALU.mult, r=["drv"], w=[("W", "RS")])
                self.op("dve", lambda e: e.tensor_tensor_scan(out=CB[R32, :], data0=rmask[R32, 0:nh], data1=RS[R32, :], initial=0.0,
                                                              op0=ALU.mult, op1=ALU.add), r=["C32"], w=[("W", "CB")])
                c3 = lambda ap: ap.rearrange("p (c t) -> p c t", t=64)
                self.tt("dve", c3(RS)[R32], c3(CB)[R32, :, 63:64].broadcast_to([32, nch, 64]), c3(CB)[R32], ALU.subtract,
                        r=[("W", "CB")], w=[("W", "RS")])
                self.act(EBT[R32, :], CB[R32, :], AF.Exp, r=[("W", "CB")], w=[("W", "EBT")])
                self.act(KDT[R32, :], RS[R32, :], AF.Exp, r=[("W", "RS")], w=[("W", "KDT")])

        blocks = [(C_GDN_B, ("ba", 0))] + [(C_GDN_QKV + b * 128, ("x", b)) for b in (4, 5, 6, 7, 0, 1, 2, 3, 8, 9, 10, 11)]
        blocks += [(C_GDN_Z + h * 128, ("z", h)) for h in range(4)]
        self.dense("w_in", l, 0, KT, blocks, self.hT_fn, h0, nh, ep, banks=(0, 1))
        if tile["prompt"] and tile["last_tile"] and h0 + nh == tile["N"]:
            self.dma("pool", self.o["o_cg_p"][l], self.ghist[:], r=["ghist"])
        osel = self.clay["onesel"][0]
        onesel = lambda q: self.C32[:, osel + q * 128:osel + (q + 1) * 128]
        sel = self.cst("sel")
        c3 = lambda ap: ap.rearrange("p (c t) -> p c t", t=64)
        for h in range(4):
            self.mm(self.PS[BN][:, h * nch:(h + 1) * nch], onesel(h), c3(EBT)[:, :, 63], True, True, r=["C32", ("W", "EBT")], w=self.psk(BN, 0, 1))
        self.cp("act", dSg[:, 0:4 * nch], self.PS[BN][:, 0:4 * nch], r=self.psk(BN, 0, 1), w=[("W", "dSg")])
        f4 = lambda: self.walloc(512, F32).rearrange("p (h n) -> p h n", h=4)
        b4 = lambda: self.walloc(512, BF16).rearrange("p (h n) -> p h n", h=4)
        colsb = self.walloc(16, F32)
        nbe = self.walloc(4, F32)
        g1, gp, Am, Bm, Xm, A2, B2, bv, rhs_sb = f4(), f4(), f4(), f4(), f4(), f4(), f4(), f4(), f4()
        attd, kdpA, kdpB, qe, u_sb = b4(), b4(), b4(), b4(), b4()
        self.memset("pool", kdpA, 0.0, w=[("W", "kdpA")])
        self.memset("pool", kdpB, 0.0, w=[("W", "kdpB")])
        bc4 = lambda ap: ap.unsqueeze(1).broadcast_to([128, 4, 128])
        colb = lambda q: colsb[:, 4 * q:4 * q + 4].unsqueeze(2).broadcast_to([128, 4, 128])
        psbank = lambda b: self.PS[b][:, :].rearrange("p (h n) -> p h n", h=4)
        psT = self.PS[2][:, :].bitcast(BF16)
        psTk = psT[:, 0:512].rearrange("p (h n) -> p h n", h=4)
        psTv = psT[:, 512:1024].rearrange("p (h n) -> p h n", h=4)
        for tb in range(nblk):
            bs = slice(tb * 128, (tb + 1) * 128)
            for q, (X, xn, sl) in enumerate(((CB, "CB", 0), (BTt, "BT", 4), (EBT, "EBT", 0), (KDT, "KDT", 0))):
                self.mm(self.PS[2][:, 4 * q:4 * q + 4], X[:, bs], sel[:, sl:sl + 4], True, True, r=[("W", xn), "C32"], w=self.psk(2))
            self.cp("act", colsb, self.PS[2][:, 0:16], r=self.psk(2), w=[("W", "colsb")])
            self.stt(nbe, colsb[:, 4:8], -1.0, colsb[:, 8:12], ALU.mult, ALU.mult, r=[("W", "colsb")], w=[("W", "nbe")])
            for h in range(4):
                self.mm(self.PS[3][:, h * 128:(h + 1) * 128], onesel(h), CB[:, bs], True, True, r=["C32", ("W", "CB")], w=self.psk(3))
                self.mm(self.PS[4][:, h * 128:(h + 1) * 128], onesel(4 + h), BTt[:, bs], True, True, r=["C32", ("W", "BT")], w=self.psk(4))
                self.mm(self.PS[5][:, h * 128:(h + 1) * 128], onesel(h), EBT[:, bs], True, True, r=["C32", ("W", "EBT")], w=self.psk(5))
            self.tt("dve", g1, psbank(3), colb(0), ALU.subtract, r=self.psk(3) + [("W", "colsb")], w=[("W", "g1")])
            self.ts1("dve", gp, g1, 0.0, ALU.max, r=[], w=[("W", "gp")])
            self.ts1("dve", g1, g1, 0.0, ALU.min, r=[], w=[("W", "g1")])
            self.act(gp, gp, AF.Exp, r=[("W", "gp")], w=[("W", "gp")], scale=-1.0)
            self.act(g1, g1, AF.Exp, r=[("W", "g1")], w=[("W", "g1")])
            for h in range(4):
                self.mm(self.PS[6][:, h * 128:(h + 1) * 128], khat[:, h, bs], khat[:, h, bs], True, True, r=[("W", "khat")], w=self.psk(6))
                self.mm(self.PS[7][:, h * 128:(h + 1) * 128], khat[:, h, bs], qhat[:, h, bs], True, True, r=[("W", "khat"), ("W", "qhat")], w=self.psk(7))
            self.tt("dve", Am, psbank(6), gp, ALU.mult, r=self.psk(6) + [("W", "gp")], w=[("W", "Am")])
            self.tt("dve", Am, Am, colb(1), ALU.mult, r=[("W", "colsb")], w=[("W", "Am")])
            self.tt("pool", Am, Am, bc4(self.cst("nml")), ALU.mult, r=["C32"], w=[("W", "Am")])
            self.tt("dve", Bm, psbank(6), g1, ALU.mult, r=self.psk(6) + [("W", "g1")], w=[("W", "Bm")])
            self.tt("dve", Bm, Bm, psbank(4), ALU.mult, r=self.psk(4), w=[("W", "Bm")])
            self.tt("pool", Bm, Bm, bc4(self.cst("nmu")), ALU.mult, r=["C32"], w=[("W", "Bm")])
            self.tt("pool", Xm, Bm, bc4(self.cst("ident")), ALU.add, r=["C32"], w=[("W", "Xm")])
            self.tt("dve", bv, psbank(7), g1, ALU.mult, r=self.psk(7) + [("W", "g1")], w=[("W", "bv")])
            self.tt("pool", attd, bv, bc4(self.cst("mask2")), ALU.mult, r=["C32"], w=[("W", "attd")])
            self.tt("dve", qe, qhat[:, :, bs], psbank(5), ALU.mult, r=self.psk(5) + [("W", "qhat")], w=[("W", "qe")])
            for j in range(5):
                for h in range(4):
                    self.mm(self.PS[3][:, h * 128:(h + 1) * 128], Bm[:, h, :], Am[:, h, :], True, True, r=[("W", "Am"), ("W", "Bm")], w=self.psk(3))
                if j < 4:
                    for h in range(4):
                        self.mm(self.PS[4][:, h * 128:(h + 1) * 128], Am[:, h, :], Bm[:, h, :], True, True, r=[("W", "Am"), ("W", "Bm")], w=self.psk(4))
                self.cp("act", A2, psbank(3), r=self.psk(3), w=[("W", "A2")])
                if j < 4:
                    self.cp("dve", B2, psbank(4), r=self.psk(4), w=[("W", "B2")])
                for h in range(4):
                    self.mm(self.PS[5][:, h * 128:(h + 1) * 128], A2[:, h, :], Xm[:, h, :], True, True, r=[("W", "A2"), ("W", "Xm")], w=self.psk(5))
                self.tt("dve", Xm, Xm, psbank(5), ALU.add, r=self.psk(5), w=[("W", "Xm")])
                Am, A2 = A2, Am
                if j < 4:
                    Bm, B2 = B2, Bm
                self.op("pool", lambda e: e.nop(), r=[("W", "Am"), ("W", "A2"), ("W", "Bm"), ("W", "B2")],
                        w=[("W", "Am"), ("W", "A2"), ("W", "Bm"), ("W", "B2")])
            for h in range(4):
                self.tr(psTk[:, h, :], khat[:, h, bs], self.identb[:], r=[("W", "khat"), "identb"], w=self.psk(2, 0, 2))
                self.tr(psTv[:, h, :], vT[:, h, bs], self.identb[:], r=[("W", "vT"), "identb"], w=self.psk(2, 2, 4))
            kdc = colsb[:, 12:16].unsqueeze(2).broadcast_to([128, 4, 128])
            self.tt("dve", kdpA[0:64], psTk[0:64], kdc[0:64], ALU.mult, r=self.psk(2, 0, 2) + [("W", "colsb")], w=[("W", "kdpA")])
            self.tt("dve", kdpB[64:128], psTk[64:128], kdc[64:128], ALU.mult, r=self.psk(2, 0, 2) + [("W", "colsb")], w=[("W", "kdpB")])
            self.tt("dve", bv, psTv, colb(1), ALU.mult, r=self.psk(2, 2, 4) + [("W", "colsb")], w=[("W", "bv")])
            for ci in range(2):
                cidx = 2 * tb + ci
                ch = chunks[cidx]
                for h in range(4):
                    sidx = 6 + h
                    if ch["first"]:
                        self.state_io(l, "gdn", ch, sidx, h, "load")
                    hs = slice(h * 128, (h + 1) * 128)
                    self.mm(self.PS[6][:, hs], khat[:, h, bs], self.Sbf[:, sidx, :], True, True, r=[("W", "khat"), ("Sb", sidx)], w=self.psk(6, h, h + 1))
                    self.stt(rhs_sb[:, h, :], self.PS[6][:, hs], nbe[:, h:h + 1], bv[:, h, :], ALU.mult, ALU.add,
                             r=self.psk(6, h, h + 1) + [("W", "nbe"), ("W", "bv")], w=[("W", "rhs", h)])
                    self.mm(self.PS[7][:, hs], Xm[:, h, :], rhs_sb[:, h, :], True, True, r=[("W", "Xm"), ("W", "rhs", h)], w=self.psk(7, h, h + 1))
                    self.cp("act", u_sb[:, h, :], self.PS[7][:, hs], r=self.psk(7, h, h + 1), w=[("W", "u", h)])
                    osl = self.PS[3][:, h * 128 + ci * 64:h * 128 + ci * 64 + 64]
                    self.mm(osl, u_sb[:, h, :], attd[:, h, ci * 64:(ci + 1) * 64], True, False, r=[("W", "u", h), ("W", "attd")], w=self.psk(3, h, h + 1))
                    self.mm(osl, self.Sbf[:, sidx, :], qe[:, h, ci * 64:(ci + 1) * 64], False, True, r=[("Sb", sidx), ("W", "qe")], w=self.psk(3, h, h + 1))
                    kp = kdpA if ci == 0 else kdpB
                    self.mm(self.PS[4][:, hs], kp[:, h, :], u_sb[:, h, :], True, True, r=[("W", "kdpA" if ci == 0 else "kdpB"), ("W", "u", h)], w=self.psk(4, h, h + 1))
                    self.stt(self.S32[:, sidx, :], self.S32[:, sidx, :], dSg[:, h * nch + cidx:h * nch + cidx + 1], self.PS[4][:, hs],
                             ALU.mult, ALU.add, r=self.psk(4, h, h + 1) + [("W", "dSg")], w=[("S", sidx)])
                    self.cp("act", self.Sbf[:, sidx, :], self.S32[:, sidx, :], r=[("S", sidx)], w=[("Sb", sidx)])
                    if ch["last"]:
                        self.state_io(l, "gdn", ch, sidx, h, "store")
            for h in range(4):
                self.cp("act", oraw[h][:, bs], self.PS[3][:, h * 128:(h + 1) * 128], r=self.psk(3, h, h + 1), w=[("W", "oraw", h)])
        self.head_norm(l, 2, nh, h0, oraw, sg, oT, "gdn_ng%d" % l)

    def a3_merge(self, l, tile, oT):
        N, tok0 = tile["N"], tile["tok0"]
        mixT = self.rview(2 * KT * N, KT * N, BF16).rearrange("p (k n) -> p k n", k=KT)
        base = 4 * KT * N
        acc = [self.rview(base + j * 4 * N, N, F32) for j in range(4)]
        base += 16 * N
        sgt = [self.rview(base + k * 4 * N, N, F32) for k in range(2)]
        base += 8 * N
        aux = self.rview(base, 16 * 512, BF16).rearrange("p (k c) -> p k c", k=16)
        ofn = lambda kt, a, b: (oT[:, kt, a:b], ("R", "oT", kt))
        pieces = [(a, min(N, a + 512)) for a in range(0, N, 512)]
        cnt = 0
        for jg in range(4):
            for n in range(4):
                self.dma("sp", aux[:, n * 4:(n + 1) * 4, :],
                         self.wb["w_branch"][l, n * 512:(n + 1) * 512, jg * 512:(jg + 1) * 512].rearrange("(k p) c -> p k c", p=128),
                         r=[("wb", "w_branch", l, n * 4 + k) for k in range(4)], w=[("W", "aux")])
            for n in range(4):
                for half in range(2):
                    wv, wk = self.load_w("w_in", l, 0, KT, C_MERGE + n * D + jg * 512 + half * 256, 256)
                    for jj in range(2):
                        jl = half * 2 + jj
                        j = jg * 4 + jl
                        gb = [(2 * (cnt % 2) + p) for p in range(len(pieces))]
                        bb = [4 + (2 * (cnt % 2) + p) for p in range(len(pieces))]
                        cnt += 1
                        for kt in range(KT):
                            for p, (a, b) in enumerate(pieces):
                                self.mm(self.PS[gb[p]][:, 0:b - a], wv[:, kt, jj * 128:(jj + 1) * 128], self.hT[:, kt, a:b],
                                        kt == 0, kt == KT - 1, r=[wk, ("hT", kt)], w=self.psk(gb[p]))
                        for kk in range(4):
                            for p, (a, b) in enumerate(pieces):
                                self.mm(self.PS[bb[p]][:, 0:b - a], aux[:, n * 4 + kk, jl * 128:(jl + 1) * 128], oT[:, n * 4 + kk, a:b],
                                        kk == 0, kk == 3, r=[("W", "aux"), ("R", "oT", n * 4 + kk)], w=self.psk(bb[p]))
                        sk = cnt % 2
                        for p, (a, b) in enumerate(pieces):
                            self.act(sgt[sk][:, a:b], self.PS[gb[p]][:, 0:b - a], AF.Sigmoid, r=self.psk(gb[p]), w=[("W", "sgt", sk)])
                            if n == 0:
                                self.tt("dve", acc[jl][:, a:b], self.PS[bb[p]][:, 0:b - a], sgt[sk][:, a:b], ALU.mult,
                                        r=self.psk(bb[p]) + [("W", "sgt", sk)], w=[("W", "acc", jl)])
                            else:
                                self.tt("dve", sgt[sk][:, a:b], self.PS[bb[p]][:, 0:b - a], sgt[sk][:, a:b], ALU.mult,
                                        r=self.psk(bb[p]), w=[("W", "sgt", sk)])
                                if n < 3:
                                    self.tt("pool", acc[jl][:, a:b], acc[jl][:, a:b], sgt[sk][:, a:b], ALU.add,
                                            r=[("W", "sgt", sk)], w=[("W", "acc", jl)])
                                else:
                                    self.tt("pool", mixT[:, j, a:b], acc[jl][:, a:b], sgt[sk][:, a:b], ALU.add,
                                            r=[("W", "sgt", sk), ("W", "acc", jl)], w=[("W", "mixT", j)])
        rb = sgt
        rkeys = [("W", "sgt", 0), ("W", "sgt", 1)]
        mfn = lambda kt, a, b: (mixT[:, kt, a:b], ("W", "mixT", kt))
        self.dense("w_out", l, 0, KT, [(j * 128, j) for j in range(KT)], mfn, 0, N, self.resid_epilogue(tile, rb, rkeys),
                   banks=(0, 1, 2, 3))


PAST_LEN = 4096


def core_inputs(inp, c, cfg, shared):
    TP, NS = cfg["TP"], cfg["NS"]
    f = lambda a: np.ascontiguousarray(np.asarray(a, np.float32))
    m = dict(shared)
    xp = np.asarray(inp["x_prompt"])
    if c < xp.shape[0]:
        m["x_p"] = f(xp[c, :TP])
    else:
        m["x_p"] = np.zeros((TP, D), np.float32)
    s0, s1 = c * NS, (c + 1) * NS
    m["x_s"] = f(np.asarray(inp["x_sample"])[s0:s1].reshape(NS * 64, D))
    m["st_gla"] = f(np.asarray(inp["state_gla"])[:, s0:s1].reshape(-1, NS, 2, 128, 128))
    m["st_hg"] = f(np.asarray(inp["state_hgrn"])[:, s0:s1])
    m["st_gdn"] = f(np.asarray(inp["state_gdn"])[:, s0:s1])
    m["st_ret"] = f(np.asarray(inp["state_ret"])[:, s0:s1])
    cg = np.asarray(inp["cache_gdn_conv"])[:, s0:s1]
    m["c_gdnT"] = f(cg.reshape(cg.shape[0], NS, 3, 12, 128).transpose(0, 1, 4, 3, 2))
    cf = np.asarray(inp["cache_ffn_conv"])[:, s0:s1]
    m["c_ffnT"] = f(cf.reshape(cf.shape[0], NS, 2, FKT, 128).transpose(0, 1, 4, 3, 2))
    return m


def shared_inputs(inp, cfg):
    TP = cfg["TP"]
    NH = min(512, cfg["NT"])
    f = lambda a: np.ascontiguousarray(np.asarray(a, np.float32))
    sh = {}
    sh["pcols"] = make_pcols(inp)
    sh["consts"] = make_consts(NH)
    sh["wgk"] = f(inp["gla_w_gk"])
    rc, rs = rot_tables(TP, PAST_LEN)
    sh["rotc"], sh["rots"] = rc, rs
    sh["w_in"] = f(inp["w_in"])
    wbr = np.asarray(inp["w_branch"], np.float32)
    sh["w_branch"] = np.ascontiguousarray(wbr.reshape(wbr.shape[0], 4 * 512, D))
    sh["w_out"] = f(inp["w_out"])
    sh["w_ffn_in"] = f(inp["w_ffn_in"])
    sh["w_ffn_out"] = f(inp["w_ffn_out"])
    return sh


_NC_CACHE = {}


def get_nc(cfg):
    key = repr(sorted(cfg.items()))
    if key not in _NC_CACHE:
        b = Builder(cfg)
        nc = b.build()
        _NC_CACHE[key] = (nc, b)
    return _NC_CACHE[key]


def kernel(**inputs):
    cfg = dict(TP=8192, NS=4, NT=1024, DEPTH=2)
    nc, b = get_nc(cfg)
    sh = shared_inputs(inputs, cfg)
    in_maps = [core_inputs(inputs, c, cfg, sh) for c in range(8)]
    res = run_bass_kernel_spmd(nc, in_maps, core_ids=list(range(8)))
    R = res.results
    L = cfg["DEPTH"]
    y_p = np.stack([R[c]["y_p"] for c in range(2)])
    y_s = np.concatenate([R[c]["y_s"].reshape(4, 64, D) for c in range(8)])
    outs = [y_p, y_s]
    outs.append(np.stack([R[c]["o_gla_p"].reshape(L, 4, 64, 128) for c in range(2)], axis=1))
    for n in ("hg", "gdn", "ret"):
        outs.append(np.stack([R[c]["o_%s_p" % n] for c in range(2)], axis=1))
    outs.append(np.stack([R[c]["o_cg_p"].transpose(0, 3, 2, 1).reshape(L, 3, 1536) for c in range(2)], axis=1))
    outs.append(np.stack([R[c]["o_cf_p"].transpose(0, 3, 2, 1).reshape(L, 2, DFF) for c in range(2)], axis=1))
    outs.append(np.concatenate([R[c]["o_gla_s"].reshape(L, 4, 4, 64, 128) for c in range(8)], axis=1))
    for n in ("hg", "gdn", "ret"):
        outs.append(np.concatenate([R[c]["o_%s_s" % n] for c in range(8)], axis=1))
    outs.append(np.concatenate([R[c]["o_cg_s"].transpose(0, 1, 4, 3, 2).reshape(L, 4, 3, 1536) for c in range(8)], axis=1))
    outs.append(np.concatenate([R[c]["o_cf_s"].transpose(0, 1, 4, 3, 2).reshape(L, 4, 2, DFF) for c in range(8)], axis=1))
    return tuple(np.ascontiguousarray(o, dtype=np.float32) for o in outs)
```

```python
import math
from contextlib import ExitStack

import numpy as np
import concourse.bass as bass
import concourse.mybir as mybir
from concourse.bass_utils import run_bass_kernel_spmd

F32 = mybir.dt.float32
BF16 = mybir.dt.bfloat16
AF = mybir.ActivationFunctionType
ALU = mybir.AluOpType

ENGS = ("pe", "act", "dve", "pool", "sp")

D = 2048
KT = 16
NIN = 15896
DFF = 5504
FKT = 43
H = 4
EPS = 1e-6
C_GLA_Q, C_GLA_K, C_GLA_V, C_GLA_G, C_GLA_LR = 0, 256, 512, 1024, 1536
C_HG_Q, C_HG_F, C_HG_I, C_HG_G = 1552, 2064, 2576, 3088
C_GDN_QKV, C_GDN_Z, C_GDN_B = 3600, 5136, 5648
C_RET_Q, C_RET_K, C_RET_V, C_RET_G = 5656, 6168, 6680, 7192
C_MERGE = 7704
WSLOT = 5632


class Op:
    __slots__ = ("eng", "emit", "waits", "sig", "is_dma")


class Prog:
    def __init__(self, nc, stack, n_dma_sems=(("sp", 24), ("pool", 24), ("act", 8))):
        self.nc = nc
        self.ops = {e: [] for e in ENGS}
        self.res = {}
        self.esem = {e: stack.enter_context(nc.semaphore("s_" + e)) for e in ENGS}
        self.ecount = {e: 0 for e in ENGS}
        self.dsem, self.dstate, self.drr = {}, {}, {}
        for e, n in n_dma_sems:
            self.dsem[e] = [stack.enter_context(nc.semaphore("d_%s%d" % (e, i))) for i in range(n)]
            self.dstate[e] = [[0, None] for _ in range(n)]
            self.drr[e] = 0
        self.waited = {e: {} for e in ENGS}
        self.frontier = {}
        self.groups = {}
        self.nops = 0

    def _need(self, op, src):
        if src is None or src is op:
            return
        if src.eng == op.eng and not src.is_dma:
            return
        sem, val = src.sig
        w = self.waited[op.eng]
        k = id(sem)
        if w.get(k, -1) >= val:
            return
        w[k] = val
        op.waits.append((sem, val))

    def _get(self, k):
        r = self.res.get(k)
        if r is None:
            r = [None, []]
            if isinstance(k, tuple) and k and k[0] in self.frontier:
                r[1] = list(self.frontier[k[0]])
            self.res[k] = r
            if isinstance(k, tuple) and k:
                self.groups.setdefault(k[0], set()).add(k)
        return r

    def fence(self, groups):
        if isinstance(groups, str):
            groups = (groups,)
        fr = {}
        for group in groups:
            for k in self.groups.get(group, ()):
                r = self.res.pop(k)
                if r[0] is not None:
                    fr[id(r[0])] = r[0]
                for o in r[1]:
                    fr[id(o)] = o
            for o in self.frontier.get(group, ()):
                fr[id(o)] = o
        best = {}
        for o in fr.values():
            sem, val = o.sig
            b = best.get(id(sem))
            if b is None or b.sig[1] < val:
                best[id(sem)] = o
        for group in groups:
            self.frontier[group] = list(best.values())
            self.groups[group] = set()

    def add(self, eng, emit, reads=(), writes=(), is_dma=False):
        op = Op()
        op.eng, op.emit, op.waits, op.is_dma = eng, emit, [], is_dma
        self.nops += 1
        if is_dma:
            pool = self.dstate[eng]
            i = self.drr[eng]
            self.drr[eng] = (i + 1) % len(pool)
            st = pool[i]
            if st[1] is not None:
                sem, val = st[1].sig
                w = self.waited[eng]
                if w.get(id(sem), -1) < val:
                    w[id(sem)] = val
                    op.waits.append((sem, val))
            st[0] += 16
            st[1] = op
            op.sig = (self.dsem[eng][i], st[0])
        else:
            self.ecount[eng] += 1
            op.sig = (self.esem[eng], self.ecount[eng])
        pr = [k for k in reads if isinstance(k, tuple) and k[0] == "ps"]
        if pr:
            reads = [k for k in reads if not (isinstance(k, tuple) and k[0] == "ps")]
            writes = list(writes) + pr
        for k in reads:
            r = self._get(k)
            self._need(op, r[0])
            r[1].append(op)
        for k in writes:
            r = self._get(k)
            self._need(op, r[0])
            for rd in r[1]:
                self._need(op, rd)
            r[0] = op
            r[1] = []
        self.ops[eng].append(op)
        return op

    def emit_all(self, final_eng="sp"):
        nc = self.nc
        finals = []
        for e in ENGS:
            if self.ecount[e] > 0 and e != final_eng:
                finals.append((self.esem[e], self.ecount[e]))
        for e in self.dsem:
            for i, st in enumerate(self.dstate[e]):
                if st[0] > 0:
                    finals.append((self.dsem[e][i], st[0]))
        with nc.Block() as block:
            def mk(e):
                def body(eng):
                    for o in self.ops[e]:
                        for (sem, val) in o.waits:
                            eng.wait_ge(sem, val)
                        ins = o.emit(eng)
                        ins.then_inc(o.sig[0], 16 if o.is_dma else 1)
                    if e == final_eng:
                        for (sem, val) in finals:
                            eng.wait_ge(sem, val)
                return body
            block.tensor(mk("pe"))
            block.scalar(mk("act"))
            block.vector(mk("dve"))
            block.gpsimd(mk("pool"))
            block.sync(mk("sp"))


def const_layout(NH):
    lay, off = {}, 0
    for name, w in (("ident", 128), ("ones", 128), ("mask2", 128), ("nml", 128), ("nmu", 128),
                    ("sel", 8), ("onesel", 1024), ("rmask", NH), ("m47", 1),
                    ("reteb", 256), ("retenb", 256), ("retkd", 256), ("rswap", 128)):
        lay[name] = (off, w)
        off += w
    return lay, off


def make_consts(NH):
    lay, cw = const_layout(NH)
    c = np.zeros((128, cw), np.float32)
    p = np.arange(128)[:, None]
    f = np.arange(128)[None, :]
    same = (p // 64) == (f // 64)
    c[:, lay["ident"][0]:lay["ident"][0] + 128] = np.eye(128)
    c[:, lay["ones"][0]:lay["ones"][0] + 128] = 1.0
    c[:, lay["mask2"][0]:lay["mask2"][0] + 128] = (same & (p <= f)).astype(np.float32)
    c[:, lay["nml"][0]:lay["nml"][0] + 128] = -(same & (f < p)).astype(np.float32)
    c[:, lay["nmu"][0]:lay["nmu"][0] + 128] = -(same & (p < f)).astype(np.float32)
    so = lay["sel"][0]
    for h in range(4):
        c[4 + h, so + h] = 1.0
        c[h, so + 4 + h] = 1.0
    oo = lay["onesel"][0]
    for h in range(4):
        c[4 + h, oo + h * 128:oo + (h + 1) * 128] = 1.0
        c[h, oo + (4 + h) * 128:oo + (5 + h) * 128] = 1.0
    ro = lay["rmask"][0]
    rm = np.ones(NH, np.float32)
    rm[::64] = 0.0
    c[:, ro:ro + NH] = rm[None, :]
    c[4:8, lay["m47"][0]] = 1.0
    for m_ in range(128):
        c[(m_ + 64) % 128, lay["rswap"][0] + m_] = 1.0
    t = np.arange(64, dtype=np.float64)
    for h in range(4):
        lg = math.log(1.0 - 2.0 ** (-5.0 - h))
        c[:, lay["reteb"][0] + h * 64:lay["reteb"][0] + (h + 1) * 64] = np.exp(lg * (t + 1))[None, :]
        c[:, lay["retenb"][0] + h * 64:lay["retenb"][0] + (h + 1) * 64] = (np.exp(-lg * (t + 1)) * 128 ** -0.5)[None, :]
        c[:, lay["retkd"][0] + h * 64:lay["retkd"][0] + (h + 1) * 64] = (np.exp(lg * (63 - t)) * 128 ** -0.5)[None, :]
    return c


def pcol_layout():
    lay, off = {}, 0
    def add(name, w):
        nonlocal off
        lay[name] = (off, w)
        off += w
    for l in range(2):
        add("nmix%d" % l, 16)
        add("nffn%d" % l, 16)
        add("bgk%d" % l, 2)
        add("gla_ng%d" % l, 1)
        add("hg_ng%d" % l, 1)
        add("gdn_ng%d" % l, 1)
        add("ret_ng%d" % l, 1)
        add("ret_nb%d" % l, 1)
        add("lbz%d" % l, 4)
        add("gcw%d" % l, 48)
        add("fcw%d" % l, 129)
        add("fcb%d" % l, 43)
        add("dtb%d" % l, 1)
        add("alog%d" % l, 1)
    add("nfin", 16)
    return lay, off


def cols(v):
    return np.ascontiguousarray(np.asarray(v, np.float32).reshape(-1, 128).T)


def make_pcols(inp):
    lay, n = pcol_layout()
    t = np.zeros((128, n), np.float32)
    def put(name, arr):
        o, w = lay[name]
        assert arr.shape == (128, w), (name, arr.shape, w)
        t[:, o:o + w] = arr
    for l in range(2):
        put("nmix%d" % l, cols(inp["norm_mix_g"][l]))
        put("nffn%d" % l, cols(inp["norm_ffn_g"][l]))
        put("bgk%d" % l, cols(inp["gla_b_gk"][l]))
        put("gla_ng%d" % l, cols(inp["gla_norm_g"][l]))
        put("hg_ng%d" % l, cols(inp["hgrn_norm_g"][l]))
        put("gdn_ng%d" % l, cols(inp["gdn_norm_g"][l]))
        put("ret_ng%d" % l, cols(inp["ret_norm_g"][l]))
        put("ret_nb%d" % l, cols(inp["ret_norm_b"][l]))
        put("lbz%d" % l, cols(inp["hgrn_lb_logits"][l]))
        put("gcw%d" % l, np.concatenate([cols(inp["gdn_conv_w"][l][j]) for j in range(4)], axis=1))
        put("fcw%d" % l, np.concatenate([cols(inp["ffn_conv_w"][l][j]) for j in range(3)], axis=1))
        put("fcb%d" % l, cols(inp["ffn_conv_b"][l]))
        a = np.zeros((128, 1), np.float32)
        a[4:8, 0] = np.asarray(inp["gdn_dt_bias"][l], np.float32)
        put("dtb%d" % l, a)
        a = np.zeros((128, 1), np.float32)
        a[4:8, 0] = np.asarray(inp["gdn_a_log"][l], np.float32)
        put("alog%d" % l, a)
    put("nfin", cols(inp["norm_final_g"]))
    return t


def rot_tables(TP, past_len):
    pos = np.concatenate([np.arange(TP), past_len + np.arange(64)]).astype(np.float32)
    inv = (1.0 / (np.float32(10000.0) ** (np.arange(0, 128, 2, dtype=np.float32) / np.float32(128)))).astype(np.float32)
    ang = (pos[None, :] * inv[:, None]).astype(np.float32)
    cos, sin = np.cos(ang).astype(np.float32), np.sin(ang).astype(np.float32)
    rc = np.concatenate([cos, cos], axis=0)
    rs = np.concatenate([-sin, sin], axis=0)
    return np.ascontiguousarray(rc), np.ascontiguousarray(rs)


class Builder:
    def __init__(self, cfg):
        self.cfg = cfg
        self.TP, self.NS, self.NT = cfg["TP"], cfg["NS"], cfg["NT"]
        self.DEPTH = cfg.get("DEPTH", 2)
        self.NH = min(512, self.NT)
        self.NTOK = self.TP + self.NS * 64
        self.parts = cfg.get("parts", ("mix", "ffn"))
        self.nc = bass.Bass("TRN2", target_bir_lowering=False)
        self.dbg = {}

    def din(self, name, shape, dt=F32):
        return self.nc.dram_tensor(name, list(shape), dt, kind="ExternalInput").ap()

    def dout(self, name, shape):
        return self.nc.dram_tensor(name, list(shape), F32, kind="ExternalOutput").ap()

    def dscr(self, name, shape, dt):
        return self.nc.dram_tensor(name, list(shape), dt).ap()

    def sb(self, name, shape, dt=F32):
        return self.st.enter_context(self.nc.sbuf_tensor(name, list(shape), dt))

    def op(self, eng, fn, r=(), w=()):
        return self.P.add(eng, fn, r, w)

    def dma(self, eng, out, in_, r=(), w=(), **kw):
        return self.P.add(eng, lambda e: e.dma_start(out=out, in_=in_, **kw), r, w, is_dma=True)

    def mm(self, out, lhsT, rhs, start, stop, r, w):
        return self.op("pe", lambda e: e.matmul(out, lhsT=lhsT, rhs=rhs, start=start, stop=stop), r, w)

    def tr(self, out, in_, ident, r, w):
        return self.op("pe", lambda e: e.transpose(out, in_, ident), r, w)

    def act(self, out, in_, func, r, w, bias=None, scale=None):
        kw = {}
        if bias is not None:
            kw["bias"] = bias
        if scale is not None:
            kw["scale"] = scale
        return self.op("act", lambda e: e.activation(out=out, in_=in_, func=func, **kw), r, w)

    def tt(self, eng, out, in0, in1, alu, r, w):
        return self.op(eng, lambda e: e.tensor_tensor(out=out, in0=in0, in1=in1, op=alu), r, w)

    def ts(self, eng, out, in0, s1, s2, op0, op1, r, w):
        return self.op(eng, lambda e: e.tensor_scalar(out=out, in0=in0, scalar1=s1, scalar2=s2, op0=op0, op1=op1), r, w)

    def ts1(self, eng, out, in0, s1, op0, r, w):
        return self.op(eng, lambda e: e.tensor_single_scalar(out=out, in_=in0, scalar=s1, op=op0), r, w)

    def stt(self, out, in0, scalar, in1, op0, op1, r, w):
        return self.op("dve", lambda e: e.scalar_tensor_tensor(out=out, in0=in0, scalar=scalar, in1=in1, op0=op0, op1=op1), r, w)

    def cp(self, eng, out, in_, r, w):
        if eng == "act":
            return self.act(out, in_, AF.Copy, r, w)
        return self.op(eng, lambda e: e.tensor_copy(out=out, in_=in_), r, w)

    def memset(self, eng, ap, val, w):
        return self.op(eng, lambda e: e.memset(ap, val), (), w)

    def psk(self, bank, q0=0, q1=4):
        return [("ps", bank)]

    def rview(self, off, n, dt):
        assert off % 4 == 0
        if dt is F32:
            assert off + 4 * n <= self.RBYTES, (off, n)
            return self.R[:, off // 4: off // 4 + n]
        assert n % 2 == 0 and off + 2 * n <= self.RBYTES, (off, n)
        return self.R[:, off // 4: off // 4 + n // 2].bitcast(BF16)

    def declare(self):
        TP, NS, DEPTH, NTOK = self.TP, self.NS, self.DEPTH, self.NTOK
        self.clay, self.CW = const_layout(self.NH)
        self.play, self.NPC = pcol_layout()
        i = {}
        i["x_p"] = self.din("x_p", [TP, D])
        i["x_s"] = self.din("x_s", [NS * 64, D])
        i["st_gla"] = self.din("st_gla", [DEPTH, NS, 2, 128, 128])
        for n in ("st_hg", "st_gdn", "st_ret"):
            i[n] = self.din(n, [DEPTH, NS, 4, 128, 128])
        i["c_gdnT"] = self.din("c_gdnT", [DEPTH, NS, 128, 12, 3])
        i["c_ffnT"] = self.din("c_ffnT", [DEPTH, NS, 128, FKT, 2])
        i["pcols"] = self.din("pcols", [128, self.NPC])
        i["consts"] = self.din("consts", [128, self.CW])
        i["wgk"] = self.din("wgk", [DEPTH, 16, 256])
        i["rotc"] = self.din("rotc", [128, TP + 64])
        i["rots"] = self.din("rots", [128, TP + 64])
        i["w_in"] = self.din("w_in", [DEPTH, D, NIN])
        i["w_branch"] = self.din("w_branch", [DEPTH, 4 * 512, D])
        i["w_out"] = self.din("w_out", [DEPTH, D, D])
        i["w_ffn_in"] = self.din("w_ffn_in", [DEPTH, D, 2 * DFF])
        i["w_ffn_out"] = self.din("w_ffn_out", [DEPTH, DFF, D])
        self.i = i
        o = {}
        o["y_p"] = self.dout("y_p", [TP, D])
        o["y_s"] = self.dout("y_s", [NS * 64, D])
        o["o_gla_p"] = self.dout("o_gla_p", [DEPTH, 2, 128, 128])
        for n in ("hg", "gdn", "ret"):
            o["o_%s_p" % n] = self.dout("o_%s_p" % n, [DEPTH, 4, 128, 128])
        o["o_cg_p"] = self.dout("o_cg_p", [DEPTH, 128, 12, 3])
        o["o_cf_p"] = self.dout("o_cf_p", [DEPTH, 128, FKT, 2])
        o["o_gla_s"] = self.dout("o_gla_s", [DEPTH, NS, 2, 128, 128])
        for n in ("hg", "gdn", "ret"):
            o["o_%s_s" % n] = self.dout("o_%s_s" % n, [DEPTH, NS, 4, 128, 128])
        o["o_cg_s"] = self.dout("o_cg_s", [DEPTH, NS, 128, 12, 3])
        o["o_cf_s"] = self.dout("o_cf_s", [DEPTH, NS, 128, FKT, 2])
        self.o = o
        self.xT = self.dscr("xT_scr", [KT, 128, NTOK], F32)
        self.wb = {n: self.dscr("wb_" + n, list(i[n].shape), BF16) for n in ("w_in", "w_branch", "w_out", "w_ffn_in", "w_ffn_out")}

    def dbg_out(self, name, sb_ap, shape, r):
        t = self.dout("dbg_" + name, shape)
        self.dbg[name] = t
        self.dma("pool", t, sb_ap, r=r)

    def build(self):
        nc = self.nc
        self.declare()
        with ExitStack() as st:
            self.st = st
            self.P = Prog(nc, st)
            self.alloc()
            self.setup()
            self.stage0()
            tiles = self.make_tiles()
            for l in range(self.DEPTH):
                for ti, tile in enumerate(tiles):
                    if "mix" in self.parts:
                        self.mixer_stage(l, ti, tile)
                    if "ffn" in self.parts:
                        self.ffn_stage(l, ti, tile)
            for ti, tile in enumerate(tiles):
                self.final_stage(ti, tile)
            self.P.emit_all()
        return nc

    def make_tiles(self):
        tiles = []
        npt = self.TP // self.NT
        for t in range(npt):
            nch = self.NT // 64
            tiles.append(dict(tok0=t * self.NT, N=self.NT, prompt=True,
                              chunks=[dict(stream=0, first=(t == 0 and c == 0), last=(t == npt - 1 and c == nch - 1),
                                           pos=t * self.NT + c * 64) for c in range(nch)],
                              nseg=1, seglen=self.NT, first_tile=(t == 0), last_tile=(t == npt - 1)))
        tiles.append(dict(tok0=self.TP, N=self.NS * 64, prompt=False,
                          chunks=[dict(stream=1 + s, first=True, last=True, pos=self.TP) for s in range(self.NS)],
                          nseg=self.NS, seglen=64, first_tile=True, last_tile=True))
        return tiles

    def alloc(self):
        nc, st = self.nc, self.st
        NT = self.NT
        self.C32 = self.sb("C32", [128, self.CW])
        self.PC = self.sb("PC", [128, self.NPC])
        self.identb = self.sb("identb", [128, 128], BF16)
        self.onesb = self.sb("onesb", [128, 128], BF16)
        self.rswapb = self.sb("rswapb", [128, 128], BF16)
        self.drv = self.sb("drv", [128, 32])
        self.hT = self.sb("hT", [128, KT, NT], BF16)
        NP = min(512, NT)
        n_a2 = 2 * KT * NT + max(76 * 1024 * self.NH // 512, 56 * 1024)
        n_a3 = 88 * NT + 16384
        n_b = 2 * FKT * NT + 2 * 4 * (NT + 2 * max(1, self.NS)) + 4 * NT + 2 * 2 * NT + 64
        n_norm = 4 * KT * NT + 8 * NP + 4 * NT + 16384
        self.RBYTES = max(n_a2, n_a3, n_b, n_norm, 32768)
        self.R = self.sb("R", [128, self.RBYTES // 4])
        self.wslot = [self.sb("wslot%d" % k, [128, WSLOT], BF16) for k in range(3)]
        self.wrr = 0
        self.S32 = self.sb("S32", [128, 14, 128])
        self.Sbf = self.sb("Sbf", [128, 14, 128], BF16)
        self.ghist = self.sb("ghist", [128, 12, 3])
        self.fhist = self.sb("fhist", [128, FKT, 2])
        self.wgk = self.sb("wgk_sb", [128, 2, 256], BF16)
        self.PS = [st.enter_context(nc.psum_tensor("psb%d" % k, [128, 512], F32)) for k in range(8)]

    def cst(self, name):
        o, w = self.clay[name]
        return self.C32[:, o:o + w]

    def pc(self, name, j0=0, j1=None):
        o, w = self.play[name]
        if j1 is None:
            j1 = w
        return self.PC[:, o + j0:o + j1]

    def setup(self):
        i = self.i
        self.dma("sp", self.C32[:], i["consts"], w=["C32"])
        self.dma("sp", self.PC[:], i["pcols"], w=["PC"])
        self.cp("dve", self.identb[:], self.cst("ident"), r=["C32"], w=["identb"])
        self.cp("dve", self.onesb[:], self.cst("ones"), r=["C32"], w=["onesb"])
        self.cp("dve", self.rswapb[:], self.cst("rswap"), r=["C32"], w=["rswapb"])
        for l in range(self.DEPTH):
            for name in ("w_in", "w_branch", "w_out", "w_ffn_in", "w_ffn_out"):
                src, dst = i[name], self.wb[name]
                rows = src.shape[1]
                step = 128
                cranges = [(0, 7680), (7680, 7704), (7704, NIN)] if name == "w_in" else [(0, src.shape[2])]
                for r0 in range(0, rows, step):
                    r1 = min(rows, r0 + step)
                    for (ca, cb) in cranges:
                        self.dma("pool", dst[l, r0:r1, ca:cb], src[l, r0:r1, ca:cb], w=[("wb", name, l, r0 // 128)],
                                 max_dma_last_dim=4096)
        self.P.fence(("R", "W"))
        wgk32 = self.rview(0, 512, F32).rearrange("p (l c) -> p l c", l=2)
        self.memset("pool", wgk32, 0.0, w=[("R", "wgk32")])
        for l in range(self.DEPTH):
            self.dma("sp", wgk32[112:128, l, :], i["wgk"][l], w=[("R", "wgk32")])
        self.cp("pool", self.wgk[:], wgk32, r=[("R", "wgk32")], w=["wgk"])
        dv = self.drv
        self.memset("dve", dv[:], 0.0, w=["drv"])
        for l in range(self.DEPTH):
            self.ts1("dve", dv[:, 2 * l:2 * l + 2], self.pc("bgk%d" % l), -1.0, ALU.mult, r=["PC"], w=["drv"])
        self.memset("dve", dv[:, 12:16], 1.0, w=["drv"])
        if self.DEPTH > 1:
            self.tt("dve", dv[:, 24:28], self.pc("lbz1"), self.pc("lbz0"), ALU.subtract, r=["PC"], w=["drv"])
            self.act(dv[:, 8:12], dv[:, 24:28], AF.Sigmoid, r=["drv"], w=["drv"])
            self.ts("dve", dv[:, 16:20], dv[:, 8:12], -1.0, 1.0, ALU.mult, ALU.add, r=["drv"], w=["drv"])
        for l in range(self.DEPTH):
            self.act(dv[:, 28:29], self.pc("alog%d" % l), AF.Exp, r=["PC", "drv"], w=["drv"])
            self.stt(dv[:, 20 + l:21 + l], dv[:, 28:29], -1.0, self.cst("m47"), ALU.mult, ALU.mult, r=["drv", "C32"], w=["drv"])

    def stage0(self):
        P = self.P
        P.fence(("R", "W"))
        nblk = self.NTOK // 128
        xin = [self.rview(k * 8192, 2048, F32) for k in range(2)]
        xo = [self.rview(16384 + k * 8192, 2048, F32) for k in range(2)]
        for b in range(nblk):
            t0 = b * 128
            k = b % 2
            src = self.i["x_p"][t0:t0 + 128, :] if t0 < self.TP else self.i["x_s"][t0 - self.TP:t0 - self.TP + 128, :]
            self.dma("sp", xin[k], src, w=[("R", "xin", k)])
            for g in range(4):
                bank = 4 * (b % 2) + g
                for j in range(4):
                    kt = g * 4 + j
                    self.tr(self.PS[bank][:, j * 128:(j + 1) * 128], xin[k][:, kt * 128:(kt + 1) * 128], self.cst("ident"),
                            r=[("R", "xin", k), "C32"], w=self.psk(bank, j, j + 1))
                eng = "act" if g % 2 == 0 else "dve"
                self.cp(eng, xo[k][:, g * 512:(g + 1) * 512], self.PS[bank][:, :], r=self.psk(bank), w=[("R", "xo", k)])
            self.dma("pool", self.xT[:, :, t0:t0 + 128].rearrange("kt p t -> p kt t"),
                     xo[k].rearrange("p (kt t) -> p kt t", kt=KT), r=[("R", "xo", k)], w=[("xT", (t0 // self.NT if t0 < self.TP else -1), j) for j in range(KT)])

    def xkey(self, tile, j):
        return ("xT", (tile["tok0"] // self.NT if tile["prompt"] else -1), j)

    def load_w(self, name, l, r0, nkt, c0, ncols, eng="sp"):
        assert nkt * ncols <= WSLOT
        s = self.wrr
        self.wrr = (self.wrr + 1) % len(self.wslot)
        view = self.wslot[s][:, 0:nkt * ncols].rearrange("p (k c) -> p k c", k=nkt)
        src = self.wb[name][l, r0:r0 + nkt * 128, c0:c0 + ncols].rearrange("(k p) c -> p k c", p=128)
        rk = [("wb", name, l, r0 // 128 + k) for k in range(nkt)]
        self.dma(eng, view, src, r=rk, w=[("w", s)])
        return view, ("w", s)

    def dense(self, wname, l, r0, nkt, blocks, act_fn, n0, N, epilogue, banks=(0, 1, 2, 3), wcols=None):
        pieces = [(a, min(N, a + 512)) for a in range(0, N, 512)]
        npc = len(pieces)
        nset = len(banks) // npc
        assert nset >= 1
        if wcols is None:
            wcols = max(128, min(512, (WSLOT // nkt) // 128 * 128))
        bi = 0
        cnt = getattr(self, "_dense_cnt", 0)
        while bi < len(blocks):
            grp = [blocks[bi]]
            while len(grp) * 128 < wcols and bi + len(grp) < len(blocks) and blocks[bi + len(grp)][0] == grp[-1][0] + 128:
                grp.append(blocks[bi + len(grp)])
            wv, wk = self.load_w(wname, l, r0, nkt, grp[0][0], 128 * len(grp))
            for gi, (c0, tag) in enumerate(grp):
                bset = [banks[(cnt % nset) * npc + p] for p in range(npc)]
                cnt += 1
                for kt in range(nkt):
                    for p, (a, b) in enumerate(pieces):
                        ap, ak = act_fn(kt, n0 + a, n0 + b)
                        self.mm(self.PS[bset[p]][:, 0:b - a], wv[:, kt, gi * 128:(gi + 1) * 128], ap,
                                kt == 0, kt == nkt - 1, r=[wk, ak], w=self.psk(bset[p]))
                epilogue(tag, [self.PS[bset[p]][:, 0:b - a] for p, (a, b) in enumerate(pieces)],
                         [self.psk(bset[p]) for p in range(npc)], pieces)
            bi += len(grp)
        self._dense_cnt = cnt

    def norm_stage(self, tile, gname, final=False):
        P = self.P
        P.fence(("R", "W"))
        N, tok0 = tile["N"], tile["tok0"]
        xall = self.rview(0, KT * N, F32).rearrange("p (k n) -> p k n", k=KT)
        sq = [self.rview(4 * KT * N + k * 4 * min(512, N), min(512, N), BF16) for k in range(2)]
        pieces = [(a, min(N, a + 512)) for a in range(0, N, 512)]
        for kt in range(KT):
            self.dma("sp", xall[:, kt, :], self.xT[kt, :, tok0:tok0 + N], r=[self.xkey(tile, kt)], w=[("R", "xall", kt)])
        c = 0
        for kt in range(KT):
            for p, (a, b) in enumerate(pieces):
                s = sq[c % 2]
                c += 1
                self.act(s[:, 0:b - a], xall[:, kt, a:b], AF.Square, r=[("R", "xall", kt)], w=[("R", "sq", (c - 1) % 2)])
                self.mm(self.PS[p][:, 0:b - a], self.onesb[:], s[:, 0:b - a], kt == 0, kt == KT - 1,
                        r=["onesb", ("R", "sq", (c - 1) % 2)], w=self.psk(p))
        rs = self.rview(4 * KT * N + 8 * min(512, N), N, F32)
        self.norm_end = 4 * KT * N + 8 * min(512, N) + 4 * N
        for p, (a, b) in enumerate(pieces):
            self.act(rs[:, a:b], self.PS[p][:, 0:b - a], AF.Ln, r=self.psk(p), w=[("R", "rstd")], bias=EPS, scale=1.0 / D)
            self.act(rs[:, a:b], rs[:, a:b], AF.Exp, r=[], w=[("R", "rstd")], scale=-0.5)
        for kt in range(KT):
            g = self.pc(gname, kt, kt + 1)
            if final:
                self.stt(xall[:, kt, :], xall[:, kt, :], g, rs[:, :], ALU.mult, ALU.mult,
                         r=["PC", ("R", "rstd"), ("R", "xall", kt)], w=[("R", "xall", kt)])
            else:
                self.stt(self.hT[:, kt, 0:N], xall[:, kt, :], g, rs[:, :], ALU.mult, ALU.mult,
                         r=["PC", ("R", "rstd"), ("R", "xall", kt)], w=[("hT", kt)])
        return xall

    def hT_fn(self, kt, a, b):
        return self.hT[:, kt, a:b], ("hT", kt)

    def resid_epilogue(self, tile, bufs, bkeys):
        N, tok0 = tile["N"], tile["tok0"]
        state = {"c": 0}
        def ep(tag, pss, pks, pieces):
            j = tag
            k = state["c"] % 2
            state["c"] += 1
            buf, bk = bufs[k], bkeys[k]
            xk = self.xkey(tile, j)
            self.dma("sp", buf, self.xT[j, :, tok0:tok0 + N], r=[xk], w=[bk])
            for p, (a, b) in enumerate(pieces):
                self.tt("dve", buf[:, a:b], buf[:, a:b], pss[p], ALU.add, r=pks[p] + [bk], w=[bk])
            self.dma("pool", self.xT[j, :, tok0:tok0 + N], buf, r=[bk], w=[xk])
        return ep

    def final_stage(self, ti, tile):
        N, tok0 = tile["N"], tile["tok0"]
        xall = self.norm_stage(tile, "nfin", final=True)
        yb = [self.rview(self.norm_end + k * 8192, 2048, F32) for k in range(2)]
        for tb in range(N // 128):
            k = tb % 2
            for g in range(4):
                bank = 4 * (tb % 2) + g
                for j in range(4):
                    kt = g * 4 + j
                    self.tr(self.PS[bank][:, j * 128:(j + 1) * 128], xall[:, kt, tb * 128:(tb + 1) * 128], self.cst("ident"),
                            r=[("R", "xall", kt), "C32"], w=self.psk(bank, j, j + 1))
                eng = "act" if g % 2 == 0 else "dve"
                self.cp(eng, yb[k][:, g * 512:(g + 1) * 512], self.PS[bank][:, :], r=self.psk(bank), w=[("R", "yb", k)])
            t0 = tok0 + tb * 128
            dst = self.o["y_p"][t0:t0 + 128, :] if t0 < self.TP else self.o["y_s"][t0 - self.TP:t0 - self.TP + 128, :]
            self.dma("pool", dst, yb[k], r=[("R", "yb", k)])

    def ffn_stage(self, l, ti, tile):
        N, tok0, nseg, L = tile["N"], tile["tok0"], tile["nseg"], tile["seglen"]
        self.norm_stage(tile, "nffn%d" % l)
        self.P.fence(("R", "W"))
        actT = self.rview(0, FKT * N, BF16).rearrange("p (k n) -> p k n", k=FKT)
        base = 2 * FKT * N
        W2 = nseg * (L + 2)
        W2a = (W2 + 1) // 2 * 2
        ae = [self.rview(base + k * 4 * W2a, W2, F32).rearrange("p (s t) -> p s t", s=nseg) for k in range(2)]
        base += 2 * 4 * W2a
        yv = self.rview(base, N, F32)
        base += 4 * N
        sv = [self.rview(base + k * 2 * N, N, BF16) for k in range(2)]
        fcw = lambda j, jf: self.pc("fcw%d" % l, j * FKT + jf, j * FKT + jf + 1)
        if tile["prompt"] and tile["first_tile"]:
            self.memset("pool", self.fhist[:], 0.0, w=["fhist"])

        def ep(tag, pss, pks, pieces):
            kind, jf = tag
            k = jf % 2
            if kind == "a":
                a_k, ak = ae[k], ("R", "ae", k)
                if tile["prompt"]:
                    self.cp("pool", a_k[:, 0, 0:2], self.fhist[:, jf, :], r=["fhist"], w=[ak])
                    for p, (a, b) in enumerate(pieces):
                        self.cp("act", a_k[:, 0, 2 + a:2 + b], pss[p], r=pks[p], w=[ak])
                else:
                    assert len(pieces) == 1
                    self.dma("sp", a_k[:, :, 0:2], self.i["c_ffnT"][l, :, :, jf, :].rearrange("s p j -> p s j"), w=[ak])
                    self.cp("act", a_k[:, :, 2:2 + L], pss[0].rearrange("p (s t) -> p s t", s=nseg), r=pks[0], w=[ak])
                y3 = yv.rearrange("p (s t) -> p s t", s=nseg)
                yk = ("R", "y")
                self.ts1("dve", y3, a_k[:, :, 0:L], fcw(0, jf), ALU.mult, r=[ak, "PC"], w=[yk])
                self.stt(y3, a_k[:, :, 1:L + 1], fcw(1, jf), y3, ALU.mult, ALU.add, r=[ak, "PC"], w=[yk])
                self.stt(y3, a_k[:, :, 2:L + 2], fcw(2, jf), y3, ALU.mult, ALU.add, r=[ak, "PC"], w=[yk])
                if tile["prompt"]:
                    self.cp("pool", self.fhist[:, jf, :], a_k[:, 0, L:L + 2], r=[ak], w=["fhist"])
                else:
                    self.dma("pool", self.o["o_cf_s"][l, :, :, jf, :].rearrange("s p j -> p s j"), a_k[:, :, L:L + 2], r=[ak])
                self.act(sv[k], yv, AF.Silu, r=[yk, "PC"], w=[("R", "s", k)], bias=self.pc("fcb%d" % l, jf, jf + 1))
            else:
                for p, (a, b) in enumerate(pieces):
                    self.tt("dve", actT[:, jf, a:b], pss[p], sv[k][:, a:b], ALU.mult, r=pks[p] + [("R", "s", k)], w=[("R", "actT", jf)])

        blocks = []
        for jp in range(0, FKT, 2):
            js = [j for j in (jp, jp + 1) if j < FKT]
            blocks += [(j * 128, ("a", j)) for j in js]
            blocks += [(DFF + j * 128, ("u", j)) for j in js]
        self.dense("w_ffn_in", l, 0, KT, blocks, self.hT_fn, 0, N, ep, wcols=256)
        if tile["prompt"] and tile["last_tile"]:
            self.dma("pool", self.o["o_cf_p"][l], self.fhist[:], r=["fhist"])
        rb = [self.rview(2 * FKT * N + k * 4 * N, N, F32) for k in range(2)]
        rkeys = [("R", "xres", 0), ("R", "xres", 1)]
        alias_r = [("R", "ae", 0), ("R", "ae", 1), ("R", "y"), ("R", "s", 0), ("R", "s", 1)]
        self.op("pool", lambda e: e.nop(), r=alias_r, w=rkeys)
        self.op("pool", lambda e: e.nop(), r=[], w=alias_r + rkeys)
        actfn = lambda kt, a, b: (actT[:, kt, a:b], ("R", "actT", kt))
        self.dense("w_ffn_out", l, 0, FKT, [(j * 128, j) for j in range(KT)], actfn, 0, N,
                   self.resid_epilogue(tile, rb, rkeys), wcols=128)

    def walloc(self, n, dt):
        nb = n * (4 if dt is F32 else 2)
        nb = (nb + 3) // 4 * 4
        v = self.rview(self.wptr, n if dt is F32 else (n + 1) // 2 * 2, dt)
        self.wptr += nb
        return v

    def dense_T(self, l, c0, ncols, h0, nh, vtok, vkey):
        cw = 256
        cnt = 0
        for cc in range(0, ncols, cw):
            wv, wk = self.load_w("w_in", l, 0, KT, c0 + cc, cw)
            for tb in range(nh // 128):
                bank = cnt % 2
                cnt += 1
                for kt in range(KT):
                    self.mm(self.PS[bank][:, 0:cw], self.hT[:, kt, h0 + tb * 128:h0 + (tb + 1) * 128], wv[:, kt, :],
                            kt == 0, kt == KT - 1, r=[wk, ("hT", kt)], w=self.psk(bank))
                self.cp("act", vtok[:, tb, cc:cc + cw], self.PS[bank][:, 0:cw], r=self.psk(bank), w=[vkey])

    def mixer_stage(self, l, ti, tile):
        N, tok0 = tile["N"], tile["tok0"]
        self.norm_stage(tile, "nmix%d" % l)
        self.P.fence(("R", "W"))
        oT = self.rview(0, KT * N, BF16).rearrange("p (k n) -> p k n", k=KT)
        self.WOFF = 2 * KT * N
        sel = self.cfg.get("mixers", ("gla", "hg", "ret", "gdn"))
        for kt in range(KT):
            if ("gla", "hg", "gdn", "ret")[kt // 4] not in sel:
                self.memset("pool", oT[:, kt, :], 0.0, w=[("R", "oT", kt)])
        for h0 in range(0, N, self.NH):
            nh = min(self.NH, N - h0)
            chunks = tile["chunks"][h0 // 64:(h0 + nh) // 64]
            if "gla" in sel:
                self.a2_gla(l, tile, h0, nh, chunks, oT)
            if "hg" in sel:
                self.a2_hg(l, tile, h0, nh, chunks, oT)
            if "ret" in sel:
                self.a2_ret(l, tile, h0, nh, chunks, oT)
            if "gdn" in sel:
                self.a2_gdn(l, tile, h0, nh, chunks, oT)
        self.P.fence("W")
        self.a3_merge(l, tile, oT)

    def state_io(self, l, mname, chunk, sidx, uidx, when):
        src = {"gla": "st_gla", "hg": "st_hg", "gdn": "st_gdn", "ret": "st_ret"}[mname]
        if when == "load":
            if chunk["stream"] == 0:
                self.memset("pool", self.S32[:, sidx, :], 0.0, w=[("S", sidx)])
                self.memset("pool", self.Sbf[:, sidx, :], 0.0, w=[("Sb", sidx)])
            else:
                s = chunk["stream"] - 1
                self.dma("sp", self.S32[:, sidx, :], self.i[src][l, s, uidx], w=[("S", sidx)])
                self.cp("act", self.Sbf[:, sidx, :], self.S32[:, sidx, :], r=[("S", sidx)], w=[("Sb", sidx)])
        else:
            if chunk["stream"] == 0:
                self.dma("pool", self.o["o_%s_p" % mname][l, uidx], self.S32[:, sidx, :], r=[("S", sidx)])
            else:
                s = chunk["stream"] - 1
                self.dma("pool", self.o["o_%s_s" % mname][l, s, uidx], self.S32[:, sidx, :], r=[("S", sidx)])

    def gla_scan(self, l, mname, mi, nh, chunks, units, vtok, vkey, oraw):
        nblk = nh // 128
        BO, BT = 5, 4
        BAq = lambda q: 2 + q % 2
        BSu = lambda u: 6 + u % 2
        att_sb = [self.walloc(128, BF16) for _ in range(4)]
        kdpA = [self.walloc(128, BF16) for _ in units]
        kdpB = [self.walloc(128, BF16) for _ in units]
        for u in range(len(units)):
            self.memset("pool", kdpA[u], 0.0, w=[("W", "kdpA", u)])
            self.memset("pool", kdpB[u], 0.0, w=[("W", "kdpB", u)])
        psT = self.PS[BT][:, :].bitcast(BF16)
        mask2 = self.cst("mask2")
        for tb in range(nblk):
            bs = slice(tb * 128, (tb + 1) * 128)
            for ui, U in enumerate(units):
                for hd in U["heads"]:
                    q = hd["h"]
                    self.mm(self.PS[BAq(q)][:, q * 128:(q + 1) * 128], hd["ktz"][:, bs], hd["qtz"][:, bs], True, True,
                            r=[hd["kkey"], hd["qkey"]], w=self.psk(BAq(q)))
                    self.tt("dve", att_sb[q], self.PS[BAq(q)][:, q * 128:(q + 1) * 128], mask2, ALU.mult,
                            r=self.psk(BAq(q)) + ["C32"], w=[("W", "att", q)])
                self.tr(psT[:, ui * 128:(ui + 1) * 128], U["kd"][:, bs], self.identb[:], r=[U["kdkey"], "identb"],
                        w=self.psk(BT, ui // 2, ui // 2 + 1))
                self.cp("act", kdpA[ui][0:64, :], psT[0:64, ui * 128:(ui + 1) * 128], r=self.psk(BT, ui // 2, ui // 2 + 1), w=[("W", "kdpA", ui)])
                self.cp("act", kdpB[ui][64:128, :], psT[64:128, ui * 128:(ui + 1) * 128], r=self.psk(BT, ui // 2, ui // 2 + 1), w=[("W", "kdpB", ui)])
            for ci in range(2):
                cidx = 2 * tb + ci
                ch = chunks[cidx]
                cs_ = slice(cidx * 64, (cidx + 1) * 64)
                for ui, U in enumerate(units):
                    sidx = U["sidx"]
                    if ch["first"]:
                        self.state_io(l, mname, ch, sidx, U["uidx"], "load")
                    for hd in U["heads"]:
                        q = hd["h"]
                        osl = self.PS[BO][:, q * 128 + ci * 64:q * 128 + ci * 64 + 64]
                        self.mm(osl, vtok[:, tb, hd["vq"] * 128:(hd["vq"] + 1) * 128], att_sb[q][:, ci * 64:(ci + 1) * 64], True, False,
                                r=[vkey, ("W", "att", q)], w=self.psk(BO, q, q + 1))
                        self.mm(osl, self.Sbf[:, sidx, :], hd["qtz"][:, cs_], False, True,
                                r=[("Sb", sidx), hd["qkey"]], w=self.psk(BO, q, q + 1))
                    kp = (kdpA if ci == 0 else kdpB)[ui]
                    kpk = ("W", "kdpA" if ci == 0 else "kdpB", ui)
                    vw = U["vw"]
                    nq = vw // 128
                    sps = self.PS[BSu(ui)][:, 0:vw]
                    spk = self.psk(BSu(ui))
                    self.mm(sps, kp, vtok[:, tb, U["vc0"]:U["vc0"] + vw], True, True, r=[kpk, vkey], w=spk)
                    dsc = U["dS"][:, cidx:cidx + 1]
                    if nq == 1:
                        self.stt(self.S32[:, sidx, :], self.S32[:, sidx, :], dsc, sps, ALU.mult, ALU.add,
                                 r=spk + [U["dskey"]], w=[("S", sidx)])
                    else:
                        self.stt(self.S32[0:64, sidx, :], self.S32[0:64, sidx, :], U["dS"][0:64, cidx:cidx + 1], sps[0:64, 0:128],
                                 ALU.mult, ALU.add, r=spk + [U["dskey"]], w=[("S", sidx)])
                        self.stt(self.S32[64:128, sidx, :], self.S32[64:128, sidx, :], U["dS"][64:128, cidx:cidx + 1], sps[64:128, 128:256],
                                 ALU.mult, ALU.add, r=spk + [U["dskey"]], w=[("S", sidx)])
                    self.cp("act", self.Sbf[:, sidx, :], self.S32[:, sidx, :], r=[("S", sidx)], w=[("Sb", sidx)])
                    if ch["last"]:
                        self.state_io(l, mname, ch, sidx, U["uidx"], "store")
            for U in units:
                for hd in U["heads"]:
                    q = hd["h"]
                    self.cp("act", oraw[q][:, bs], self.PS[BO][:, q * 128:(q + 1) * 128], r=self.psk(BO, q, q + 1), w=[("W", "oraw", q)])

    def head_norm(self, l, mi, nh, h0, oraw, sg, oT, gname, bname=None):
        BN = 6
        sq = self.walloc(nh, BF16)
        rs = self.walloc(nh, F32)
        t = self.walloc(nh, F32)
        ones = self.cst("ones")
        for h in range(4):
            ok = ("W", "oraw", h)
            src = oraw[h]
            if bname is not None:
                self.mm(self.PS[BN][:, 0:nh], ones, oraw[h], True, True, r=["C32", ok], w=self.psk(BN))
                self.stt(oraw[h], self.PS[BN][:, 0:nh], -1.0 / 128, oraw[h], ALU.mult, ALU.add, r=self.psk(BN) + [ok], w=[ok])
            self.act(sq, src, AF.Square, r=[ok], w=[("W", "hn_sq")])
            self.mm(self.PS[BN][:, 0:nh], self.onesb[:], sq, True, True, r=["onesb", ("W", "hn_sq")], w=self.psk(BN))
            self.act(rs, self.PS[BN][:, 0:nh], AF.Ln, r=self.psk(BN), w=[("W", "hn_rs")], bias=EPS, scale=1.0 / 128)
            self.act(rs, rs, AF.Exp, r=[], w=[("W", "hn_rs")], scale=-0.5)
            self.stt(t, src, self.pc(gname), rs, ALU.mult, ALU.mult, r=[ok, "PC", ("W", "hn_rs")], w=[("W", "hn_t")])
            okey = ("R", "oT", mi * 4 + h)
            if bname is None:
                self.tt("dve", oT[:, mi * 4 + h, h0:h0 + nh], t, sg[h], ALU.mult, r=[("W", "hn_t"), ("W", "sg", h)], w=[okey])
            else:
                self.stt(oT[:, mi * 4 + h, h0:h0 + nh], t, self.pc(bname), sg[h], ALU.add, ALU.mult,
                         r=[("W", "hn_t"), ("W", "sg", h), "PC"], w=[okey])

    def decay_ops(self, cs, cskey, nh, sb, eb, enb, kdec, dS, keytag):
        nch = nh // 64
        cs3 = cs.rearrange("p (c t) -> p c t", t=64)
        kd3 = kdec.rearrange("p (c t) -> p c t", t=64)
        if eb is not None:
            self.act(eb, cs, AF.Exp, r=[cskey], w=[("W", keytag, "eb")], scale=sb)
        self.act(enb, cs, AF.Exp, r=[cskey], w=[("W", keytag, "enb")], scale=-sb)
        self.act(dS[:, 0:nch], cs3[:, :, 63], AF.Exp, r=[cskey], w=[("W", keytag, "dS")], scale=sb)
        self.tt("dve", kd3, cs3[:, :, 63:64].broadcast_to([128, nch, 64]), cs3, ALU.subtract, r=[cskey], w=[("W", keytag, "kdec")])
        self.act(kdec, kdec, AF.Exp, r=[], w=[("W", keytag, "kdec")], scale=sb)

    def a2_gla(self, l, tile, h0, nh, chunks, oT):
        P = self.P
        P.fence("W")
        self.wptr = self.WOFF
        nch, nblk = nh // 64, nh // 128
        vtok = self.walloc(nblk * 512, BF16).rearrange("p (b c) -> p b c", b=nblk)
        ktz = [self.walloc(nh, BF16) for _ in range(4)]
        qtz = [self.walloc(nh, BF16) for _ in range(4)]
        kd = [self.walloc(nh, BF16) for _ in range(2)]
        sg = [self.walloc(nh, BF16) for _ in range(4)]
        oraw = [self.walloc(nh, F32) for _ in range(4)]
        dS = [self.walloc(max(nch, 2), F32) for _ in range(2)]
        lrT = self.walloc(nh, BF16)
        cs = self.walloc(nh, F32)
        eb = self.walloc(nh, F32)
        enb = self.walloc(nh, F32)
        kdec = self.walloc(nh, F32)
        tmp = self.walloc(nh, F32)
        rmask = self.cst("rmask")
        for h in range(4):
            self.memset("pool", ktz[h], 0.0, w=[("W", "ktz", h)])
            self.memset("pool", qtz[h], 0.0, w=[("W", "qtz", h)])
        self.dense_T(l, C_GLA_V, 512, h0, nh, vtok, ("W", "vtok"))

        def ep_lr(tag, pss, pks, pieces):
            self.cp("act", lrT, pss[0], r=pks[0], w=[("W", "lrT")])
        self.dense("w_in", l, 0, KT, [(C_GLA_LR + 16 - 128, 0)], self.hT_fn, h0, nh, ep_lr, banks=(0, 1))

        def ep_k(tag, pss, pks, pieces):
            jb = tag
            ps = pss[0]
            for hh in range(2):
                rs_ = slice(hh * 64, hh * 64 + 64)
                self.tt("dve", ktz[2 * jb + hh][rs_, :], ps[rs_, :], enb[rs_, :], ALU.mult,
                        r=pks[0] + [("W", "g", "enb")], w=[("W", "ktz", 2 * jb + hh)])
            self.tt("dve", kd[jb], ps, kdec, ALU.mult, r=pks[0] + [("W", "g", "kdec")], w=[("W", "kd", jb)])

        def ep_q(tag, pss, pks, pieces):
            jb = tag
            ps = pss[0]
            for hh in range(2):
                rs_ = slice(hh * 64, hh * 64 + 64)
                self.stt(qtz[2 * jb + hh][rs_, :], ps[rs_, :], 0.125, eb[rs_, :], ALU.mult, ALU.mult,
                         r=pks[0] + [("W", "g", "eb")], w=[("W", "qtz", 2 * jb + hh)])

        for jb in range(2):
            BG = 7
            self.mm(self.PS[BG][:, 0:nh], self.wgk[:, l, jb * 128:(jb + 1) * 128], lrT, True, True,
                    r=["wgk", ("W", "lrT")], w=self.psk(BG))
            self.act(tmp, self.PS[BG][:, 0:nh], AF.Exp, r=self.psk(BG) + ["drv"], w=[("W", "g", "tmp")],
                     scale=-1.0, bias=self.drv[:, 2 * l + jb:2 * l + jb + 1])
            self.act(tmp, tmp, AF.Ln, r=[], w=[("W", "g", "tmp")], bias=1.0)
            self.op("dve", lambda e, cs=cs, tmp=tmp: e.tensor_tensor_scan(out=cs, data0=rmask[:, 0:nh], data1=tmp, initial=0.0,
                                                                         op0=ALU.mult, op1=ALU.add),
                    r=[("W", "g", "tmp"), "C32"], w=[("W", "g", "cs")])
            self.decay_ops(cs, ("W", "g", "cs"), nh, -1.0 / 16, eb, enb, kdec, dS[jb], "g")
            self.dense("w_in", l, 0, KT, [(C_GLA_K + jb * 128, jb)], self.hT_fn, h0, nh, ep_k, banks=(0, 1))
            self.dense("w_in", l, 0, KT, [(C_GLA_Q + jb * 128, jb)], self.hT_fn, h0, nh, ep_q, banks=(0, 1))

        def ep_g(tag, pss, pks, pieces):
            self.act(sg[tag], pss[0], AF.Silu, r=pks[0], w=[("W", "sg", tag)])
        self.dense("w_in", l, 0, KT, [(C_GLA_G + h * 128, h) for h in range(4)], self.hT_fn, h0, nh, ep_g, banks=(0, 1))
        units = []
        for jb in range(2):
            units.append(dict(kd=kd[jb], kdkey=("W", "kd", jb), dS=dS[jb], dskey=("W", "g", "dS"), sidx=jb, uidx=jb,
                              vc0=jb * 256, vw=256,
                              heads=[dict(h=2 * jb + hh, ktz=ktz[2 * jb + hh], qtz=qtz[2 * jb + hh], kkey=("W", "ktz", 2 * jb + hh),
                                          qkey=("W", "qtz", 2 * jb + hh), vq=2 * jb + hh) for hh in range(2)]))
        if self.cfg.get("dbg") and l == 0 and tile["tok0"] == 0 and h0 == 0:
            self.dbg_out("cs", cs, [128, nh], [("W", "g", "cs")])
            self.dbg_out("eb", eb, [128, nh], [("W", "g", "eb")])
            self.dbg_out("kdec", kdec, [128, nh], [("W", "g", "kdec")])
            t32 = self.walloc(nh, F32)
            for nm, ap, key in (("ktz1", ktz[1], ("W", "ktz", 1)), ("qtz0", qtz[0], ("W", "qtz", 0)), ("kd0", kd[0], ("W", "kd", 0)), ("lrT", lrT, ("W", "lrT"))):
                t32 = self.walloc(nh, F32)
                self.cp("dve", t32, ap, r=[key], w=[("W", "dbg", nm)])
                self.dbg_out(nm, t32, [128, nh], [("W", "dbg", nm)])
            t33 = self.walloc(nblk * 512, F32)
            self.cp("dve", t33, vtok.rearrange("p b c -> p (b c)"), r=[("W", "vtok")], w=[("W", "dbg", "vtok")])
            self.dbg_out("vtok", t33, [128, nblk * 512], [("W", "dbg", "vtok")])
            self.dbg_out("dS0", dS[0], [128, 2], [("W", "g", "dS")])
        if self.cfg.get("stop") == "prep":
            return
        self.gla_scan(l, "gla", 0, nh, chunks, units, vtok, ("W", "vtok"), oraw)
        if self.cfg.get("dbg") and l == 0 and tile["tok0"] == 0 and h0 == 0:
            self.dbg_out("oraw0", oraw[0], [128, nh], [("W", "oraw", 0)])
            self.dbg_out("S0", self.S32[:, 0, :], [128, 128], [("S", 0)])
        if self.cfg.get("stop") == "scan":
            return
        self.head_norm(l, 0, nh, h0, oraw, sg, oT, "gla_ng%d" % l)

    def a2_hg(self, l, tile, h0, nh, chunks, oT):
        self.P.fence("W")
        self.wptr = self.WOFF
        nch, nblk = nh // 64, nh // 128
        vtok = self.walloc(nblk * 512, BF16).rearrange("p (b c) -> p b c", b=nblk)
        ktz = [self.walloc(nh, BF16) for _ in range(4)]
        qtz = [self.walloc(nh, BF16) for _ in range(4)]
        kd = [self.walloc(nh, BF16) for _ in range(4)]
        sg = [self.walloc(nh, BF16) for _ in range(4)]
        oraw = [self.walloc(nh, F32) for _ in range(4)]
        dS = [self.walloc(max(nch, 2), F32) for _ in range(4)]
        cs = self.walloc(nh, F32)
        eb = [self.walloc(nh, F32) for _ in range(2)]
        enb = self.walloc(nh, F32)
        kdec = self.walloc(nh, F32)
        tA = self.walloc(nh, F32)
        tB = self.walloc(nh, F32)
        rmask = self.cst("rmask")
        self.dense_T(l, C_HG_I, 512, h0, nh, vtok, ("W", "vtok"))

        def ep(tag, pss, pks, pieces):
            kind, h = tag
            ps = pss[0]
            if kind == "f":
                lb = self.drv[:, 4 + 4 * l + h:5 + 4 * l + h]
                oml = self.drv[:, 12 + 4 * l + h:13 + 4 * l + h]
                self.act(tA, ps, AF.Sigmoid, r=pks[0], w=[("W", "h", "tA")])
                self.ts("dve", tA, tA, oml, lb, ALU.mult, ALU.add, r=["drv"], w=[("W", "h", "tA")])
                self.act(tB, tA, AF.Ln, r=[("W", "h", "tA")], w=[("W", "h", "tB")])
                self.op("dve", lambda e: e.tensor_tensor_scan(out=cs, data0=rmask[:, 0:nh], data1=tB, initial=0.0,
                                                              op0=ALU.mult, op1=ALU.add),
                        r=[("W", "h", "tB"), "C32"], w=[("W", "h", "cs")])
                self.ts("dve", tA, tA, -1.0, 1.0, ALU.mult, ALU.add, r=[], w=[("W", "h", "tA")])
                self.decay_ops_h(cs, nh, eb[h % 2], enb, kdec, dS[h], h)
                self.tt("pool", ktz[h], tA, enb, ALU.mult, r=[("W", "h", "tA"), ("W", "h", "enb")], w=[("W", "ktz", h)])
                self.tt("pool", kd[h], tA, kdec, ALU.mult, r=[("W", "h", "tA"), ("W", "h", "kdec")], w=[("W", "kd", h)])
            elif kind == "q":
                self.act(tB, ps, AF.Silu, r=pks[0], w=[("W", "h", "tB")])
                self.tt("dve", qtz[h], tB, eb[h % 2], ALU.mult, r=[("W", "h", "tB"), ("W", "h", "eb", h % 2)], w=[("W", "qtz", h)])
            else:
                self.act(sg[h], ps, AF.Silu, r=pks[0], w=[("W", "sg", h)])
        blocks = []
        for h in range(4):
            blocks += [(C_HG_F + h * 128, ("f", h)), (C_HG_Q + h * 128, ("q", h))]
        blocks += [(C_HG_G + h * 128, ("g", h)) for h in range(4)]
        self.dense("w_in", l, 0, KT, blocks, self.hT_fn, h0, nh, ep, banks=(0, 1))
        units = [dict(kd=kd[h], kdkey=("W", "kd", h), dS=dS[h], dskey=("W", "h", "dS", h), sidx=2 + h, uidx=h, vc0=h * 128, vw=128,
                      heads=[dict(h=h, ktz=ktz[h], qtz=qtz[h], kkey=("W", "ktz", h), qkey=("W", "qtz", h), vq=h)]) for h in range(4)]
        self.gla_scan(l, "hg", 1, nh, chunks, units, vtok, ("W", "vtok"), oraw)
        self.head_norm(l, 1, nh, h0, oraw, sg, oT, "hg_ng%d" % l)

    def decay_ops_h(self, cs, nh, eb, enb, kdec, dS, h):
        nch = nh // 64
        cs3 = cs.rearrange("p (c t) -> p c t", t=64)
        kd3 = kdec.rearrange("p (c t) -> p c t", t=64)
        ck = ("W", "h", "cs")
        self.act(eb, cs, AF.Exp, r=[ck], w=[("W", "h", "eb", h % 2)])
        self.act(enb, cs, AF.Exp, r=[ck], w=[("W", "h", "enb")], scale=-1.0)
        self.act(dS[:, 0:nch], cs3[:, :, 63], AF.Exp, r=[ck], w=[("W", "h", "dS", h)])
        self.tt("dve", kd3, cs3[:, :, 63:64].broadcast_to([128, nch, 64]), cs3, ALU.subtract, r=[ck], w=[("W", "h", "kdec")])
        self.act(kdec, kdec, AF.Exp, r=[], w=[("W", "h", "kdec")])

    def load_w_swapped(self, l, c0):
        s = self.wrr
        self.wrr = (self.wrr + 1) % len(self.wslot)
        view = self.wslot[s][:, 0:KT * 128].rearrange("p (k c) -> p k c", k=KT)
        rk = [("wb", "w_in", l, k) for k in range(KT)]
        for (d0, s0) in ((0, c0 + 64), (64, c0)):
            src = self.wb["w_in"][l, :, s0:s0 + 64].rearrange("(k p) c -> p k c", p=128)
            self.dma("sp", view[:, :, d0:d0 + 64], src, r=rk, w=[("w", s)])
        return view, ("w", s)

    def a2_ret(self, l, tile, h0, nh, chunks, oT):
        self.P.fence("W")
        self.wptr = self.WOFF
        nch, nblk = nh // 64, nh // 128
        vtok = self.walloc(nblk * 512, BF16).rearrange("p (b c) -> p b c", b=nblk)
        ktz = [self.walloc(nh, BF16) for _ in range(4)]
        qtz = [self.walloc(nh, BF16) for _ in range(4)]
        kd = [self.walloc(nh, BF16) for _ in range(4)]
        sg = [self.walloc(nh, BF16) for _ in range(4)]
        oraw = [self.walloc(nh, F32) for _ in range(4)]
        dS = [self.walloc(max(nch, 2), F32) for _ in range(4)]
        rc = self.walloc(nh, F32)
        rsn = self.walloc(nh, F32)
        t1 = [self.walloc(nh, F32) for _ in range(2)]
        t2 = self.walloc(nh, F32)
        xb = [self.walloc(nh, BF16) for _ in range(2)]
        if tile["prompt"]:
            p0 = tile["tok0"] + h0
            self.dma("sp", rc, self.i["rotc"][:, p0:p0 + nh], w=[("W", "rc")])
            self.dma("sp", rsn, self.i["rots"][:, p0:p0 + nh], w=[("W", "rsn")])
        else:
            for c in range(nch):
                self.dma("sp", rc[:, c * 64:(c + 1) * 64], self.i["rotc"][:, self.TP:self.TP + 64], w=[("W", "rc")])
                self.dma("sp", rsn[:, c * 64:(c + 1) * 64], self.i["rots"][:, self.TP:self.TP + 64], w=[("W", "rsn")])
        for h in range(4):
            self.memset("pool", dS[h], math.exp(64 * math.log(1.0 - 2.0 ** (-5.0 - h))), w=[("W", "r", "dS", h)])
        self.dense_T(l, C_RET_V, 512, h0, nh, vtok, ("W", "vtok"))

        def tab(name, h):
            o, w = self.clay[name]
            return self.C32[:, o + h * 64:o + (h + 1) * 64].unsqueeze(1).broadcast_to([128, nch, 64])
        v3 = lambda ap: ap.rearrange("p (c t) -> p c t", t=64)

        def ep(tag, pss, pks, pieces):
            kind, h = tag
            ps = pss[0]
            i_ = 0 if kind[0] == "q" else 1
            if kind in ("q", "k"):
                self.cp("act", xb[i_], ps, r=pks[0], w=[("W", "r", "xb", i_)])
                self.tt("dve", t1[i_], ps, rc, ALU.mult, r=pks[0] + [("W", "rc")], w=[("W", "r", "t1", i_)])
            elif kind in ("qs", "ks"):
                self.tt("dve", t2, ps, rsn, ALU.mult, r=pks[0] + [("W", "rsn")], w=[("W", "r", "t2")])
                self.tt("pool", t1[i_], t1[i_], t2, ALU.add, r=[("W", "r", "t2")], w=[("W", "r", "t1", i_)])
                if kind == "qs":
                    self.tt("pool", v3(qtz[h]), v3(t1[i_]), tab("reteb", h), ALU.mult, r=["C32"], w=[("W", "qtz", h)])
                else:
                    self.tt("pool", v3(ktz[h]), v3(t1[i_]), tab("retenb", h), ALU.mult, r=["C32"], w=[("W", "ktz", h)])
                    self.tt("pool", v3(kd[h]), v3(t1[i_]), tab("retkd", h), ALU.mult, r=["C32"], w=[("W", "kd", h)])
            else:
                self.act(sg[h], ps, AF.Silu, r=pks[0], w=[("W", "sg", h)])

        for h in range(4):
            for (cbase, kind) in ((C_RET_Q, "q"), (C_RET_K, "k")):
                self.dense("w_in", l, 0, KT, [(cbase + h * 128, (kind, h))], self.hT_fn, h0, nh, ep, banks=(0, 1))
                bank = 6 + (h % 2)
                self.mm(self.PS[bank][:, 0:nh], self.rswapb[:], xb[0 if kind == "q" else 1], True, True,
                        r=["rswapb", ("W", "r", "xb", 0 if kind == "q" else 1)], w=self.psk(bank))
                ep((kind + "s", h), [self.PS[bank][:, 0:nh]], [self.psk(bank)], [(0, nh)])
        self.dense("w_in", l, 0, KT, [(C_RET_G + h * 128, ("g", h)) for h in range(4)], self.hT_fn, h0, nh, ep, banks=(0, 1))
        units = [dict(kd=kd[h], kdkey=("W", "kd", h), dS=dS[h], dskey=("W", "r", "dS", h), sidx=10 + h, uidx=h, vc0=h * 128, vw=128,
                      heads=[dict(h=h, ktz=ktz[h], qtz=qtz[h], kkey=("W", "ktz", h), qkey=("W", "qtz", h), vq=h)]) for h in range(4)]
        self.gla_scan(l, "ret", 3, nh, chunks, units, vtok, ("W", "vtok"), oraw)
        self.head_norm(l, 3, nh, h0, oraw, sg, oT, "ret_ng%d" % l, "ret_nb%d" % l)

    def a2_gdn(self, l, tile, h0, nh, chunks, oT):
        self.P.fence("W")
        self.wptr = self.WOFF
        nch, nblk = nh // 64, nh // 128
        if tile["nseg"] == 1:
            nsg, L = 1, nh
        else:
            nsg, L = nch, 64
        v4 = lambda ap: ap.rearrange("p (h n) -> p h n", h=4)
        qhat = v4(self.walloc(4 * nh, BF16))
        khat = v4(self.walloc(4 * nh, BF16))
        vT = v4(self.walloc(4 * nh, BF16))
        sg = [self.walloc(nh, BF16) for _ in range(4)]
        oraw = [self.walloc(nh, F32) for _ in range(4)]
        XW = nsg * (L + 3)
        xe = [self.walloc(XW + (XW % 2), F32)[:, 0:XW].rearrange("p (s t) -> p s t", s=nsg) for _ in range(2)]
        yv = self.walloc(nh, F32)
        sv = self.walloc(nh, F32)
        sqv = self.walloc(nh, F32)
        sqb = self.walloc(nh, BF16)
        BTt = self.walloc(nh, F32)
        CB = self.walloc(nh, F32)
        RS = self.walloc(nh, F32)
        EBT = self.walloc(nh, F32)
        KDT = self.walloc(nh, F32)
        dSg = self.walloc(4 * max(nch, 2), F32)
        rmask = self.cst("rmask")
        ones = self.cst("ones")
        for t_, nm in ((BTt, "BT"), (CB, "CB"), (RS, "RS"), (EBT, "EBT"), (KDT, "KDT")):
            self.memset("pool", t_, 0.0, w=[("W", nm)])
        gcw = lambda j, blk: self.pc("gcw%d" % l, j * 12 + blk, j * 12 + blk + 1)
        if tile["prompt"] and tile["first_tile"] and h0 == 0:
            self.memset("pool", self.ghist[:], 0.0, w=["ghist"])
        s_first = chunks[0]["stream"] - 1
        BN = 7

        def ep(tag, pss, pks, pieces):
            kind, idx = tag
            ps = pss[0]
            if kind == "x":
                blk = idx
                k = blk % 2
                x_k, xk = xe[k], ("W", "xe", k)
                if tile["prompt"]:
                    self.cp("pool", x_k[:, 0, 0:3], self.ghist[:, blk, :], r=["ghist"], w=[xk])
                else:
                    self.dma("sp", x_k[:, :, 0:3], self.i["c_gdnT"][l, s_first:s_first + nsg, :, blk, :].rearrange("s p j -> p s j"), w=[xk])
                self.cp("act", x_k[:, :, 3:3 + L], ps.rearrange("p (s t) -> p s t", s=nsg), r=pks[0], w=[xk])
                y3 = yv.rearrange("p (s t) -> p s t", s=nsg)
                yk = ("W", "yv")
                self.ts1("dve", y3, x_k[:, :, 0:L], gcw(0, blk), ALU.mult, r=[xk, "PC"], w=[yk])
                for j in range(1, 4):
                    self.stt(y3, x_k[:, :, j:j + L], gcw(j, blk), y3, ALU.mult, ALU.add, r=[xk, "PC"], w=[yk])
                if tile["prompt"]:
                    self.cp("pool", self.ghist[:, blk, :], x_k[:, 0, L:L + 3], r=[xk], w=["ghist"])
                else:
                    self.dma("pool", self.o["o_cg_s"][l, s_first:s_first + nsg, :, blk, :].rearrange("s p j -> p s j"), x_k[:, :, L:L + 3], r=[xk])
                h = blk % 4
                if blk < 8:
                    self.act(sv, yv, AF.Silu, r=[yk], w=[("W", "sv")])
                    self.act(sqb, sv, AF.Square, r=[], w=[("W", "sqb")])
                    self.mm(self.PS[BN][:, 0:nh], self.onesb[:], sqb, True, True, r=["onesb", ("W", "sqb")], w=self.psk(BN))
                    self.act(sqv, self.PS[BN][:, 0:nh], AF.Ln, r=self.psk(BN), w=[("W", "sqv")], bias=EPS)
                    self.act(sqv, sqv, AF.Exp, r=[], w=[("W", "sqv")], scale=-0.5)
                    if blk < 4:
                        self.stt(qhat[:, h, :], sv, 128 ** -0.5, sqv, ALU.mult, ALU.mult, r=[("W", "sv"), ("W", "sqv")], w=[("W", "qhat")])
                    else:
                        self.tt("dve", khat[:, h, :], sv, sqv, ALU.mult, r=[("W", "sv"), ("W", "sqv")], w=[("W", "khat")])
                else:
                    self.act(vT[:, h, :], yv, AF.Silu, r=[yk], w=[("W", "vT")])
            elif kind == "z":
                self.act(sg[idx], ps, AF.Silu, r=pks[0], w=[("W", "sg", idx)])
            else:
                R32 = slice(0, 32)
                self.act(BTt[R32, :], ps[R32, :], AF.Sigmoid, r=pks[0], w=[("W", "BT")])
                self.act(RS[R32, :], ps[R32, :], AF.Exp, r=pks[0] + ["PC"], w=[("W", "RS")], bias=self.pc("dtb%d" % l)[R32, :])
                self.act(RS[R32, :], RS[R32, :], AF.Ln, r=[], w=[("W", "RS")], bias=1.0)
                self.ts1("dve", RS[R32, :], RS[R32, :], self.drv[R32, 20 + l:21 + l], ALU.mult, r=["drv"], w=[("W", "RS")])
                self.op("dve", lambda e: e.tensor_tensor_scan(out=CB[R32, :], data0=rmask[R32, 0:nh], data1=RS[R32, :], initial=0.0,
                                                              op0=ALU.mult, op1=ALU.add), r=["C32"], w=[("W", "CB")])
                c3 = lambda ap: ap.rearrange("p (c t) -> p c t", t=64)
                self.tt("dve", c3(RS)[R32], c3(CB)[R32, :, 63:64].broadcast_to([32, nch, 64]), c3(CB)[R32], ALU.subtract,
                        r=[("W", "CB")], w=[("W", "RS")])
                self.act(EBT[R32, :], CB[R32, :], AF.Exp, r=[("W", "CB")], w=[("W", "EBT")])
                self.act(KDT[R32, :], RS[R32, :], AF.Exp, r=[("W", "RS")], w=[("W", "KDT")])

        blocks = [(C_GDN_B, ("ba", 0))] + [(C_GDN_QKV + b * 128, ("x", b)) for b in (4, 5, 6, 7, 0, 1, 2, 3, 8, 9, 10, 11)]
        blocks += [(C_GDN_Z + h * 128, ("z", h)) for h in range(4)]
        self.dense("w_in", l, 0, KT, blocks, self.hT_fn, h0, nh, ep, banks=(0, 1))
        if tile["prompt"] and tile["last_tile"] and h0 + nh == tile["N"]:
            self.dma("pool", self.o["o_cg_p"][l], self.ghist[:], r=["ghist"])
        osel = self.clay["onesel"][0]
        onesel = lambda q: self.C32[:, osel + q * 128:osel + (q + 1) * 128]
        sel = self.cst("sel")
        c3 = lambda ap: ap.rearrange("p (c t) -> p c t", t=64)
        for h in range(4):
            self.mm(self.PS[BN][:, h * nch:(h + 1) * nch], onesel(h), c3(EBT)[:, :, 63], True, True, r=["C32", ("W", "EBT")], w=self.psk(BN, 0, 1))
        self.cp("act", dSg[:, 0:4 * nch], self.PS[BN][:, 0:4 * nch], r=self.psk(BN, 0, 1), w=[("W", "dSg")])
        f4 = lambda: self.walloc(512, F32).rearrange("p (h n) -> p h n", h=4)
        b4 = lambda: self.walloc(512, BF16).rearrange("p (h n) -> p h n", h=4)
        colsb = self.walloc(16, F32)
        nbe = self.walloc(4, F32)
        g1, gp, X32, bv, tA_, tB_ = f4(), f4(), f4(), f4(), f4(), f4()
        attd, kdpA, kdpB, qe, u_sb, Am, Bm, Xm, A2, B2, rhs_sb = b4(), b4(), b4(), b4(), b4(), b4(), b4(), b4(), b4(), b4(), b4()
        self.memset("pool", kdpA, 0.0, w=[("W", "kdpA")])
        self.memset("pool", kdpB, 0.0, w=[("W", "kdpB")])
        bc4 = lambda ap: ap.unsqueeze(1).broadcast_to([128, 4, 128])
        colb = lambda q: colsb[:, 4 * q:4 * q + 4].unsqueeze(2).broadcast_to([128, 4, 128])
        psbank = lambda b: self.PS[b][:, :].rearrange("p (h n) -> p h n", h=4)
        psT = self.PS[2][:, :].bitcast(BF16)
        psTk = psT[:, 0:512].rearrange("p (h n) -> p h n", h=4)
        psTv = psT[:, 512:1024].rearrange("p (h n) -> p h n", h=4)
        for tb in range(nblk):
            bs = slice(tb * 128, (tb + 1) * 128)
            for q, (X, xn, sl) in enumerate(((CB, "CB", 0), (BTt, "BT", 4), (EBT, "EBT", 0), (KDT, "KDT", 0))):
                self.mm(self.PS[2][:, 4 * q:4 * q + 4], X[:, bs], sel[:, sl:sl + 4], True, True, r=[("W", xn), "C32"], w=self.psk(2))
            self.cp("act", colsb, self.PS[2][:, 0:16], r=self.psk(2), w=[("W", "colsb")])
            self.stt(nbe, colsb[:, 4:8], -1.0, colsb[:, 8:12], ALU.mult, ALU.mult, r=[("W", "colsb")], w=[("W", "nbe")])
            for h in range(4):
                self.mm(self.PS[3][:, h * 128:(h + 1) * 128], onesel(h), CB[:, bs], True, True, r=["C32", ("W", "CB")], w=self.psk(3))
                self.mm(self.PS[4][:, h * 128:(h + 1) * 128], onesel(4 + h), BTt[:, bs], True, True, r=["C32", ("W", "BT")], w=self.psk(4))
                self.mm(self.PS[5][:, h * 128:(h + 1) * 128], onesel(h), EBT[:, bs], True, True, r=["C32", ("W", "EBT")], w=self.psk(5))
            self.tt("dve", g1, psbank(3), colb(0), ALU.subtract, r=self.psk(3) + [("W", "colsb")], w=[("W", "g1")])
            self.ts1("dve", gp, g1, 0.0, ALU.max, r=[], w=[("W", "gp")])
            self.ts1("dve", g1, g1, 0.0, ALU.min, r=[], w=[("W", "g1")])
            self.act(gp, gp, AF.Exp, r=[("W", "gp")], w=[("W", "gp")], scale=-1.0)
            self.act(g1, g1, AF.Exp, r=[("W", "g1")], w=[("W", "g1")])
            for h in range(4):
                self.mm(self.PS[6][:, h * 128:(h + 1) * 128], khat[:, h, bs], khat[:, h, bs], True, True, r=[("W", "khat")], w=self.psk(6))
                self.mm(self.PS[7][:, h * 128:(h + 1) * 128], khat[:, h, bs], qhat[:, h, bs], True, True, r=[("W", "khat"), ("W", "qhat")], w=self.psk(7))
            self.tt("dve", tA_, psbank(6), gp, ALU.mult, r=self.psk(6) + [("W", "gp")], w=[("W", "tA_")])
            self.tt("dve", tA_, tA_, colb(1), ALU.mult, r=[("W", "colsb")], w=[("W", "tA_")])
            self.tt("pool", Am, tA_, bc4(self.cst("nml")), ALU.mult, r=["C32", ("W", "tA_")], w=[("W", "Am")])
            self.tt("dve", tB_, psbank(6), g1, ALU.mult, r=self.psk(6) + [("W", "g1")], w=[("W", "tB_")])
            self.tt("dve", tB_, tB_, psbank(4), ALU.mult, r=self.psk(4), w=[("W", "tB_")])
            self.tt("pool", tB_, tB_, bc4(self.cst("nmu")), ALU.mult, r=["C32"], w=[("W", "tB_")])
            self.cp("pool", Bm, tB_, r=[("W", "tB_")], w=[("W", "Bm")])
            self.tt("pool", X32, tB_, bc4(self.cst("ident")), ALU.add, r=["C32", ("W", "tB_")], w=[("W", "X32")])
            self.cp("pool", Xm, X32, r=[("W", "X32")], w=[("W", "Xm")])
            self.tt("dve", bv, psbank(7), g1, ALU.mult, r=self.psk(7) + [("W", "g1")], w=[("W", "bv")])
            self.tt("pool", attd, bv, bc4(self.cst("mask2")), ALU.mult, r=["C32"], w=[("W", "attd")])
            self.tt("dve", qe, qhat[:, :, bs], psbank(5), ALU.mult, r=self.psk(5) + [("W", "qhat")], w=[("W", "qe")])
            for j in range(5):
                for h in range(4):
                    self.mm(self.PS[3][:, h * 128:(h + 1) * 128], Bm[:, h, :], Am[:, h, :], True, True, r=[("W", "Am"), ("W", "Bm")], w=self.psk(3))
                if j < 4:
                    for h in range(4):
                        self.mm(self.PS[4][:, h * 128:(h + 1) * 128], Am[:, h, :], Bm[:, h, :], True, True, r=[("W", "Am"), ("W", "Bm")], w=self.psk(4))
                self.cp("act", A2, psbank(3), r=self.psk(3), w=[("W", "A2")])
                if j < 4:
                    self.cp("dve", B2, psbank(4), r=self.psk(4), w=[("W", "B2")])
                for h in range(4):
                    self.mm(self.PS[5][:, h * 128:(h + 1) * 128], A2[:, h, :], Xm[:, h, :], True, True, r=[("W", "A2"), ("W", "Xm")], w=self.psk(5))
                self.tt("dve", X32, X32, psbank(5), ALU.add, r=self.psk(5), w=[("W", "X32")])
                self.cp("pool", Xm, X32, r=[("W", "X32")], w=[("W", "Xm")])
                Am, A2 = A2, Am
                if j < 4:
                    Bm, B2 = B2, Bm
                self.op("pool", lambda e: e.nop(), r=[("W", "Am"), ("W", "A2"), ("W", "Bm"), ("W", "B2")],
                        w=[("W", "Am"), ("W", "A2"), ("W", "Bm"), ("W", "B2")])
            for h in range(4):
                self.tr(psTk[:, h, :], khat[:, h, bs], self.identb[:], r=[("W", "khat"), "identb"], w=self.psk(2, 0, 2))
                self.tr(psTv[:, h, :], vT[:, h, bs], self.identb[:], r=[("W", "vT"), "identb"], w=self.psk(2, 2, 4))
            kdc = colsb[:, 12:16].unsqueeze(2).broadcast_to([128, 4, 128])
            self.tt("dve", kdpA[0:64], psTk[0:64], kdc[0:64], ALU.mult, r=self.psk(2, 0, 2) + [("W", "colsb")], w=[("W", "kdpA")])
            self.tt("dve", kdpB[64:128], psTk[64:128], kdc[64:128], ALU.mult, r=self.psk(2, 0, 2) + [("W", "colsb")], w=[("W", "kdpB")])
            self.tt("dve", bv, psTv, colb(1), ALU.mult, r=self.psk(2, 2, 4) + [("W", "colsb")], w=[("W", "bv")])
            for ci in range(2):
                cidx = 2 * tb + ci
                ch = chunks[cidx]
                for h in range(4):
                    sidx = 6 + h
                    if ch["first"]:
                        self.state_io(l, "gdn", ch, sidx, h, "load")
                    hs = slice(h * 128, (h + 1) * 128)
                    self.mm(self.PS[6][:, hs], khat[:, h, bs], self.Sbf[:, sidx, :], True, True, r=[("W", "khat"), ("Sb", sidx)], w=self.psk(6, h, h + 1))
                    self.stt(rhs_sb[:, h, :], self.PS[6][:, hs], nbe[:, h:h + 1], bv[:, h, :], ALU.mult, ALU.add,
                             r=self.psk(6, h, h + 1) + [("W", "nbe"), ("W", "bv")], w=[("W", "rhs", h)])
                    self.mm(self.PS[7][:, hs], Xm[:, h, :], rhs_sb[:, h, :], True, True, r=[("W", "Xm"), ("W", "rhs", h)], w=self.psk(7, h, h + 1))
                    self.cp("act", u_sb[:, h, :], self.PS[7][:, hs], r=self.psk(7, h, h + 1), w=[("W", "u", h)])
                    osl = self.PS[3][:, h * 128 + ci * 64:h * 128 + ci * 64 + 64]
                    self.mm(osl, u_sb[:, h, :], attd[:, h, ci * 64:(ci + 1) * 64], True, False, r=[("W", "u", h), ("W", "attd")], w=self.psk(3, h, h + 1))
                    self.mm(osl, self.Sbf[:, sidx, :], qe[:, h, ci * 64:(ci + 1) * 64], False, True, r=[("Sb", sidx), ("W", "qe")], w=self.psk(3, h, h + 1))
                    kp = kdpA if ci == 0 else kdpB
                    self.mm(self.PS[4][:, hs], kp[:, h, :], u_sb[:, h, :], True, True, r=[("W", "kdpA" if ci == 0 else "kdpB"), ("W", "u", h)], w=self.psk(4, h, h + 1))
                    self.stt(self.S32[:, sidx, :], self.S32[:, sidx, :], dSg[:, h * nch + cidx:h * nch + cidx + 1], self.PS[4][:, hs],
                             ALU.mult, ALU.add, r=self.psk(4, h, h + 1) + [("W", "dSg")], w=[("S", sidx)])
                    self.cp("act", self.Sbf[:, sidx, :], self.S32[:, sidx, :], r=[("S", sidx)], w=[("Sb", sidx)])
                    if ch["last"]:
                        self.state_io(l, "gdn", ch, sidx, h, "store")
            for h in range(4):
                self.cp("act", oraw[h][:, bs], self.PS[3][:, h * 128:(h + 1) * 128], r=self.psk(3, h, h + 1), w=[("W", "oraw", h)])
        self.head_norm(l, 2, nh, h0, oraw, sg, oT, "gdn_ng%d" % l)

    def a3_merge(self, l, tile, oT):
        N, tok0 = tile["N"], tile["tok0"]
        mixT = self.rview(2 * KT * N, KT * N, BF16).rearrange("p (k n) -> p k n", k=KT)
        base = 4 * KT * N
        acc = [self.rview(base + j * 4 * N, N, F32) for j in range(4)]
        base += 16 * N
        sgt = [self.rview(base + k * 4 * N, N, F32) for k in range(2)]
        base += 8 * N
        aux = self.rview(base, 16 * 512, BF16).rearrange("p (k c) -> p k c", k=16)
        ofn = lambda kt, a, b: (oT[:, kt, a:b], ("R", "oT", kt))
        pieces = [(a, min(N, a + 512)) for a in range(0, N, 512)]
        cnt = 0
        for jg in range(4):
            for n in range(4):
                self.dma("sp", aux[:, n * 4:(n + 1) * 4, :],
                         self.wb["w_branch"][l, n * 512:(n + 1) * 512, jg * 512:(jg + 1) * 512].rearrange("(k p) c -> p k c", p=128),
                         r=[("wb", "w_branch", l, n * 4 + k) for k in range(4)], w=[("W", "aux")])
            for n in range(4):
                for half in range(2):
                    wv, wk = self.load_w("w_in", l, 0, KT, C_MERGE + n * D + jg * 512 + half * 256, 256)
                    for jj in range(2):
                        jl = half * 2 + jj
                        j = jg * 4 + jl
                        gb = [(2 * (cnt % 2) + p) for p in range(len(pieces))]
                        bb = [4 + (2 * (cnt % 2) + p) for p in range(len(pieces))]
                        cnt += 1
                        for kt in range(KT):
                            for p, (a, b) in enumerate(pieces):
                                self.mm(self.PS[gb[p]][:, 0:b - a], wv[:, kt, jj * 128:(jj + 1) * 128], self.hT[:, kt, a:b],
                                        kt == 0, kt == KT - 1, r=[wk, ("hT", kt)], w=self.psk(gb[p]))
                        for kk in range(4):
                            for p, (a, b) in enumerate(pieces):
                                self.mm(self.PS[bb[p]][:, 0:b - a], aux[:, n * 4 + kk, jl * 128:(jl + 1) * 128], oT[:, n * 4 + kk, a:b],
                                        kk == 0, kk == 3, r=[("W", "aux"), ("R", "oT", n * 4 + kk)], w=self.psk(bb[p]))
                        sk = cnt % 2
                        for p, (a, b) in enumerate(pieces):
                            self.act(sgt[sk][:, a:b], self.PS[gb[p]][:, 0:b - a], AF.Sigmoid, r=self.psk(gb[p]), w=[("W", "sgt", sk)])
                            if n == 0:
                                self.tt("dve", acc[jl][:, a:b], self.PS[bb[p]][:, 0:b - a], sgt[sk][:, a:b], ALU.mult,
                                        r=self.psk(bb[p]) + [("W", "sgt", sk)], w=[("W", "acc", jl)])
                            else:
                                self.tt("dve", sgt[sk][:, a:b], self.PS[bb[p]][:, 0:b - a], sgt[sk][:, a:b], ALU.mult,
                                        r=self.psk(bb[p]), w=[("W", "sgt", sk)])
                                if n < 3:
                                    self.tt("pool", acc[jl][:, a:b], acc[jl][:, a:b], sgt[sk][:, a:b], ALU.add,
                                            r=[("W", "sgt", sk)], w=[("W", "acc", jl)])
                                else:
                                    self.tt("pool", mixT[:, j, a:b], acc[jl][:, a:b], sgt[sk][:, a:b], ALU.add,
                                            r=[("W", "sgt", sk), ("W", "acc", jl)], w=[("W", "mixT", j)])
        rb = sgt
        rkeys = [("W", "sgt", 0), ("W", "sgt", 1)]
        mfn = lambda kt, a, b: (mixT[:, kt, a:b], ("W", "mixT", kt))
        self.dense("w_out", l, 0, KT, [(j * 128, j) for j in range(KT)], mfn, 0, N, self.resid_epilogue(tile, rb, rkeys),
                   banks=(0, 1, 2, 3))


PAST_LEN = 4096


def core_inputs(inp, c, cfg, shared):
    TP, NS = cfg["TP"], cfg["NS"]
    f = lambda a: np.ascontiguousarray(np.asarray(a, np.float32))
    m = dict(shared)
    xp = np.asarray(inp["x_prompt"])
    if c < xp.shape[0]:
        m["x_p"] = f(xp[c, :TP])
    else:
        m["x_p"] = np.zeros((TP, D), np.float32)
    s0, s1 = c * NS, (c + 1) * NS
    m["x_s"] = f(np.asarray(inp["x_sample"])[s0:s1].reshape(NS * 64, D))
    m["st_gla"] = f(np.asarray(inp["state_gla"])[:, s0:s1].reshape(-1, NS, 2, 128, 128))
    m["st_hg"] = f(np.asarray(inp["state_hgrn"])[:, s0:s1])
    m["st_gdn"] = f(np.asarray(inp["state_gdn"])[:, s0:s1])
    m["st_ret"] = f(np.asarray(inp["state_ret"])[:, s0:s1])
    cg = np.asarray(inp["cache_gdn_conv"])[:, s0:s1]
    m["c_gdnT"] = f(cg.reshape(cg.shape[0], NS, 3, 12, 128).transpose(0, 1, 4, 3, 2))
    cf = np.asarray(inp["cache_ffn_conv"])[:, s0:s1]
    m["c_ffnT"] = f(cf.reshape(cf.shape[0], NS, 2, FKT, 128).transpose(0, 1, 4, 3, 2))
    return m


def shared_inputs(inp, cfg):
    TP = cfg["TP"]
    NH = min(512, cfg["NT"])
    f = lambda a: np.ascontiguousarray(np.asarray(a, np.float32))
    sh = {}
    sh["pcols"] = make_pcols(inp)
    sh["consts"] = make_consts(NH)
    sh["wgk"] = f(inp["gla_w_gk"])
    rc, rs = rot_tables(TP, PAST_LEN)
    sh["rotc"], sh["rots"] = rc, rs
    sh["w_in"] = f(inp["w_in"])
    wbr = np.asarray(inp["w_branch"], np.float32)
    sh["w_branch"] = np.ascontiguousarray(wbr.reshape(wbr.shape[0], 4 * 512, D))
    sh["w_out"] = f(inp["w_out"])
    sh["w_ffn_in"] = f(inp["w_ffn_in"])
    sh["w_ffn_out"] = f(inp["w_ffn_out"])
    return sh


_NC_CACHE = {}


def get_nc(cfg):
    key = repr(sorted(cfg.items()))
    if key not in _NC_CACHE:
        b = Builder(cfg)
        nc = b.build()
        _NC_CACHE[key] = (nc, b)
    return _NC_CACHE[key]


def kernel(**inputs):
    cfg = dict(TP=8192, NS=4, NT=1024, DEPTH=2)
    nc, b = get_nc(cfg)
    sh = shared_inputs(inputs, cfg)
    in_maps = [core_inputs(inputs, c, cfg, sh) for c in range(8)]
    res = run_bass_kernel_spmd(nc, in_maps, core_ids=list(range(8)))
    R = res.results
    L = cfg["DEPTH"]
    y_p = np.stack([R[c]["y_p"] for c in range(2)])
    y_s = np.concatenate([R[c]["y_s"].reshape(4, 64, D) for c in range(8)])
    outs = [y_p, y_s]
    outs.append(np.stack([R[c]["o_gla_p"].reshape(L, 4, 64, 128) for c in range(2)], axis=1))
    for n in ("hg", "gdn", "ret"):
        outs.append(np.stack([R[c]["o_%s_p" % n] for c in range(2)], axis=1))
    outs.append(np.stack([R[c]["o_cg_p"].transpose(0, 3, 2, 1).reshape(L, 3, 1536) for c in range(2)], axis=1))
    outs.append(np.stack([R[c]["o_cf_p"].transpose(0, 3, 2, 1).reshape(L, 2, DFF) for c in range(2)], axis=1))
    outs.append(np.concatenate([R[c]["o_gla_s"].reshape(L, 4, 4, 64, 128) for c in range(8)], axis=1))
    for n in ("hg", "gdn", "ret"):
        outs.append(np.concatenate([R[c]["o_%s_s" % n] for c in range(8)], axis=1))
    outs.append(np.concatenate([R[c]["o_cg_s"].transpose(0, 1, 4, 3, 2).reshape(L, 4, 3, 1536) for c in range(8)], axis=1))
    outs.append(np.concatenate([R[c]["o_cf_s"].transpose(0, 1, 4, 3, 2).reshape(L, 4, 2, DFF) for c in range(8)], axis=1))
    return tuple(np.ascontiguousarray(o, dtype=np.float32) for o in outs)
```

```python
import math
from contextlib import ExitStack

import numpy as np
import concourse.bass as bass
import concourse.mybir as mybir
from concourse.bass_utils import run_bass_kernel_spmd

F32 = mybir.dt.float32
BF16 = mybir.dt.bfloat16
AF = mybir.ActivationFunctionType
ALU = mybir.AluOpType

ENGS = ("pe", "act", "dve", "pool", "sp")

D = 2048
KT = 16
NIN = 15896
DFF = 5504
FKT = 43
H = 4
EPS = 1e-6
C_GLA_Q, C_GLA_K, C_GLA_V, C_GLA_G, C_GLA_LR = 0, 256, 512, 1024, 1536
C_HG_Q, C_HG_F, C_HG_I, C_HG_G = 1552, 2064, 2576, 3088
C_GDN_QKV, C_GDN_Z, C_GDN_B = 3600, 5136, 5648
C_RET_Q, C_RET_K, C_RET_V, C_RET_G = 5656, 6168, 6680, 7192
C_MERGE = 7704
WSLOT = 5632


class Op:
    __slots__ = ("eng", "emit", "waits", "sig", "is_dma")


class Prog:
    def __init__(self, nc, stack, n_dma_sems=(("sp", 24), ("pool", 24), ("act", 8))):
        self.nc = nc
        self.ops = {e: [] for e in ENGS}
        self.res = {}
        self.esem = {e: stack.enter_context(nc.semaphore("s_" + e)) for e in ENGS}
        self.ecount = {e: 0 for e in ENGS}
        self.dsem, self.dstate, self.drr = {}, {}, {}
        for e, n in n_dma_sems:
            self.dsem[e] = [stack.enter_context(nc.semaphore("d_%s%d" % (e, i))) for i in range(n)]
            self.dstate[e] = [[0, None] for _ in range(n)]
            self.drr[e] = 0
        self.waited = {e: {} for e in ENGS}
        self.frontier = {}
        self.groups = {}
        self.nops = 0

    def _need(self, op, src):
        if src is None or src is op:
            return
        if src.eng == op.eng and not src.is_dma:
            return
        sem, val = src.sig
        w = self.waited[op.eng]
        k = id(sem)
        if w.get(k, -1) >= val:
            return
        w[k] = val
        op.waits.append((sem, val))

    def _get(self, k):
        r = self.res.get(k)
        if r is None:
            r = [None, []]
            if isinstance(k, tuple) and k and k[0] in self.frontier:
                r[1] = list(self.frontier[k[0]])
            self.res[k] = r
            if isinstance(k, tuple) and k:
                self.groups.setdefault(k[0], set()).add(k)
        return r

    def fence(self, groups):
        if isinstance(groups, str):
            groups = (groups,)
        fr = {}
        for group in groups:
            for k in self.groups.get(group, ()):
                r = self.res.pop(k)
                if r[0] is not None:
                    fr[id(r[0])] = r[0]
                for o in r[1]:
                    fr[id(o)] = o
            for o in self.frontier.get(group, ()):
                fr[id(o)] = o
        best = {}
        for o in fr.values():
            sem, val = o.sig
            b = best.get(id(sem))
            if b is None or b.sig[1] < val:
                best[id(sem)] = o
        for group in groups:
            self.frontier[group] = list(best.values())
            self.groups[group] = set()

    def add(self, eng, emit, reads=(), writes=(), is_dma=False):
        op = Op()
        op.eng, op.emit, op.waits, op.is_dma = eng, emit, [], is_dma
        self.nops += 1
        if is_dma:
            pool = self.dstate[eng]
            i = self.drr[eng]
            self.drr[eng] = (i + 1) % len(pool)
            st = pool[i]
            if st[1] is not None:
                sem, val = st[1].sig
                w = self.waited[eng]
                if w.get(id(sem), -1) < val:
                    w[id(sem)] = val
                    op.waits.append((sem, val))
            st[0] += 16
            st[1] = op
            op.sig = (self.dsem[eng][i], st[0])
        else:
            self.ecount[eng] += 1
            op.sig = (self.esem[eng], self.ecount[eng])
        pr = [k for k in reads if isinstance(k, tuple) and k[0] == "ps"]
        if pr:
            reads = [k for k in reads if not (isinstance(k, tuple) and k[0] == "ps")]
            writes = list(writes) + pr
        for k in reads:
            r = self._get(k)
            self._need(op, r[0])
            r[1].append(op)
        for k in writes:
            r = self._get(k)
            self._need(op, r[0])
            for rd in r[1]:
                self._need(op, rd)
            r[0] = op
            r[1] = []
        self.ops[eng].append(op)
        return op

    def emit_all(self, final_eng="sp"):
        nc = self.nc
        finals = []
        for e in ENGS:
            if self.ecount[e] > 0 and e != final_eng:
                finals.append((self.esem[e], self.ecount[e]))
        for e in self.dsem:
            for i, st in enumerate(self.dstate[e]):
                if st[0] > 0:
                    finals.append((self.dsem[e][i], st[0]))
        with nc.Block() as block:
            def mk(e):
                def body(eng):
                    for o in self.ops[e]:
                        for (sem, val) in o.waits:
                            eng.wait_ge(sem, val)
                        ins = o.emit(eng)
                        ins.then_inc(o.sig[0], 16 if o.is_dma else 1)
                    if e == final_eng:
                        for (sem, val) in finals:
                            eng.wait_ge(sem, val)
                return body
            block.tensor(mk("pe"))
            block.scalar(mk("act"))
            block.vector(mk("dve"))
            block.gpsimd(mk("pool"))
            block.sync(mk("sp"))


def const_layout(NH):
    lay, off = {}, 0
    for name, w in (("ident", 128), ("ones", 128), ("mask2", 128), ("nml", 128), ("nmu", 128),
                    ("pmui", 128), ("sel", 8), ("onesel", 1024), ("rmask", NH), ("m47", 1),
                    ("reteb", 256), ("retenb", 256), ("retkd", 256), ("rswap", 128)):
        lay[name] = (off, w)
        off += w
    return lay, off


def make_consts(NH):
    lay, cw = const_layout(NH)
    c = np.zeros((128, cw), np.float32)
    p = np.arange(128)[:, None]
    f = np.arange(128)[None, :]
    same = (p // 64) == (f // 64)
    c[:, lay["ident"][0]:lay["ident"][0] + 128] = np.eye(128)
    c[:, lay["ones"][0]:lay["ones"][0] + 128] = 1.0
    c[:, lay["mask2"][0]:lay["mask2"][0] + 128] = (same & (p <= f)).astype(np.float32)
    BIG = 1.0e4
    c[:, lay["nml"][0]:lay["nml"][0] + 128] = BIG * (~(same & (f < p))).astype(np.float32)
    c[:, lay["nmu"][0]:lay["nmu"][0] + 128] = BIG * (~(same & (p < f))).astype(np.float32)
    c[:, lay["pmui"][0]:lay["pmui"][0] + 128] = BIG * (~(same & (p <= f))).astype(np.float32)
    so = lay["sel"][0]
    for h in range(4):
        c[4 + h, so + h] = 1.0
        c[h, so + 4 + h] = 1.0
    oo = lay["onesel"][0]
    for h in range(4):
        c[4 + h, oo + h * 128:oo + (h + 1) * 128] = 1.0
        c[h, oo + (4 + h) * 128:oo + (5 + h) * 128] = 1.0
    ro = lay["rmask"][0]
    rm = np.ones(NH, np.float32)
    rm[::64] = 0.0
    c[:, ro:ro + NH] = rm[None, :]
    c[4:8, lay["m47"][0]] = 1.0
    for m_ in range(128):
        c[(m_ + 64) % 128, lay["rswap"][0] + m_] = 1.0
    t = np.arange(64, dtype=np.float64)
    for h in range(4):
        lg = math.log(1.0 - 2.0 ** (-5.0 - h))
        c[:, lay["reteb"][0] + h * 64:lay["reteb"][0] + (h + 1) * 64] = np.exp(lg * (t + 1))[None, :]
        c[:, lay["retenb"][0] + h * 64:lay["retenb"][0] + (h + 1) * 64] = (np.exp(-lg * (t + 1)) * 128 ** -0.5)[None, :]
        c[:, lay["retkd"][0] + h * 64:lay["retkd"][0] + (h + 1) * 64] = (np.exp(lg * (63 - t)) * 128 ** -0.5)[None, :]
    return c


def pcol_layout():
    lay, off = {}, 0
    def add(name, w):
        nonlocal off
        lay[name] = (off, w)
        off += w
    for l in range(2):
        add("nmix%d" % l, 16)
        add("nffn%d" % l, 16)
        add("bgk%d" % l, 2)
        add("gla_ng%d" % l, 1)
        add("hg_ng%d" % l, 1)
        add("gdn_ng%d" % l, 1)
        add("ret_ng%d" % l, 1)
        add("ret_nb%d" % l, 1)
        add("lbz%d" % l, 4)
        add("gcw%d" % l, 48)
        add("fcw%d" % l, 129)
        add("fcb%d" % l, 43)
        add("dtb%d" % l, 1)
        add("alog%d" % l, 1)
    add("nfin", 16)
    return lay, off


def cols(v):
    return np.ascontiguousarray(np.asarray(v, np.float32).reshape(-1, 128).T)


def make_pcols(inp):
    lay, n = pcol_layout()
    t = np.zeros((128, n), np.float32)
    def put(name, arr):
        o, w = lay[name]
        assert arr.shape == (128, w), (name, arr.shape, w)
        t[:, o:o + w] = arr
    for l in range(2):
        put("nmix%d" % l, cols(inp["norm_mix_g"][l]))
        put("nffn%d" % l, cols(inp["norm_ffn_g"][l]))
        put("bgk%d" % l, cols(inp["gla_b_gk"][l]))
        put("gla_ng%d" % l, cols(inp["gla_norm_g"][l]))
        put("hg_ng%d" % l, cols(inp["hgrn_norm_g"][l]))
        put("gdn_ng%d" % l, cols(inp["gdn_norm_g"][l]))
        put("ret_ng%d" % l, cols(inp["ret_norm_g"][l]))
        put("ret_nb%d" % l, cols(inp["ret_norm_b"][l]))
        put("lbz%d" % l, cols(inp["hgrn_lb_logits"][l]))
        put("gcw%d" % l, np.concatenate([cols(inp["gdn_conv_w"][l][j]) for j in range(4)], axis=1))
        put("fcw%d" % l, np.concatenate([cols(inp["ffn_conv_w"][l][j]) for j in range(3)], axis=1))
        put("fcb%d" % l, cols(inp["ffn_conv_b"][l]))
        a = np.zeros((128, 1), np.float32)
        a[4:8, 0] = np.asarray(inp["gdn_dt_bias"][l], np.float32)
        put("dtb%d" % l, a)
        a = np.zeros((128, 1), np.float32)
        a[4:8, 0] = np.asarray(inp["gdn_a_log"][l], np.float32)
        put("alog%d" % l, a)
    put("nfin", cols(inp["norm_final_g"]))
    return t


def rot_tables(TP, past_len):
    pos = np.concatenate([np.arange(TP), past_len + np.arange(64)]).astype(np.float32)
    inv = (1.0 / (np.float32(10000.0) ** (np.arange(0, 128, 2, dtype=np.float32) / np.float32(128)))).astype(np.float32)
    ang = (pos[None, :] * inv[:, None]).astype(np.float32)
    cos, sin = np.cos(ang).astype(np.float32), np.sin(ang).astype(np.float32)
    rc = np.concatenate([cos, cos], axis=0)
    rs = np.concatenate([-sin, sin], axis=0)
    return np.ascontiguousarray(rc), np.ascontiguousarray(rs)


class Builder:
    def __init__(self, cfg):
        self.cfg = cfg
        self.TP, self.NS, self.NT = cfg["TP"], cfg["NS"], cfg["NT"]
        self.DEPTH = cfg.get("DEPTH", 2)
        self.NH = min(512, self.NT)
        self.NTOK = self.TP + self.NS * 64
        self.parts = cfg.get("parts", ("mix", "ffn"))
        self.nc = bass.Bass("TRN2", target_bir_lowering=False)
        self.dbg = {}

    def din(self, name, shape, dt=F32):
        return self.nc.dram_tensor(name, list(shape), dt, kind="ExternalInput").ap()

    def dout(self, name, shape):
        return self.nc.dram_tensor(name, list(shape), F32, kind="ExternalOutput").ap()

    def dscr(self, name, shape, dt):
        return self.nc.dram_tensor(name, list(shape), dt).ap()

    def sb(self, name, shape, dt=F32):
        return self.st.enter_context(self.nc.sbuf_tensor(name, list(shape), dt))

    def op(self, eng, fn, r=(), w=()):
        return self.P.add(eng, fn, r, w)

    def dma(self, eng, out, in_, r=(), w=(), **kw):
        return self.P.add(eng, lambda e: e.dma_start(out=out, in_=in_, **kw), r, w, is_dma=True)

    def mm(self, out, lhsT, rhs, start, stop, r, w):
        return self.op("pe", lambda e: e.matmul(out, lhsT=lhsT, rhs=rhs, start=start, stop=stop), r, w)

    def tr(self, out, in_, ident, r, w):
        return self.op("pe", lambda e: e.transpose(out, in_, ident), r, w)

    def act(self, out, in_, func, r, w, bias=None, scale=None):
        kw = {}
        if bias is not None:
            kw["bias"] = bias
        if scale is not None:
            kw["scale"] = scale
        return self.op("act", lambda e: e.activation(out=out, in_=in_, func=func, **kw), r, w)

    def tt(self, eng, out, in0, in1, alu, r, w):
        return self.op(eng, lambda e: e.tensor_tensor(out=out, in0=in0, in1=in1, op=alu), r, w)

    def ts(self, eng, out, in0, s1, s2, op0, op1, r, w):
        return self.op(eng, lambda e: e.tensor_scalar(out=out, in0=in0, scalar1=s1, scalar2=s2, op0=op0, op1=op1), r, w)

    def ts1(self, eng, out, in0, s1, op0, r, w):
        return self.op(eng, lambda e: e.tensor_single_scalar(out=out, in_=in0, scalar=s1, op=op0), r, w)

    def stt(self, out, in0, scalar, in1, op0, op1, r, w):
        return self.op("dve", lambda e: e.scalar_tensor_tensor(out=out, in0=in0, scalar=scalar, in1=in1, op0=op0, op1=op1), r, w)

    def cp(self, eng, out, in_, r, w):
        if eng == "act":
            return self.act(out, in_, AF.Copy, r, w)
        return self.op(eng, lambda e: e.tensor_copy(out=out, in_=in_), r, w)

    def memset(self, eng, ap, val, w):
        return self.op(eng, lambda e: e.memset(ap, val), (), w)

    def psk(self, bank, q0=0, q1=4):
        return [("ps", bank)]

    def rview(self, off, n, dt):
        assert off % 4 == 0
        if dt is F32:
            assert off + 4 * n <= self.RBYTES, (off, n)
            return self.R[:, off // 4: off // 4 + n]
        assert n % 2 == 0 and off + 2 * n <= self.RBYTES, (off, n)
        return self.R[:, off // 4: off // 4 + n // 2].bitcast(BF16)

    def declare(self):
        TP, NS, DEPTH, NTOK = self.TP, self.NS, self.DEPTH, self.NTOK
        self.clay, self.CW = const_layout(self.NH)
        self.play, self.NPC = pcol_layout()
        i = {}
        i["x_p"] = self.din("x_p", [TP, D])
        i["x_s"] = self.din("x_s", [NS * 64, D])
        i["st_gla"] = self.din("st_gla", [DEPTH, NS, 2, 128, 128])
        for n in ("st_hg", "st_gdn", "st_ret"):
            i[n] = self.din(n, [DEPTH, NS, 4, 128, 128])
        i["c_gdnT"] = self.din("c_gdnT", [DEPTH, NS, 128, 12, 3])
        i["c_ffnT"] = self.din("c_ffnT", [DEPTH, NS, 128, FKT, 2])
        i["pcols"] = self.din("pcols", [128, self.NPC])
        i["consts"] = self.din("consts", [128, self.CW])
        i["wgk"] = self.din("wgk", [DEPTH, 16, 256])
        i["rotc"] = self.din("rotc", [128, TP + 64])
        i["rots"] = self.din("rots", [128, TP + 64])
        i["w_in"] = self.din("w_in", [DEPTH, D, NIN])
        i["w_branch"] = self.din("w_branch", [DEPTH, 4 * 512, D])
        i["w_out"] = self.din("w_out", [DEPTH, D, D])
        i["w_ffn_in"] = self.din("w_ffn_in", [DEPTH, D, 2 * DFF])
        i["w_ffn_out"] = self.din("w_ffn_out", [DEPTH, DFF, D])
        self.i = i
        o = {}
        o["y_p"] = self.dout("y_p", [TP, D])
        o["y_s"] = self.dout("y_s", [NS * 64, D])
        o["o_gla_p"] = self.dout("o_gla_p", [DEPTH, 2, 128, 128])
        for n in ("hg", "gdn", "ret"):
            o["o_%s_p" % n] = self.dout("o_%s_p" % n, [DEPTH, 4, 128, 128])
        o["o_cg_p"] = self.dout("o_cg_p", [DEPTH, 128, 12, 3])
        o["o_cf_p"] = self.dout("o_cf_p", [DEPTH, 128, FKT, 2])
        o["o_gla_s"] = self.dout("o_gla_s", [DEPTH, NS, 2, 128, 128])
        for n in ("hg", "gdn", "ret"):
            o["o_%s_s" % n] = self.dout("o_%s_s" % n, [DEPTH, NS, 4, 128, 128])
        o["o_cg_s"] = self.dout("o_cg_s", [DEPTH, NS, 128, 12, 3])
        o["o_cf_s"] = self.dout("o_cf_s", [DEPTH, NS, 128, FKT, 2])
        self.o = o
        self.xT = self.dscr("xT_scr", [KT, 128, NTOK], F32)
        self.wb = {n: self.dscr("wb_" + n, list(i[n].shape), BF16) for n in ("w_in", "w_branch", "w_out", "w_ffn_in", "w_ffn_out")}

    def dbg_out(self, name, sb_ap, shape, r):
        t = self.dout("dbg_" + name, shape)
        self.dbg[name] = t
        self.dma("pool", t, sb_ap, r=r)

    def build(self):
        nc = self.nc
        self.declare()
        with ExitStack() as st:
            self.st = st
            self.P = Prog(nc, st)
            self.alloc()
            self.setup()
            self.stage0()
            tiles = self.make_tiles()
            for l in range(self.DEPTH):
                for ti, tile in enumerate(tiles):
                    if "mix" in self.parts:
                        self.mixer_stage(l, ti, tile)
                    if "ffn" in self.parts:
                        self.ffn_stage(l, ti, tile)
            for ti, tile in enumerate(tiles):
                self.final_stage(ti, tile)
            self.P.emit_all()
        return nc

    def make_tiles(self):
        tiles = []
        npt = self.TP // self.NT
        for t in range(npt):
            nch = self.NT // 64
            tiles.append(dict(tok0=t * self.NT, N=self.NT, prompt=True,
                              chunks=[dict(stream=0, first=(t == 0 and c == 0), last=(t == npt - 1 and c == nch - 1),
                                           pos=t * self.NT + c * 64) for c in range(nch)],
                              nseg=1, seglen=self.NT, first_tile=(t == 0), last_tile=(t == npt - 1)))
        tiles.append(dict(tok0=self.TP, N=self.NS * 64, prompt=False,
                          chunks=[dict(stream=1 + s, first=True, last=True, pos=self.TP) for s in range(self.NS)],
                          nseg=self.NS, seglen=64, first_tile=True, last_tile=True))
        return tiles

    def alloc(self):
        nc, st = self.nc, self.st
        NT = self.NT
        self.C32 = self.sb("C32", [128, self.CW])
        self.PC = self.sb("PC", [128, self.NPC])
        self.identb = self.sb("identb", [128, 128], BF16)
        self.onesb = self.sb("onesb", [128, 128], BF16)
        self.rswapb = self.sb("rswapb", [128, 128], BF16)
        self.drv = self.sb("drv", [128, 32])
        self.hT = self.sb("hT", [128, KT, NT], BF16)
        NP = min(512, NT)
        n_a2 = 2 * KT * NT + max(76 * 1024 * self.NH // 512, 56 * 1024)
        n_a3 = 88 * NT + 16384
        n_b = 2 * FKT * NT + 2 * 4 * (NT + 2 * max(1, self.NS)) + 4 * NT + 2 * 2 * NT + 64
        n_norm = 4 * KT * NT + 8 * NP + 4 * NT + 16384
        self.RBYTES = max(n_a2, n_a3, n_b, n_norm, 32768)
        self.R = self.sb("R", [128, self.RBYTES // 4])
        self.wslot = [self.sb("wslot%d" % k, [128, WSLOT], BF16) for k in range(3)]
        self.wrr = 0
        self.S32 = self.sb("S32", [128, 14, 128])
        self.Sbf = self.sb("Sbf", [128, 14, 128], BF16)
        self.ghist = self.sb("ghist", [128, 12, 3])
        self.fhist = self.sb("fhist", [128, FKT, 2])
        self.wgk = self.sb("wgk_sb", [128, 2, 256], BF16)
        self.PS = [st.enter_context(nc.psum_tensor("psb%d" % k, [128, 512], F32)) for k in range(8)]

    def cst(self, name):
        o, w = self.clay[name]
        return self.C32[:, o:o + w]

    def pc(self, name, j0=0, j1=None):
        o, w = self.play[name]
        if j1 is None:
            j1 = w
        return self.PC[:, o + j0:o + j1]

    def setup(self):
        i = self.i
        self.dma("sp", self.C32[:], i["consts"], w=["C32"])
        self.dma("sp", self.PC[:], i["pcols"], w=["PC"])
        self.cp("dve", self.identb[:], self.cst("ident"), r=["C32"], w=["identb"])
        self.cp("dve", self.onesb[:], self.cst("ones"), r=["C32"], w=["onesb"])
        self.cp("dve", self.rswapb[:], self.cst("rswap"), r=["C32"], w=["rswapb"])
        for l in range(self.DEPTH):
            for name in ("w_in", "w_branch", "w_out", "w_ffn_in", "w_ffn_out"):
                src, dst = i[name], self.wb[name]
                rows = src.shape[1]
                step = 128
                cranges = [(0, 7680), (7680, 7704), (7704, NIN)] if name == "w_in" else [(0, src.shape[2])]
                for r0 in range(0, rows, step):
                    r1 = min(rows, r0 + step)
                    for (ca, cb) in cranges:
                        self.dma("pool", dst[l, r0:r1, ca:cb], src[l, r0:r1, ca:cb], w=[("wb", name, l, r0 // 128)],
                                 max_dma_last_dim=4096)
        self.P.fence(("R", "W"))
        wgk32 = self.rview(0, 512, F32).rearrange("p (l c) -> p l c", l=2)
        self.memset("pool", wgk32, 0.0, w=[("R", "wgk32")])
        for l in range(self.DEPTH):
            self.dma("sp", wgk32[112:128, l, :], i["wgk"][l], w=[("R", "wgk32")])
        self.cp("pool", self.wgk[:], wgk32, r=[("R", "wgk32")], w=["wgk"])
        dv = self.drv
        self.memset("dve", dv[:], 0.0, w=["drv"])
        for l in range(self.DEPTH):
            self.ts1("dve", dv[:, 2 * l:2 * l + 2], self.pc("bgk%d" % l), -1.0, ALU.mult, r=["PC"], w=["drv"])
        self.memset("dve", dv[:, 12:16], 1.0, w=["drv"])
        if self.DEPTH > 1:
            self.tt("dve", dv[:, 24:28], self.pc("lbz1"), self.pc("lbz0"), ALU.subtract, r=["PC"], w=["drv"])
            self.act(dv[:, 8:12], dv[:, 24:28], AF.Sigmoid, r=["drv"], w=["drv"])
            self.ts("dve", dv[:, 16:20], dv[:, 8:12], -1.0, 1.0, ALU.mult, ALU.add, r=["drv"], w=["drv"])
        for l in range(self.DEPTH):
            self.act(dv[:, 28:29], self.pc("alog%d" % l), AF.Exp, r=["PC", "drv"], w=["drv"])
            self.stt(dv[:, 20 + l:21 + l], dv[:, 28:29], -1.0, self.cst("m47"), ALU.mult, ALU.mult, r=["drv", "C32"], w=["drv"])

    def stage0(self):
        P = self.P
        P.fence(("R", "W"))
        nblk = self.NTOK // 128
        xin = [self.rview(k * 8192, 2048, F32) for k in range(2)]
        xo = [self.rview(16384 + k * 8192, 2048, F32) for k in range(2)]
        for b in range(nblk):
            t0 = b * 128
            k = b % 2
            src = self.i["x_p"][t0:t0 + 128, :] if t0 < self.TP else self.i["x_s"][t0 - self.TP:t0 - self.TP + 128, :]
            self.dma("sp", xin[k], src, w=[("R", "xin", k)])
            for g in range(4):
                bank = 4 * (b % 2) + g
                for j in range(4):
                    kt = g * 4 + j
                    self.tr(self.PS[bank][:, j * 128:(j + 1) * 128], xin[k][:, kt * 128:(kt + 1) * 128], self.cst("ident"),
                            r=[("R", "xin", k), "C32"], w=self.psk(bank, j, j + 1))
                eng = "act" if g % 2 == 0 else "dve"
                self.cp(eng, xo[k][:, g * 512:(g + 1) * 512], self.PS[bank][:, :], r=self.psk(bank), w=[("R", "xo", k)])
            self.dma("sp", self.xT[:, :, t0:t0 + 128].rearrange("kt p t -> p kt t"),
                     xo[k].rearrange("p (kt t) -> p kt t", kt=KT), r=[("R", "xo", k)], w=[("xT", (t0 // self.NT if t0 < self.TP else -1), j) for j in range(KT)])

    def xkey(self, tile, j):
        return ("xT", (tile["tok0"] // self.NT if tile["prompt"] else -1), j)

    def load_w(self, name, l, r0, nkt, c0, ncols, eng="sp"):
        assert nkt * ncols <= WSLOT
        s = self.wrr
        self.wrr = (self.wrr + 1) % len(self.wslot)
        view = self.wslot[s][:, 0:nkt * ncols].rearrange("p (k c) -> p k c", k=nkt)
        src = self.wb[name][l, r0:r0 + nkt * 128, c0:c0 + ncols].rearrange("(k p) c -> p k c", p=128)
        rk = [("wb", name, l, r0 // 128 + k) for k in range(nkt)]
        self.dma(eng, view, src, r=rk, w=[("w", s)])
        return view, ("w", s)

    def dense(self, wname, l, r0, nkt, blocks, act_fn, n0, N, epilogue, banks=(0, 1, 2, 3), wcols=None):
        pieces = [(a, min(N, a + 512)) for a in range(0, N, 512)]
        npc = len(pieces)
        nset = len(banks) // npc
        assert nset >= 1
        if wcols is None:
            wcols = max(128, min(512, (WSLOT // nkt) // 128 * 128))
        bi = 0
        cnt = getattr(self, "_dense_cnt", 0)
        while bi < len(blocks):
            grp = [blocks[bi]]
            while len(grp) * 128 < wcols and bi + len(grp) < len(blocks) and blocks[bi + len(grp)][0] == grp[-1][0] + 128:
                grp.append(blocks[bi + len(grp)])
            wv, wk = self.load_w(wname, l, r0, nkt, grp[0][0], 128 * len(grp))
            for gi, (c0, tag) in enumerate(grp):
                bset = [banks[(cnt % nset) * npc + p] for p in range(npc)]
                cnt += 1
                for kt in range(nkt):
                    for p, (a, b) in enumerate(pieces):
                        ap, ak = act_fn(kt, n0 + a, n0 + b)
                        self.mm(self.PS[bset[p]][:, 0:b - a], wv[:, kt, gi * 128:(gi + 1) * 128], ap,
                                kt == 0, kt == nkt - 1, r=[wk, ak], w=self.psk(bset[p]))
                epilogue(tag, [self.PS[bset[p]][:, 0:b - a] for p, (a, b) in enumerate(pieces)],
                         [self.psk(bset[p]) for p in range(npc)], pieces)
            bi += len(grp)
        self._dense_cnt = cnt

    def norm_stage(self, tile, gname, final=False):
        P = self.P
        P.fence(("R", "W"))
        N, tok0 = tile["N"], tile["tok0"]
        xall = self.rview(0, KT * N, F32).rearrange("p (k n) -> p k n", k=KT)
        sq = [self.rview(4 * KT * N + k * 4 * min(512, N), min(512, N), BF16) for k in range(2)]
        pieces = [(a, min(N, a + 512)) for a in range(0, N, 512)]
        for kt in range(KT):
            self.dma("sp", xall[:, kt, :], self.xT[kt, :, tok0:tok0 + N], r=[self.xkey(tile, kt)], w=[("R", "xall", kt)])
        c = 0
        for kt in range(KT):
            for p, (a, b) in enumerate(pieces):
                s = sq[c % 2]
                c += 1
                self.act(s[:, 0:b - a], xall[:, kt, a:b], AF.Square, r=[("R", "xall", kt)], w=[("R", "sq", (c - 1) % 2)])
                self.mm(self.PS[p][:, 0:b - a], self.onesb[:], s[:, 0:b - a], kt == 0, kt == KT - 1,
                        r=["onesb", ("R", "sq", (c - 1) % 2)], w=self.psk(p))
        rs = self.rview(4 * KT * N + 8 * min(512, N), N, F32)
        self.norm_end = 4 * KT * N + 8 * min(512, N) + 4 * N
        for p, (a, b) in enumerate(pieces):
            self.act(rs[:, a:b], self.PS[p][:, 0:b - a], AF.Ln, r=self.psk(p), w=[("R", "rstd")], bias=EPS, scale=1.0 / D)
            self.act(rs[:, a:b], rs[:, a:b], AF.Exp, r=[], w=[("R", "rstd")], scale=-0.5)
        for kt in range(KT):
            g = self.pc(gname, kt, kt + 1)
            if final:
                self.stt(xall[:, kt, :], xall[:, kt, :], g, rs[:, :], ALU.mult, ALU.mult,
                         r=["PC", ("R", "rstd"), ("R", "xall", kt)], w=[("R", "xall", kt)])
            else:
                self.stt(self.hT[:, kt, 0:N], xall[:, kt, :], g, rs[:, :], ALU.mult, ALU.mult,
                         r=["PC", ("R", "rstd"), ("R", "xall", kt)], w=[("hT", kt)])
        return xall

    def hT_fn(self, kt, a, b):
        return self.hT[:, kt, a:b], ("hT", kt)

    def resid_epilogue(self, tile, bufs, bkeys):
        N, tok0 = tile["N"], tile["tok0"]
        state = {"c": 0}
        def ep(tag, pss, pks, pieces):
            j = tag
            k = state["c"] % 2
            state["c"] += 1
            buf, bk = bufs[k], bkeys[k]
            xk = self.xkey(tile, j)
            self.dma("sp", buf, self.xT[j, :, tok0:tok0 + N], r=[xk], w=[bk])
            for p, (a, b) in enumerate(pieces):
                self.tt("dve", buf[:, a:b], buf[:, a:b], pss[p], ALU.add, r=pks[p] + [bk], w=[bk])
            self.dma("pool", self.xT[j, :, tok0:tok0 + N], buf, r=[bk], w=[xk])
        return ep

    def final_stage(self, ti, tile):
        N, tok0 = tile["N"], tile["tok0"]
        xall = self.norm_stage(tile, "nfin", final=True)
        yb = [self.rview(self.norm_end + k * 8192, 2048, F32) for k in range(2)]
        for tb in range(N // 128):
            k = tb % 2
            for g in range(4):
                bank = 4 * (tb % 2) + g
                for j in range(4):
                    kt = g * 4 + j
                    self.tr(self.PS[bank][:, j * 128:(j + 1) * 128], xall[:, kt, tb * 128:(tb + 1) * 128], self.cst("ident"),
                            r=[("R", "xall", kt), "C32"], w=self.psk(bank, j, j + 1))
                eng = "act" if g % 2 == 0 else "dve"
                self.cp(eng, yb[k][:, g * 512:(g + 1) * 512], self.PS[bank][:, :], r=self.psk(bank), w=[("R", "yb", k)])
            t0 = tok0 + tb * 128
            dst = self.o["y_p"][t0:t0 + 128, :] if t0 < self.TP else self.o["y_s"][t0 - self.TP:t0 - self.TP + 128, :]
            self.dma("pool", dst, yb[k], r=[("R", "yb", k)])

    def ffn_stage(self, l, ti, tile):
        N, tok0, nseg, L = tile["N"], tile["tok0"], tile["nseg"], tile["seglen"]
        self.norm_stage(tile, "nffn%d" % l)
        self.P.fence(("R", "W"))
        actT = self.rview(0, FKT * N, BF16).rearrange("p (k n) -> p k n", k=FKT)
        base = 2 * FKT * N
        W2 = nseg * (L + 2)
        W2a = (W2 + 1) // 2 * 2
        ae = [self.rview(base + k * 4 * W2a, W2, F32).rearrange("p (s t) -> p s t", s=nseg) for k in range(2)]
        base += 2 * 4 * W2a
        yv = self.rview(base, N, F32)
        base += 4 * N
        sv = [self.rview(base + k * 2 * N, N, BF16) for k in range(2)]
        fcw = lambda j, jf: self.pc("fcw%d" % l, j * FKT + jf, j * FKT + jf + 1)
        if tile["prompt"] and tile["first_tile"]:
            self.memset("pool", self.fhist[:], 0.0, w=["fhist"])

        def ep(tag, pss, pks, pieces):
            kind, jf = tag
            k = jf % 2
            if kind == "a":
                a_k, ak = ae[k], ("R", "ae", k)
                if tile["prompt"]:
                    self.cp("pool", a_k[:, 0, 0:2], self.fhist[:, jf, :], r=["fhist"], w=[ak])
                    for p, (a, b) in enumerate(pieces):
                        self.cp("act", a_k[:, 0, 2 + a:2 + b], pss[p], r=pks[p], w=[ak])
                else:
                    assert len(pieces) == 1
                    self.dma("sp", a_k[:, :, 0:2], self.i["c_ffnT"][l, :, :, jf, :].rearrange("s p j -> p s j"), w=[ak])
                    self.cp("act", a_k[:, :, 2:2 + L], pss[0].rearrange("p (s t) -> p s t", s=nseg), r=pks[0], w=[ak])
                y3 = yv.rearrange("p (s t) -> p s t", s=nseg)
                yk = ("R", "y")
                self.ts1("dve", y3, a_k[:, :, 0:L], fcw(0, jf), ALU.mult, r=[ak, "PC"], w=[yk])
                self.stt(y3, a_k[:, :, 1:L + 1], fcw(1, jf), y3, ALU.mult, ALU.add, r=[ak, "PC"], w=[yk])
                self.stt(y3, a_k[:, :, 2:L + 2], fcw(2, jf), y3, ALU.mult, ALU.add, r=[ak, "PC"], w=[yk])
                if tile["prompt"]:
                    self.cp("pool", self.fhist[:, jf, :], a_k[:, 0, L:L + 2], r=[ak], w=["fhist"])
                else:
                    self.dma("pool", self.o["o_cf_s"][l, :, :, jf, :].rearrange("s p j -> p s j"), a_k[:, :, L:L + 2], r=[ak])
                self.act(sv[k], yv, AF.Silu, r=[yk, "PC"], w=[("R", "s", k)], bias=self.pc("fcb%d" % l, jf, jf + 1))
            else:
                for p, (a, b) in enumerate(pieces):
                    self.tt("dve", actT[:, jf, a:b], pss[p], sv[k][:, a:b], ALU.mult, r=pks[p] + [("R", "s", k)], w=[("R", "actT", jf)])

        blocks = []
        for jp in range(0, FKT, 2):
            js = [j for j in (jp, jp + 1) if j < FKT]
            blocks += [(j * 128, ("a", j)) for j in js]
            blocks += [(DFF + j * 128, ("u", j)) for j in js]
        self.dense("w_ffn_in", l, 0, KT, blocks, self.hT_fn, 0, N, ep, wcols=256)
        if tile["prompt"] and tile["last_tile"]:
            self.dma("pool", self.o["o_cf_p"][l], self.fhist[:], r=["fhist"])
        rb = [self.rview(2 * FKT * N + k * 4 * N, N, F32) for k in range(2)]
        rkeys = [("R", "xres", 0), ("R", "xres", 1)]
        alias_r = [("R", "ae", 0), ("R", "ae", 1), ("R", "y"), ("R", "s", 0), ("R", "s", 1)]
        self.op("pool", lambda e: e.nop(), r=alias_r, w=rkeys)
        self.op("pool", lambda e: e.nop(), r=[], w=alias_r + rkeys)
        actfn = lambda kt, a, b: (actT[:, kt, a:b], ("R", "actT", kt))
        self.dense("w_ffn_out", l, 0, FKT, [(j * 128, j) for j in range(KT)], actfn, 0, N,
                   self.resid_epilogue(tile, rb, rkeys), wcols=128)

    def walloc(self, n, dt):
        nb = n * (4 if dt is F32 else 2)
        nb = (nb + 3) // 4 * 4
        v = self.rview(self.wptr, n if dt is F32 else (n + 1) // 2 * 2, dt)
        self.wptr += nb
        return v

    def dense_T(self, l, c0, ncols, h0, nh, vtok, vkey):
        cw = 256
        cnt = 0
        for cc in range(0, ncols, cw):
            wv, wk = self.load_w("w_in", l, 0, KT, c0 + cc, cw)
            for tb in range(nh // 128):
                bank = cnt % 2
                cnt += 1
                for kt in range(KT):
                    self.mm(self.PS[bank][:, 0:cw], self.hT[:, kt, h0 + tb * 128:h0 + (tb + 1) * 128], wv[:, kt, :],
                            kt == 0, kt == KT - 1, r=[wk, ("hT", kt)], w=self.psk(bank))
                self.cp("act", vtok[:, tb, cc:cc + cw], self.PS[bank][:, 0:cw], r=self.psk(bank), w=[vkey])

    def mixer_stage(self, l, ti, tile):
        N, tok0 = tile["N"], tile["tok0"]
        self.norm_stage(tile, "nmix%d" % l)
        self.P.fence(("R", "W"))
        oT = self.rview(0, KT * N, BF16).rearrange("p (k n) -> p k n", k=KT)
        self.WOFF = 2 * KT * N
        sel = self.cfg.get("mixers", ("gla", "hg", "ret", "gdn"))
        for kt in range(KT):
            if ("gla", "hg", "gdn", "ret")[kt // 4] not in sel:
                self.memset("pool", oT[:, kt, :], 0.0, w=[("R", "oT", kt)])
        for h0 in range(0, N, self.NH):
            nh = min(self.NH, N - h0)
            chunks = tile["chunks"][h0 // 64:(h0 + nh) // 64]
            if "gla" in sel:
                self.a2_gla(l, tile, h0, nh, chunks, oT)
            if "hg" in sel:
                self.a2_hg(l, tile, h0, nh, chunks, oT)
            if "ret" in sel:
                self.a2_ret(l, tile, h0, nh, chunks, oT)
            if "gdn" in sel:
                self.a2_gdn(l, tile, h0, nh, chunks, oT)
        self.P.fence("W")
        self.a3_merge(l, tile, oT)

    def state_io(self, l, mname, chunk, sidx, uidx, when):
        src = {"gla": "st_gla", "hg": "st_hg", "gdn": "st_gdn", "ret": "st_ret"}[mname]
        if when == "load":
            if chunk["stream"] == 0:
                self.memset("pool", self.S32[:, sidx, :], 0.0, w=[("S", sidx)])
                self.memset("pool", self.Sbf[:, sidx, :], 0.0, w=[("Sb", sidx)])
            else:
                s = chunk["stream"] - 1
                self.dma("sp", self.S32[:, sidx, :], self.i[src][l, s, uidx], w=[("S", sidx)])
                self.cp("act", self.Sbf[:, sidx, :], self.S32[:, sidx, :], r=[("S", sidx)], w=[("Sb", sidx)])
        else:
            if chunk["stream"] == 0:
                self.dma("pool", self.o["o_%s_p" % mname][l, uidx], self.S32[:, sidx, :], r=[("S", sidx)])
            else:
                s = chunk["stream"] - 1
                self.dma("pool", self.o["o_%s_s" % mname][l, s, uidx], self.S32[:, sidx, :], r=[("S", sidx)])

    def gla_scan(self, l, mname, mi, nh, chunks, units, vtok, vkey, oraw):
        nblk = nh // 128
        BO, BT = 5, 4
        BAq = lambda q: 2 + q % 2
        BSu = lambda u: 6 + u % 2
        att_sb = [self.walloc(128, BF16) for _ in range(4)]
        kdpA = [self.walloc(128, BF16) for _ in units]
        kdpB = [self.walloc(128, BF16) for _ in units]
        for u in range(len(units)):
            self.memset("pool", kdpA[u], 0.0, w=[("W", "kdpA", u)])
            self.memset("pool", kdpB[u], 0.0, w=[("W", "kdpB", u)])
        psT = self.PS[BT][:, :].bitcast(BF16)
        mask2 = self.cst("mask2")
        for tb in range(nblk):
            bs = slice(tb * 128, (tb + 1) * 128)
            for ui, U in enumerate(units):
                for hd in U["heads"]:
                    q = hd["h"]
                    self.mm(self.PS[BAq(q)][:, q * 128:(q + 1) * 128], hd["ktz"][:, bs], hd["qtz"][:, bs], True, True,
                            r=[hd["kkey"], hd["qkey"]], w=self.psk(BAq(q)))
                    self.tt("dve", att_sb[q], self.PS[BAq(q)][:, q * 128:(q + 1) * 128], mask2, ALU.mult,
                            r=self.psk(BAq(q)) + ["C32"], w=[("W", "att", q)])
                self.tr(psT[:, ui * 128:(ui + 1) * 128], U["kd"][:, bs], self.identb[:], r=[U["kdkey"], "identb"],
                        w=self.psk(BT, ui // 2, ui // 2 + 1))
                self.cp("act", kdpA[ui][0:64, :], psT[0:64, ui * 128:(ui + 1) * 128], r=self.psk(BT, ui // 2, ui // 2 + 1), w=[("W", "kdpA", ui)])
                self.cp("act", kdpB[ui][64:128, :], psT[64:128, ui * 128:(ui + 1) * 128], r=self.psk(BT, ui // 2, ui // 2 + 1), w=[("W", "kdpB", ui)])
            for ci in range(2):
                cidx = 2 * tb + ci
                ch = chunks[cidx]
                cs_ = slice(cidx * 64, (cidx + 1) * 64)
                for ui, U in enumerate(units):
                    sidx = U["sidx"]
                    if ch["first"]:
                        self.state_io(l, mname, ch, sidx, U["uidx"], "load")
                    for hd in U["heads"]:
                        q = hd["h"]
                        osl = self.PS[BO][:, q * 128 + ci * 64:q * 128 + ci * 64 + 64]
                        self.mm(osl, vtok[:, tb, hd["vq"] * 128:(hd["vq"] + 1) * 128], att_sb[q][:, ci * 64:(ci + 1) * 64], True, False,
                                r=[vkey, ("W", "att", q)], w=self.psk(BO, q, q + 1))
                        self.mm(osl, self.Sbf[:, sidx, :], hd["qtz"][:, cs_], False, True,
                                r=[("Sb", sidx), hd["qkey"]], w=self.psk(BO, q, q + 1))
                    kp = (kdpA if ci == 0 else kdpB)[ui]
                    kpk = ("W", "kdpA" if ci == 0 else "kdpB", ui)
                    vw = U["vw"]
                    nq = vw // 128
                    sps = self.PS[BSu(ui)][:, 0:vw]
                    spk = self.psk(BSu(ui))
                    self.mm(sps, kp, vtok[:, tb, U["vc0"]:U["vc0"] + vw], True, True, r=[kpk, vkey], w=spk)
                    dsc = U["dS"][:, cidx:cidx + 1]
                    if nq == 1:
                        self.stt(self.S32[:, sidx, :], self.S32[:, sidx, :], dsc, sps, ALU.mult, ALU.add,
                                 r=spk + [U["dskey"]], w=[("S", sidx)])
                    else:
                        self.stt(self.S32[0:64, sidx, :], self.S32[0:64, sidx, :], U["dS"][0:64, cidx:cidx + 1], sps[0:64, 0:128],
                                 ALU.mult, ALU.add, r=spk + [U["dskey"]], w=[("S", sidx)])
                        self.stt(self.S32[64:128, sidx, :], self.S32[64:128, sidx, :], U["dS"][64:128, cidx:cidx + 1], sps[64:128, 128:256],
                                 ALU.mult, ALU.add, r=spk + [U["dskey"]], w=[("S", sidx)])
                    self.cp("act", self.Sbf[:, sidx, :], self.S32[:, sidx, :], r=[("S", sidx)], w=[("Sb", sidx)])
                    if ch["last"]:
                        self.state_io(l, mname, ch, sidx, U["uidx"], "store")
            for U in units:
                for hd in U["heads"]:
                    q = hd["h"]
                    self.cp("act", oraw[q][:, bs], self.PS[BO][:, q * 128:(q + 1) * 128], r=self.psk(BO, q, q + 1), w=[("W", "oraw", q)])

    def head_norm(self, l, mi, nh, h0, oraw, sg, oT, gname, bname=None):
        BN = 6
        sq = self.walloc(nh, BF16)
        rs = self.walloc(nh, F32)
        t = self.walloc(nh, F32)
        ones = self.cst("ones")
        for h in range(4):
            ok = ("W", "oraw", h)
            src = oraw[h]
            if bname is not None:
                self.mm(self.PS[BN][:, 0:nh], ones, oraw[h], True, True, r=["C32", ok], w=self.psk(BN))
                self.stt(oraw[h], self.PS[BN][:, 0:nh], -1.0 / 128, oraw[h], ALU.mult, ALU.add, r=self.psk(BN) + [ok], w=[ok])
            self.act(sq, src, AF.Square, r=[ok], w=[("W", "hn_sq")])
            self.mm(self.PS[BN][:, 0:nh], self.onesb[:], sq, True, True, r=["onesb", ("W", "hn_sq")], w=self.psk(BN))
            self.act(rs, self.PS[BN][:, 0:nh], AF.Ln, r=self.psk(BN), w=[("W", "hn_rs")], bias=EPS, scale=1.0 / 128)
            self.act(rs, rs, AF.Exp, r=[], w=[("W", "hn_rs")], scale=-0.5)
            self.stt(t, src, self.pc(gname), rs, ALU.mult, ALU.mult, r=[ok, "PC", ("W", "hn_rs")], w=[("W", "hn_t")])
            okey = ("R", "oT", mi * 4 + h)
            if bname is None:
                self.tt("dve", oT[:, mi * 4 + h, h0:h0 + nh], t, sg[h], ALU.mult, r=[("W", "hn_t"), ("W", "sg", h)], w=[okey])
            else:
                self.stt(oT[:, mi * 4 + h, h0:h0 + nh], t, self.pc(bname), sg[h], ALU.add, ALU.mult,
                         r=[("W", "hn_t"), ("W", "sg", h), "PC"], w=[okey])

    def decay_ops(self, cs, cskey, nh, sb, eb, enb, kdec, dS, keytag):
        nch = nh // 64
        cs3 = cs.rearrange("p (c t) -> p c t", t=64)
        kd3 = kdec.rearrange("p (c t) -> p c t", t=64)
        if eb is not None:
            self.act(eb, cs, AF.Exp, r=[cskey], w=[("W", keytag, "eb")], scale=sb)
        self.act(enb, cs, AF.Exp, r=[cskey], w=[("W", keytag, "enb")], scale=-sb)
        self.act(dS[:, 0:nch], cs3[:, :, 63], AF.Exp, r=[cskey], w=[("W", keytag, "dS")], scale=sb)
        self.tt("dve", kd3, cs3[:, :, 63:64].broadcast_to([128, nch, 64]), cs3, ALU.subtract, r=[cskey], w=[("W", keytag, "kdec")])
        self.act(kdec, kdec, AF.Exp, r=[], w=[("W", keytag, "kdec")], scale=sb)

    def a2_gla(self, l, tile, h0, nh, chunks, oT):
        P = self.P
        P.fence("W")
        self.wptr = self.WOFF
        nch, nblk = nh // 64, nh // 128
        vtok = self.walloc(nblk * 512, BF16).rearrange("p (b c) -> p b c", b=nblk)
        ktz = [self.walloc(nh, BF16) for _ in range(4)]
        qtz = [self.walloc(nh, BF16) for _ in range(4)]
        kd = [self.walloc(nh, BF16) for _ in range(2)]
        sg = [self.walloc(nh, BF16) for _ in range(4)]
        oraw = [self.walloc(nh, F32) for _ in range(4)]
        dS = [self.walloc(max(nch, 2), F32) for _ in range(2)]
        lrT = self.walloc(nh, BF16)
        cs = self.walloc(nh, F32)
        eb = self.walloc(nh, F32)
        enb = self.walloc(nh, F32)
        kdec = self.walloc(nh, F32)
        tmp = self.walloc(nh, F32)
        rmask = self.cst("rmask")
        for h in range(4):
            self.memset("pool", ktz[h], 0.0, w=[("W", "ktz", h)])
            self.memset("pool", qtz[h], 0.0, w=[("W", "qtz", h)])
        self.dense_T(l, C_GLA_V, 512, h0, nh, vtok, ("W", "vtok"))

        def ep_lr(tag, pss, pks, pieces):
            self.cp("act", lrT, pss[0], r=pks[0], w=[("W", "lrT")])
        self.dense("w_in", l, 0, KT, [(C_GLA_LR + 16 - 128, 0)], self.hT_fn, h0, nh, ep_lr, banks=(0, 1))

        def ep_k(tag, pss, pks, pieces):
            jb = tag
            ps = pss[0]
            for hh in range(2):
                rs_ = slice(hh * 64, hh * 64 + 64)
                self.tt("dve", ktz[2 * jb + hh][rs_, :], ps[rs_, :], enb[rs_, :], ALU.mult,
                        r=pks[0] + [("W", "g", "enb")], w=[("W", "ktz", 2 * jb + hh)])
            self.tt("dve", kd[jb], ps, kdec, ALU.mult, r=pks[0] + [("W", "g", "kdec")], w=[("W", "kd", jb)])

        def ep_q(tag, pss, pks, pieces):
            jb = tag
            ps = pss[0]
            for hh in range(2):
                rs_ = slice(hh * 64, hh * 64 + 64)
                self.stt(qtz[2 * jb + hh][rs_, :], ps[rs_, :], 0.125, eb[rs_, :], ALU.mult, ALU.mult,
                         r=pks[0] + [("W", "g", "eb")], w=[("W", "qtz", 2 * jb + hh)])

        for jb in range(2):
            BG = 7
            self.mm(self.PS[BG][:, 0:nh], self.wgk[:, l, jb * 128:(jb + 1) * 128], lrT, True, True,
                    r=["wgk", ("W", "lrT")], w=self.psk(BG))
            self.act(tmp, self.PS[BG][:, 0:nh], AF.Exp, r=self.psk(BG) + ["drv"], w=[("W", "g", "tmp")],
                     scale=-1.0, bias=self.drv[:, 2 * l + jb:2 * l + jb + 1])
            self.act(tmp, tmp, AF.Ln, r=[], w=[("W", "g", "tmp")], bias=1.0)
            self.op("dve", lambda e, cs=cs, tmp=tmp: e.tensor_tensor_scan(out=cs, data0=rmask[:, 0:nh], data1=tmp, initial=0.0,
                                                                         op0=ALU.mult, op1=ALU.add),
                    r=[("W", "g", "tmp"), "C32"], w=[("W", "g", "cs")])
            self.decay_ops(cs, ("W", "g", "cs"), nh, -1.0 / 16, eb, enb, kdec, dS[jb], "g")
            self.dense("w_in", l, 0, KT, [(C_GLA_K + jb * 128, jb)], self.hT_fn, h0, nh, ep_k, banks=(0, 1))
            self.dense("w_in", l, 0, KT, [(C_GLA_Q + jb * 128, jb)], self.hT_fn, h0, nh, ep_q, banks=(0, 1))

        def ep_g(tag, pss, pks, pieces):
            self.act(sg[tag], pss[0], AF.Silu, r=pks[0], w=[("W", "sg", tag)])
        self.dense("w_in", l, 0, KT, [(C_GLA_G + h * 128, h) for h in range(4)], self.hT_fn, h0, nh, ep_g, banks=(0, 1))
        units = []
        for jb in range(2):
            units.append(dict(kd=kd[jb], kdkey=("W", "kd", jb), dS=dS[jb], dskey=("W", "g", "dS"), sidx=jb, uidx=jb,
                              vc0=jb * 256, vw=256,
                              heads=[dict(h=2 * jb + hh, ktz=ktz[2 * jb + hh], qtz=qtz[2 * jb + hh], kkey=("W", "ktz", 2 * jb + hh),
                                          qkey=("W", "qtz", 2 * jb + hh), vq=2 * jb + hh) for hh in range(2)]))
        if self.cfg.get("dbg") and l == 0 and tile["tok0"] == 0 and h0 == 0:
            self.dbg_out("cs", cs, [128, nh], [("W", "g", "cs")])
            self.dbg_out("eb", eb, [128, nh], [("W", "g", "eb")])
            self.dbg_out("kdec", kdec, [128, nh], [("W", "g", "kdec")])
            t32 = self.walloc(nh, F32)
            for nm, ap, key in (("ktz1", ktz[1], ("W", "ktz", 1)), ("qtz0", qtz[0], ("W", "qtz", 0)), ("kd0", kd[0], ("W", "kd", 0)), ("lrT", lrT, ("W", "lrT"))):
                t32 = self.walloc(nh, F32)
                self.cp("dve", t32, ap, r=[key], w=[("W", "dbg", nm)])
                self.dbg_out(nm, t32, [128, nh], [("W", "dbg", nm)])
            t33 = self.walloc(nblk * 512, F32)
            self.cp("dve", t33, vtok.rearrange("p b c -> p (b c)"), r=[("W", "vtok")], w=[("W", "dbg", "vtok")])
            self.dbg_out("vtok", t33, [128, nblk * 512], [("W", "dbg", "vtok")])
            self.dbg_out("dS0", dS[0], [128, 2], [("W", "g", "dS")])
        if self.cfg.get("stop") == "prep":
            return
        self.gla_scan(l, "gla", 0, nh, chunks, units, vtok, ("W", "vtok"), oraw)
        if self.cfg.get("dbg") and l == 0 and tile["tok0"] == 0 and h0 == 0:
            self.dbg_out("oraw0", oraw[0], [128, nh], [("W", "oraw", 0)])
            self.dbg_out("S0", self.S32[:, 0, :], [128, 128], [("S", 0)])
        if self.cfg.get("stop") == "scan":
            return
        self.head_norm(l, 0, nh, h0, oraw, sg, oT, "gla_ng%d" % l)

    def a2_hg(self, l, tile, h0, nh, chunks, oT):
        self.P.fence("W")
        self.wptr = self.WOFF
        nch, nblk = nh // 64, nh // 128
        vtok = self.walloc(nblk * 512, BF16).rearrange("p (b c) -> p b c", b=nblk)
        ktz = [self.walloc(nh, BF16) for _ in range(4)]
        qtz = [self.walloc(nh, BF16) for _ in range(4)]
        kd = [self.walloc(nh, BF16) for _ in range(4)]
        sg = [self.walloc(nh, BF16) for _ in range(4)]
        oraw = [self.walloc(nh, F32) for _ in range(4)]
        dS = [self.walloc(max(nch, 2), F32) for _ in range(4)]
        cs = self.walloc(nh, F32)
        eb = [self.walloc(nh, F32) for _ in range(2)]
        enb = self.walloc(nh, F32)
        kdec = self.walloc(nh, F32)
        tA = self.walloc(nh, F32)
        tB = self.walloc(nh, F32)
        rmask = self.cst("rmask")
        self.dense_T(l, C_HG_I, 512, h0, nh, vtok, ("W", "vtok"))

        def ep(tag, pss, pks, pieces):
            kind, h = tag
            ps = pss[0]
            if kind == "f":
                lb = self.drv[:, 4 + 4 * l + h:5 + 4 * l + h]
                oml = self.drv[:, 12 + 4 * l + h:13 + 4 * l + h]
                self.act(tA, ps, AF.Sigmoid, r=pks[0], w=[("W", "h", "tA")])
                self.ts("dve", tA, tA, oml, lb, ALU.mult, ALU.add, r=["drv"], w=[("W", "h", "tA")])
                self.act(tB, tA, AF.Ln, r=[("W", "h", "tA")], w=[("W", "h", "tB")])
                self.op("dve", lambda e: e.tensor_tensor_scan(out=cs, data0=rmask[:, 0:nh], data1=tB, initial=0.0,
                                                              op0=ALU.mult, op1=ALU.add),
                        r=[("W", "h", "tB"), "C32"], w=[("W", "h", "cs")])
                self.ts("dve", tA, tA, -1.0, 1.0, ALU.mult, ALU.add, r=[], w=[("W", "h", "tA")])
                self.decay_ops_h(cs, nh, eb[h % 2], enb, kdec, dS[h], h)
                self.tt("pool", ktz[h], tA, enb, ALU.mult, r=[("W", "h", "tA"), ("W", "h", "enb")], w=[("W", "ktz", h)])
                self.tt("pool", kd[h], tA, kdec, ALU.mult, r=[("W", "h", "tA"), ("W", "h", "kdec")], w=[("W", "kd", h)])
            elif kind == "q":
                self.act(tB, ps, AF.Silu, r=pks[0], w=[("W", "h", "tB")])
                self.tt("dve", qtz[h], tB, eb[h % 2], ALU.mult, r=[("W", "h", "tB"), ("W", "h", "eb", h % 2)], w=[("W", "qtz", h)])
            else:
                self.act(sg[h], ps, AF.Silu, r=pks[0], w=[("W", "sg", h)])
        blocks = []
        for h in range(4):
            blocks += [(C_HG_F + h * 128, ("f", h)), (C_HG_Q + h * 128, ("q", h))]
        blocks += [(C_HG_G + h * 128, ("g", h)) for h in range(4)]
        self.dense("w_in", l, 0, KT, blocks, self.hT_fn, h0, nh, ep, banks=(0, 1))
        units = [dict(kd=kd[h], kdkey=("W", "kd", h), dS=dS[h], dskey=("W", "h", "dS", h), sidx=2 + h, uidx=h, vc0=h * 128, vw=128,
                      heads=[dict(h=h, ktz=ktz[h], qtz=qtz[h], kkey=("W", "ktz", h), qkey=("W", "qtz", h), vq=h)]) for h in range(4)]
        self.gla_scan(l, "hg", 1, nh, chunks, units, vtok, ("W", "vtok"), oraw)
        self.head_norm(l, 1, nh, h0, oraw, sg, oT, "hg_ng%d" % l)

    def decay_ops_h(self, cs, nh, eb, enb, kdec, dS, h):
        nch = nh // 64
        cs3 = cs.rearrange("p (c t) -> p c t", t=64)
        kd3 = kdec.rearrange("p (c t) -> p c t", t=64)
        ck = ("W", "h", "cs")
        self.act(eb, cs, AF.Exp, r=[ck], w=[("W", "h", "eb", h % 2)])
        self.act(enb, cs, AF.Exp, r=[ck], w=[("W", "h", "enb")], scale=-1.0)
        self.act(dS[:, 0:nch], cs3[:, :, 63], AF.Exp, r=[ck], w=[("W", "h", "dS", h)])
        self.tt("dve", kd3, cs3[:, :, 63:64].broadcast_to([128, nch, 64]), cs3, ALU.subtract, r=[ck], w=[("W", "h", "kdec")])
        self.act(kdec, kdec, AF.Exp, r=[], w=[("W", "h", "kdec")])

    def load_w_swapped(self, l, c0):
        s = self.wrr
        self.wrr = (self.wrr + 1) % len(self.wslot)
        view = self.wslot[s][:, 0:KT * 128].rearrange("p (k c) -> p k c", k=KT)
        rk = [("wb", "w_in", l, k) for k in range(KT)]
        for (d0, s0) in ((0, c0 + 64), (64, c0)):
            src = self.wb["w_in"][l, :, s0:s0 + 64].rearrange("(k p) c -> p k c", p=128)
            self.dma("sp", view[:, :, d0:d0 + 64], src, r=rk, w=[("w", s)])
        return view, ("w", s)

    def a2_ret(self, l, tile, h0, nh, chunks, oT):
        self.P.fence("W")
        self.wptr = self.WOFF
        nch, nblk = nh // 64, nh // 128
        vtok = self.walloc(nblk * 512, BF16).rearrange("p (b c) -> p b c", b=nblk)
        ktz = [self.walloc(nh, BF16) for _ in range(4)]
        qtz = [self.walloc(nh, BF16) for _ in range(4)]
        kd = [self.walloc(nh, BF16) for _ in range(4)]
        sg = [self.walloc(nh, BF16) for _ in range(4)]
        oraw = [self.walloc(nh, F32) for _ in range(4)]
        dS = [self.walloc(max(nch, 2), F32) for _ in range(4)]
        rc = self.walloc(nh, F32)
        rsn = self.walloc(nh, F32)
        t1 = [self.walloc(nh, F32) for _ in range(2)]
        t2 = self.walloc(nh, F32)
        xb = [self.walloc(nh, BF16) for _ in range(2)]
        if tile["prompt"]:
            p0 = tile["tok0"] + h0
            self.dma("sp", rc, self.i["rotc"][:, p0:p0 + nh], w=[("W", "rc")])
            self.dma("sp", rsn, self.i["rots"][:, p0:p0 + nh], w=[("W", "rsn")])
        else:
            for c in range(nch):
                self.dma("sp", rc[:, c * 64:(c + 1) * 64], self.i["rotc"][:, self.TP:self.TP + 64], w=[("W", "rc")])
                self.dma("sp", rsn[:, c * 64:(c + 1) * 64], self.i["rots"][:, self.TP:self.TP + 64], w=[("W", "rsn")])
        for h in range(4):
            self.memset("pool", dS[h], math.exp(64 * math.log(1.0 - 2.0 ** (-5.0 - h))), w=[("W", "r", "dS", h)])
        self.dense_T(l, C_RET_V, 512, h0, nh, vtok, ("W", "vtok"))

        def tab(name, h):
            o, w = self.clay[name]
            return self.C32[:, o + h * 64:o + (h + 1) * 64].unsqueeze(1).broadcast_to([128, nch, 64])
        v3 = lambda ap: ap.rearrange("p (c t) -> p c t", t=64)

        def ep(tag, pss, pks, pieces):
            kind, h = tag
            ps = pss[0]
            i_ = 0 if kind[0] == "q" else 1
            if kind in ("q", "k"):
                self.cp("act", xb[i_], ps, r=pks[0], w=[("W", "r", "xb", i_)])
                self.tt("dve", t1[i_], ps, rc, ALU.mult, r=pks[0] + [("W", "rc")], w=[("W", "r", "t1", i_)])
            elif kind in ("qs", "ks"):
                self.tt("dve", t2, ps, rsn, ALU.mult, r=pks[0] + [("W", "rsn")], w=[("W", "r", "t2")])
                self.tt("pool", t1[i_], t1[i_], t2, ALU.add, r=[("W", "r", "t2")], w=[("W", "r", "t1", i_)])
                if kind == "qs":
                    self.tt("pool", v3(qtz[h]), v3(t1[i_]), tab("reteb", h), ALU.mult, r=["C32"], w=[("W", "qtz", h)])
                else:
                    self.tt("pool", v3(ktz[h]), v3(t1[i_]), tab("retenb", h), ALU.mult, r=["C32"], w=[("W", "ktz", h)])
                    self.tt("pool", v3(kd[h]), v3(t1[i_]), tab("retkd", h), ALU.mult, r=["C32"], w=[("W", "kd", h)])
            else:
                self.act(sg[h], ps, AF.Silu, r=pks[0], w=[("W", "sg", h)])

        for h in range(4):
            for (cbase, kind) in ((C_RET_Q, "q"), (C_RET_K, "k")):
                self.dense("w_in", l, 0, KT, [(cbase + h * 128, (kind, h))], self.hT_fn, h0, nh, ep, banks=(0, 1))
                bank = 6 + (h % 2)
                self.mm(self.PS[bank][:, 0:nh], self.rswapb[:], xb[0 if kind == "q" else 1], True, True,
                        r=["rswapb", ("W", "r", "xb", 0 if kind == "q" else 1)], w=self.psk(bank))
                ep((kind + "s", h), [self.PS[bank][:, 0:nh]], [self.psk(bank)], [(0, nh)])
        self.dense("w_in", l, 0, KT, [(C_RET_G + h * 128, ("g", h)) for h in range(4)], self.hT_fn, h0, nh, ep, banks=(0, 1))
        units = [dict(kd=kd[h], kdkey=("W", "kd", h), dS=dS[h], dskey=("W", "r", "dS", h), sidx=10 + h, uidx=h, vc0=h * 128, vw=128,
                      heads=[dict(h=h, ktz=ktz[h], qtz=qtz[h], kkey=("W", "ktz", h), qkey=("W", "qtz", h), vq=h)]) for h in range(4)]
        self.gla_scan(l, "ret", 3, nh, chunks, units, vtok, ("W", "vtok"), oraw)
        self.head_norm(l, 3, nh, h0, oraw, sg, oT, "ret_ng%d" % l, "ret_nb%d" % l)

    def a2_gdn(self, l, tile, h0, nh, chunks, oT):
        self.P.fence("W")
        self.wptr = self.WOFF
        nch, nblk = nh // 64, nh // 128
        if tile["nseg"] == 1:
            nsg, L = 1, nh
        else:
            nsg, L = nch, 64
        v4 = lambda ap: ap.rearrange("p (h n) -> p h n", h=4)
        qhat = v4(self.walloc(4 * nh, BF16))
        khat = v4(self.walloc(4 * nh, BF16))
        vT = v4(self.walloc(4 * nh, BF16))
        sg = [self.walloc(nh, BF16) for _ in range(4)]
        oraw = [self.walloc(nh, F32) for _ in range(4)]
        XW = nsg * (L + 3)
        xe = [self.walloc(XW + (XW % 2), F32)[:, 0:XW].rearrange("p (s t) -> p s t", s=nsg) for _ in range(2)]
        yv = self.walloc(nh, F32)
        sqv = self.walloc(nh, F32)
        sqb = self.walloc(nh, BF16)
        BTt = self.walloc(nh, F32)
        CB = self.walloc(nh, F32)
        RS = self.walloc(nh, F32)
        EBT = self.walloc(nh, F32)
        KDT = self.walloc(nh, F32)
        dSg = self.walloc(4 * max(nch, 2), F32)
        rmask = self.cst("rmask")
        ones = self.cst("ones")
        for t_, nm in ((BTt, "BT"), (CB, "CB"), (RS, "RS"), (EBT, "EBT"), (KDT, "KDT")):
            self.memset("pool", t_, 0.0, w=[("W", nm)])
        gcw = lambda j, blk: self.pc("gcw%d" % l, j * 12 + blk, j * 12 + blk + 1)
        if tile["prompt"] and tile["first_tile"] and h0 == 0:
            self.memset("pool", self.ghist[:], 0.0, w=["ghist"])
        s_first = chunks[0]["stream"] - 1
        BN = 7

        def ep(tag, pss, pks, pieces):
            kind, idx = tag
            ps = pss[0]
            if kind == "x":
                blk = idx
                k = blk % 2
                x_k, xk = xe[k], ("W", "xe", k)
                if tile["prompt"]:
                    self.cp("pool", x_k[:, 0, 0:3], self.ghist[:, blk, :], r=["ghist"], w=[xk])
                else:
                    self.dma("sp", x_k[:, :, 0:3], self.i["c_gdnT"][l, s_first:s_first + nsg, :, blk, :].rearrange("s p j -> p s j"), w=[xk])
                self.cp("act", x_k[:, :, 3:3 + L], ps.rearrange("p (s t) -> p s t", s=nsg), r=pks[0], w=[xk])
                y3 = yv.rearrange("p (s t) -> p s t", s=nsg)
                yk = ("W", "yv")
                self.ts1("dve", y3, x_k[:, :, 0:L], gcw(0, blk), ALU.mult, r=[xk, "PC"], w=[yk])
                for j in range(1, 4):
                    self.stt(y3, x_k[:, :, j:j + L], gcw(j, blk), y3, ALU.mult, ALU.add, r=[xk, "PC"], w=[yk])
                if tile["prompt"]:
                    self.cp("pool", self.ghist[:, blk, :], x_k[:, 0, L:L + 3], r=[xk], w=["ghist"])
                else:
                    self.dma("pool", self.o["o_cg_s"][l, s_first:s_first + nsg, :, blk, :].rearrange("s p j -> p s j"), x_k[:, :, L:L + 3], r=[xk])
                h = blk % 4
                if blk < 8:
                    dst = qhat if blk < 4 else khat
                    self.act(dst[:, h, :], yv, AF.Silu, r=[yk], w=[("W", "qhat" if blk < 4 else "khat")])
                else:
                    self.act(vT[:, h, :], yv, AF.Silu, r=[yk], w=[("W", "vT")])
            elif kind == "z":
                self.act(sg[idx], ps, AF.Silu, r=pks[0], w=[("W", "sg", idx)])
            else:
                R32 = slice(0, 32)
                self.act(BTt[R32, :], ps[R32, :], AF.Sigmoid, r=pks[0], w=[("W", "BT")])
                self.act(RS[R32, :], ps[R32, :], AF.Exp, r=pks[0] + ["PC"], w=[("W", "RS")], bias=self.pc("dtb%d" % l)[R32, :])
                self.act(RS[R32, :], RS[R32, :], AF.Ln, r=[], w=[("W", "RS")], bias=1.0)
                self.ts1("dve", RS[R32, :], RS[R32, :], self.drv[R32, 20 + l:21 + l], ALU.mult, r=["drv"], w=[("W", "RS")])
                self.op("dve", lambda e: e.tensor_tensor_scan(out=CB[R32, :], data0=rmask[R32, 0:nh], data1=RS[R32, :], initial=0.0,
                                                              op0=ALU.mult, op1=ALU.add), r=["C32"], w=[("W", "CB")])
                c3 = lambda ap: ap.rearrange("p (c t) -> p c t", t=64)
                self.tt("dve", c3(RS)[R32], c3(CB)[R32, :, 63:64].broadcast_to([32, nch, 64]), c3(CB)[R32], ALU.subtract,
                        r=[("W", "CB")], w=[("W", "RS")])
                self.act(EBT[R32, :], CB[R32, :], AF.Exp, r=[("W", "CB")], w=[("W", "EBT")])
                self.act(KDT[R32, :], RS[R32, :], AF.Exp, r=[("W", "RS")], w=[("W", "KDT")])

        blocks = [(C_GDN_B, ("ba", 0))] + [(C_GDN_QKV + b * 128, ("x", b)) for b in (4, 5, 6, 7, 0, 1, 2, 3, 8, 9, 10, 11)]
        blocks += [(C_GDN_Z + h * 128, ("z", h)) for h in range(4)]
        self.dense("w_in", l, 0, KT, blocks, self.hT_fn, h0, nh, ep, banks=(0, 1))
        if tile["prompt"] and tile["last_tile"] and h0 + nh == tile["N"]:
            self.dma("pool", self.o["o_cg_p"][l], self.ghist[:], r=["ghist"])
        for blk in range(8):
            h = blk % 4
            dst, dk = (qhat, ("W", "qhat")) if blk < 4 else (khat, ("W", "khat"))
            self.act(sqb, dst[:, h, :], AF.Square, r=[dk], w=[("W", "sqb")])
            self.mm(self.PS[BN][:, 0:nh], self.onesb[:], sqb, True, True, r=["onesb", ("W", "sqb")], w=self.psk(BN))
            self.act(sqv, self.PS[BN][:, 0:nh], AF.Ln, r=self.psk(BN), w=[("W", "sqv")], bias=EPS)
            self.act(sqv, sqv, AF.Exp, r=[], w=[("W", "sqv")], scale=-0.5)
            if blk < 4:
                self.stt(dst[:, h, :], dst[:, h, :], 128 ** -0.5, sqv, ALU.mult, ALU.mult, r=[("W", "sqv")], w=[dk])
            else:
                self.tt("dve", dst[:, h, :], dst[:, h, :], sqv, ALU.mult, r=[("W", "sqv")], w=[dk])
        osel = self.clay["onesel"][0]
        onesel = lambda q: self.C32[:, osel + q * 128:osel + (q + 1) * 128]
        sel = self.cst("sel")
        c3 = lambda ap: ap.rearrange("p (c t) -> p c t", t=64)
        for h in range(4):
            self.mm(self.PS[BN][:, h * nch:(h + 1) * nch], onesel(h), c3(EBT)[:, :, 63], True, True, r=["C32", ("W", "EBT")], w=self.psk(BN, 0, 1))
        self.cp("act", dSg[:, 0:4 * nch], self.PS[BN][:, 0:4 * nch], r=self.psk(BN, 0, 1), w=[("W", "dSg")])
        f4 = lambda: self.walloc(512, F32).rearrange("p (h n) -> p h n", h=4)
        b4 = lambda: self.walloc(512, BF16).rearrange("p (h n) -> p h n", h=4)
        colsb = self.walloc(20, F32)
        nbe = self.walloc(4, F32)
        g1, gA, gBs, gBi, X32, bv, tA_ = f4(), f4(), f4(), f4(), f4(), f4(), f4()
        tB_ = tA_
        attd, kdpA, kdpB, qe, u_sb, Am, Bm, Xm, A2, B2, rhs_sb = b4(), b4(), b4(), b4(), b4(), b4(), b4(), b4(), b4(), b4(), b4()
        self.memset("pool", kdpA, 0.0, w=[("W", "kdpA")])
        self.memset("pool", kdpB, 0.0, w=[("W", "kdpB")])
        bc4 = lambda ap: ap.unsqueeze(1).broadcast_to([128, 4, 128])
        colb = lambda q: colsb[:, 4 * q:4 * q + 4].unsqueeze(2).broadcast_to([128, 4, 128])
        psbank = lambda b: self.PS[b][:, :].rearrange("p (h n) -> p h n", h=4)
        psT = self.PS[2][:, :].bitcast(BF16)
        psTk = psT[:, 0:512].rearrange("p (h n) -> p h n", h=4)
        psTv = psT[:, 512:1024].rearrange("p (h n) -> p h n", h=4)
        for tb in range(nblk):
            bs = slice(tb * 128, (tb + 1) * 128)
            for q, (X, xn, sl) in enumerate(((CB, "CB", 0), (BTt, "BT", 4), (EBT, "EBT", 0), (KDT, "KDT", 0))):
                self.mm(self.PS[2][:, 4 * q:4 * q + 4], X[:, bs], sel[:, sl:sl + 4], True, True, r=[("W", xn), "C32"], w=self.psk(2))
            self.cp("act", colsb[:, 0:16], self.PS[2][:, 0:16], r=self.psk(2), w=[("W", "colsb")])
            self.stt(nbe, colsb[:, 4:8], -1.0, colsb[:, 8:12], ALU.mult, ALU.mult, r=[("W", "colsb")], w=[("W", "nbe")])
            self.ts1("dve", colsb[:, 16:20], colsb[:, 4:8], -1.0, ALU.mult, r=[], w=[("W", "colsb")])
            for h in range(4):
                self.mm(self.PS[3][:, h * 128:(h + 1) * 128], onesel(h), CB[:, bs], True, True, r=["C32", ("W", "CB")], w=self.psk(3))
                self.mm(self.PS[4][:, h * 128:(h + 1) * 128], onesel(4 + h), BTt[:, bs], True, True, r=["C32", ("W", "BT")], w=self.psk(4))
                self.mm(self.PS[5][:, h * 128:(h + 1) * 128], onesel(h), EBT[:, bs], True, True, r=["C32", ("W", "EBT")], w=self.psk(5))
            self.tt("dve", g1, psbank(3), colb(0), ALU.subtract, r=self.psk(3) + [("W", "colsb")], w=[("W", "g1")])
            self.stt(gA, g1, 0.0, bc4(self.cst("nml")), ALU.max, ALU.add, r=[("W", "g1"), "C32"], w=[("W", "gA")])
            self.stt(gBs, g1, 0.0, bc4(self.cst("nmu")), ALU.min, ALU.subtract, r=[("W", "g1"), "C32"], w=[("W", "gBs")])
            self.stt(gBi, g1, 0.0, bc4(self.cst("pmui")), ALU.min, ALU.subtract, r=[("W", "g1"), "C32"], w=[("W", "gBi")])
            self.act(gA, gA, AF.Exp, r=[], w=[("W", "gA")], scale=-1.0)
            self.act(gBs, gBs, AF.Exp, r=[], w=[("W", "gBs")])
            self.act(gBi, gBi, AF.Exp, r=[], w=[("W", "gBi")])
            for h in range(4):
                self.mm(self.PS[6][:, h * 128:(h + 1) * 128], khat[:, h, bs], khat[:, h, bs], True, True, r=[("W", "khat")], w=self.psk(6))
                self.mm(self.PS[7][:, h * 128:(h + 1) * 128], khat[:, h, bs], qhat[:, h, bs], True, True, r=[("W", "khat"), ("W", "qhat")], w=self.psk(7))
            self.tt("dve", tA_, psbank(6), gA, ALU.mult, r=self.psk(6) + [("W", "gA")], w=[("W", "tA_")])
            self.tt("dve", Am, tA_, colb(4), ALU.mult, r=[("W", "colsb")], w=[("W", "Am")])
            self.tt("dve", tB_, psbank(6), gBs, ALU.mult, r=self.psk(6) + [("W", "gBs")], w=[("W", "tB_")])
            self.stt(Bm, tB_, -1.0, psbank(4), ALU.mult, ALU.mult, r=self.psk(4), w=[("W", "Bm")])
            self.tt("dve", X32, Bm, bc4(self.cst("ident")), ALU.add, r=["C32"], w=[("W", "X32")])
            self.cp("act", Xm, X32, r=[("W", "X32")], w=[("W", "Xm")])
            self.tt("dve", attd, psbank(7), gBi, ALU.mult, r=self.psk(7) + [("W", "gBi")], w=[("W", "attd")])
            self.tt("dve", qe, qhat[:, :, bs], psbank(5), ALU.mult, r=self.psk(5) + [("W", "qhat")], w=[("W", "qe")])
            for j in range(5):
                for h in range(4):
                    self.mm(self.PS[3][:, h * 128:(h + 1) * 128], Bm[:, h, :], Am[:, h, :], True, True, r=[("W", "Am"), ("W", "Bm")], w=self.psk(3))
                if j < 4:
                    for h in range(4):
                        self.mm(self.PS[4][:, h * 128:(h + 1) * 128], Am[:, h, :], Bm[:, h, :], True, True, r=[("W", "Am"), ("W", "Bm")], w=self.psk(4))
                self.cp("act", A2, psbank(3), r=self.psk(3), w=[("W", "A2")])
                if j < 4:
                    self.cp("dve", B2, psbank(4), r=self.psk(4), w=[("W", "B2")])
                for h in range(4):
                    self.mm(self.PS[5][:, h * 128:(h + 1) * 128], A2[:, h, :], Xm[:, h, :], True, True, r=[("W", "A2"), ("W", "Xm")], w=self.psk(5))
                self.tt("dve", X32, X32, psbank(5), ALU.add, r=self.psk(5), w=[("W", "X32")])
                self.cp("act", Xm, X32, r=[("W", "X32")], w=[("W", "Xm")])
                Am, A2 = A2, Am
                if j < 4:
                    Bm, B2 = B2, Bm
                self.op("pool", lambda e: e.nop(), r=[("W", "Am"), ("W", "A2"), ("W", "Bm"), ("W", "B2")],
                        w=[("W", "Am"), ("W", "A2"), ("W", "Bm"), ("W", "B2")])
            for h in range(4):
                self.tr(psTk[:, h, :], khat[:, h, bs], self.identb[:], r=[("W", "khat"), "identb"], w=self.psk(2, 0, 2))
                self.tr(psTv[:, h, :], vT[:, h, bs], self.identb[:], r=[("W", "vT"), "identb"], w=self.psk(2, 2, 4))
            kdc = colsb[:, 12:16].unsqueeze(2).broadcast_to([128, 4, 128])
            self.tt("dve", kdpA[0:64], psTk[0:64], kdc[0:64], ALU.mult, r=self.psk(2, 0, 2) + [("W", "colsb")], w=[("W", "kdpA")])
            self.tt("dve", kdpB[64:128], psTk[64:128], kdc[64:128], ALU.mult, r=self.psk(2, 0, 2) + [("W", "colsb")], w=[("W", "kdpB")])
            self.tt("dve", bv, psTv, colb(1), ALU.mult, r=self.psk(2, 2, 4) + [("W", "colsb")], w=[("W", "bv")])
            for ci in range(2):
                cidx = 2 * tb + ci
                ch = chunks[cidx]
                for h in range(4):
                    sidx = 6 + h
                    if ch["first"]:
                        self.state_io(l, "gdn", ch, sidx, h, "load")
                    hs = slice(h * 128, (h + 1) * 128)
                    self.mm(self.PS[6][:, hs], khat[:, h, bs], self.Sbf[:, sidx, :], True, True, r=[("W", "khat"), ("Sb", sidx)], w=self.psk(6, h, h + 1))
                    self.stt(rhs_sb[:, h, :], self.PS[6][:, hs], nbe[:, h:h + 1], bv[:, h, :], ALU.mult, ALU.add,
                             r=self.psk(6, h, h + 1) + [("W", "nbe"), ("W", "bv")], w=[("W", "rhs", h)])
                    self.mm(self.PS[7][:, hs], Xm[:, h, :], rhs_sb[:, h, :], True, True, r=[("W", "Xm"), ("W", "rhs", h)], w=self.psk(7, h, h + 1))
                    self.cp("act", u_sb[:, h, :], self.PS[7][:, hs], r=self.psk(7, h, h + 1), w=[("W", "u", h)])
                    osl = self.PS[3][:, h * 128 + ci * 64:h * 128 + ci * 64 + 64]
                    self.mm(osl, u_sb[:, h, :], attd[:, h, ci * 64:(ci + 1) * 64], True, False, r=[("W", "u", h), ("W", "attd")], w=self.psk(3, h, h + 1))
                    self.mm(osl, self.Sbf[:, sidx, :], qe[:, h, ci * 64:(ci + 1) * 64], False, True, r=[("Sb", sidx), ("W", "qe")], w=self.psk(3, h, h + 1))
                    kp = kdpA if ci == 0 else kdpB
                    self.mm(self.PS[4][:, hs], kp[:, h, :], u_sb[:, h, :], True, True, r=[("W", "kdpA" if ci == 0 else "kdpB"), ("W", "u", h)], w=self.psk(4, h, h + 1))
                    self.stt(self.S32[:, sidx, :], self.S32[:, sidx, :], dSg[:, h * nch + cidx:h * nch + cidx + 1], self.PS[4][:, hs],
                             ALU.mult, ALU.add, r=self.psk(4, h, h + 1) + [("W", "dSg")], w=[("S", sidx)])
                    self.cp("act", self.Sbf[:, sidx, :], self.S32[:, sidx, :], r=[("S", sidx)], w=[("Sb", sidx)])
                    if ch["last"]:
                        self.state_io(l, "gdn", ch, sidx, h, "store")
            for h in range(4):
                self.cp("act", oraw[h][:, bs], self.PS[3][:, h * 128:(h + 1) * 128], r=self.psk(3, h, h + 1), w=[("W", "oraw", h)])
        self.head_norm(l, 2, nh, h0, oraw, sg, oT, "gdn_ng%d" % l)

    def a3_merge(self, l, tile, oT):
        N, tok0 = tile["N"], tile["tok0"]
        mixT = self.rview(2 * KT * N, KT * N, BF16).rearrange("p (k n) -> p k n", k=KT)
        base = 4 * KT * N
        acc = [self.rview(base + j * 4 * N, N, F32) for j in range(4)]
        base += 16 * N
        sgt = [self.rview(base + k * 4 * N, N, F32) for k in range(2)]
        base += 8 * N
        aux = self.rview(base, 16 * 512, BF16).rearrange("p (k c) -> p k c", k=16)
        ofn = lambda kt, a, b: (oT[:, kt, a:b], ("R", "oT", kt))
        pieces = [(a, min(N, a + 512)) for a in range(0, N, 512)]
        cnt = 0
        for jg in range(4):
            for n in range(4):
                self.dma("sp", aux[:, n * 4:(n + 1) * 4, :],
                         self.wb["w_branch"][l, n * 512:(n + 1) * 512, jg * 512:(jg + 1) * 512].rearrange("(k p) c -> p k c", p=128),
                         r=[("wb", "w_branch", l, n * 4 + k) for k in range(4)], w=[("W", "aux")])
            for n in range(4):
                for half in range(2):
                    wv, wk = self.load_w("w_in", l, 0, KT, C_MERGE + n * D + jg * 512 + half * 256, 256)
                    for jj in range(2):
                        jl = half * 2 + jj
                        j = jg * 4 + jl
                        gb = [(2 * (cnt % 2) + p) for p in range(len(pieces))]
                        bb = [4 + (2 * (cnt % 2) + p) for p in range(len(pieces))]
                        cnt += 1
                        for kt in range(KT):
                            for p, (a, b) in enumerate(pieces):
                                self.mm(self.PS[gb[p]][:, 0:b - a], wv[:, kt, jj * 128:(jj + 1) * 128], self.hT[:, kt, a:b],
                                        kt == 0, kt == KT - 1, r=[wk, ("hT", kt)], w=self.psk(gb[p]))
                        for kk in range(4):
                            for p, (a, b) in enumerate(pieces):
                                self.mm(self.PS[bb[p]][:, 0:b - a], aux[:, n * 4 + kk, jl * 128:(jl + 1) * 128], oT[:, n * 4 + kk, a:b],
                                        kk == 0, kk == 3, r=[("W", "aux"), ("R", "oT", n * 4 + kk)], w=self.psk(bb[p]))
                        sk = cnt % 2
                        for p, (a, b) in enumerate(pieces):
                            self.act(sgt[sk][:, a:b], self.PS[gb[p]][:, 0:b - a], AF.Sigmoid, r=self.psk(gb[p]), w=[("W", "sgt", sk)])
                            if n == 0:
                                self.tt("dve", acc[jl][:, a:b], self.PS[bb[p]][:, 0:b - a], sgt[sk][:, a:b], ALU.mult,
                                        r=self.psk(bb[p]) + [("W", "sgt", sk)], w=[("W", "acc", jl)])
                            else:
                                self.tt("dve", sgt[sk][:, a:b], self.PS[bb[p]][:, 0:b - a], sgt[sk][:, a:b], ALU.mult,
                                        r=self.psk(bb[p]), w=[("W", "sgt", sk)])
                                if n < 3:
                                    self.tt("pool", acc[jl][:, a:b], acc[jl][:, a:b], sgt[sk][:, a:b], ALU.add,
                                            r=[("W", "sgt", sk)], w=[("W", "acc", jl)])
                                else:
                                    self.tt("pool", mixT[:, j, a:b], acc[jl][:, a:b], sgt[sk][:, a:b], ALU.add,
                                            r=[("W", "sgt", sk), ("W", "acc", jl)], w=[("W", "mixT", j)])
        rb = sgt
        rkeys = [("W", "sgt", 0), ("W", "sgt", 1)]
        mfn = lambda kt, a, b: (mixT[:, kt, a:b], ("W", "mixT", kt))
        self.dense("w_out", l, 0, KT, [(j * 128, j) for j in range(KT)], mfn, 0, N, self.resid_epilogue(tile, rb, rkeys),
                   banks=(0, 1, 2, 3))


PAST_LEN = 4096


def core_inputs(inp, c, cfg, shared):
    TP, NS = cfg["TP"], cfg["NS"]
    f = lambda a: np.ascontiguousarray(np.asarray(a, np.float32))
    m = dict(shared)
    xp = np.asarray(inp["x_prompt"])
    if c < xp.shape[0]:
        m["x_p"] = f(xp[c, :TP])
    else:
        m["x_p"] = np.zeros((TP, D), np.float32)
    s0, s1 = c * NS, (c + 1) * NS
    m["x_s"] = f(np.asarray(inp["x_sample"])[s0:s1].reshape(NS * 64, D))
    m["st_gla"] = f(np.asarray(inp["state_gla"])[:, s0:s1].reshape(-1, NS, 2, 128, 128))
    m["st_hg"] = f(np.asarray(inp["state_hgrn"])[:, s0:s1])
    m["st_gdn"] = f(np.asarray(inp["state_gdn"])[:, s0:s1])
    m["st_ret"] = f(np.asarray(inp["state_ret"])[:, s0:s1])
    cg = np.asarray(inp["cache_gdn_conv"])[:, s0:s1]
    m["c_gdnT"] = f(cg.reshape(cg.shape[0], NS, 3, 12, 128).transpose(0, 1, 4, 3, 2))
    cf = np.asarray(inp["cache_ffn_conv"])[:, s0:s1]
    m["c_ffnT"] = f(cf.reshape(cf.shape[0], NS, 2, FKT, 128).transpose(0, 1, 4, 3, 2))
    return m


def shared_inputs(inp, cfg):
    TP = cfg["TP"]
    NH = min(512, cfg["NT"])
    f = lambda a: np.ascontiguousarray(np.asarray(a, np.float32))
    sh = {}
    sh["pcols"] = make_pcols(inp)
    sh["consts"] = make_consts(NH)
    sh["wgk"] = f(inp["gla_w_gk"])
    rc, rs = rot_tables(TP, PAST_LEN)
    sh["rotc"], sh["rots"] = rc, rs
    sh["w_in"] = f(inp["w_in"])
    wbr = np.asarray(inp["w_branch"], np.float32)
    sh["w_branch"] = np.ascontiguousarray(wbr.reshape(wbr.shape[0], 4 * 512, D))
    sh["w_out"] = f(inp["w_out"])
    sh["w_ffn_in"] = f(inp["w_ffn_in"])
    sh["w_ffn_out"] = f(inp["w_ffn_out"])
    return sh


_NC_CACHE = {}


def get_nc(cfg):
    key = repr(sorted(cfg.items()))
    if key not in _NC_CACHE:
        b = Builder(cfg)
        nc = b.build()
        _NC_CACHE[key] = (nc, b)
    return _NC_CACHE[key]


def kernel(**inputs):
    cfg = dict(TP=8192, NS=4, NT=1024, DEPTH=2)
    nc, b = get_nc(cfg)
    sh = shared_inputs(inputs, cfg)
    in_maps = [core_inputs(inputs, c, cfg, sh) for c in range(8)]
    res = run_bass_kernel_spmd(nc, in_maps, core_ids=list(range(8)))
    R = res.results
    L = cfg["DEPTH"]
    y_p = np.stack([R[c]["y_p"] for c in range(2)])
    y_s = np.concatenate([R[c]["y_s"].reshape(4, 64, D) for c in range(8)])
    outs = [y_p, y_s]
    outs.append(np.stack([R[c]["o_gla_p"].reshape(L, 4, 64, 128) for c in range(2)], axis=1))
    for n in ("hg", "gdn", "ret"):
        outs.append(np.stack([R[c]["o_%s_p" % n] for c in range(2)], axis=1))
    outs.append(np.stack([R[c]["o_cg_p"].transpose(0, 3, 2, 1).reshape(L, 3, 1536) for c in range(2)], axis=1))
    outs.append(np.stack([R[c]["o_cf_p"].transpose(0, 3, 2, 1).reshape(L, 2, DFF) for c in range(2)], axis=1))
    outs.append(np.concatenate([R[c]["o_gla_s"].reshape(L, 4, 4, 64, 128) for c in range(8)], axis=1))
    for n in ("hg", "gdn", "ret"):
        outs.append(np.concatenate([R[c]["o_%s_s" % n] for c in range(8)], axis=1))
    outs.append(np.concatenate([R[c]["o_cg_s"].transpose(0, 1, 4, 3, 2).reshape(L, 4, 3, 1536) for c in range(8)], axis=1))
    outs.append(np.concatenate([R[c]["o_cf_s"].transpose(0, 1, 4, 3, 2).reshape(L, 4, 2, DFF) for c in range(8)], axis=1))
    return tuple(np.ascontiguousarray(o, dtype=np.float32) for o in outs)
```

```python
import math
from contextlib import ExitStack

import numpy as np
import concourse.bass as bass
import concourse.mybir as mybir
from concourse.bass_utils import run_bass_kernel_spmd

F32 = mybir.dt.float32
BF16 = mybir.dt.bfloat16
AF = mybir.ActivationFunctionType
ALU = mybir.AluOpType

ENGS = ("pe", "act", "dve", "pool", "sp")

D = 2048
KT = 16
NIN = 15896
DFF = 5504
FKT = 43
H = 4
EPS = 1e-6
C_GLA_Q, C_GLA_K, C_GLA_V, C_GLA_G, C_GLA_LR = 0, 256, 512, 1024, 1536
C_HG_Q, C_HG_F, C_HG_I, C_HG_G = 1552, 2064, 2576, 3088
C_GDN_QKV, C_GDN_Z, C_GDN_B = 3600, 5136, 5648
C_RET_Q, C_RET_K, C_RET_V, C_RET_G = 5656, 6168, 6680, 7192
C_MERGE = 7704
WSLOT = 5632


class Op:
    __slots__ = ("eng", "emit", "waits", "sig", "is_dma")


class Prog:
    def __init__(self, nc, stack, n_dma_sems=(("sp", 24), ("pool", 24), ("act", 8))):
        self.nc = nc
        self.ops = {e: [] for e in ENGS}
        self.res = {}
        self.esem = {e: stack.enter_context(nc.semaphore("s_" + e)) for e in ENGS}
        self.ecount = {e: 0 for e in ENGS}
        self.dsem, self.dstate, self.drr = {}, {}, {}
        for e, n in n_dma_sems:
            self.dsem[e] = [stack.enter_context(nc.semaphore("d_%s%d" % (e, i))) for i in range(n)]
            self.dstate[e] = [[0, None] for _ in range(n)]
            self.drr[e] = 0
        self.waited = {e: {} for e in ENGS}
        self.frontier = {}
        self.groups = {}
        self.nops = 0

    def _need(self, op, src):
        if src is None or src is op:
            return
        if src.eng == op.eng and not src.is_dma:
            return
        sem, val = src.sig
        w = self.waited[op.eng]
        k = id(sem)
        if w.get(k, -1) >= val:
            return
        w[k] = val
        op.waits.append((sem, val))

    def _get(self, k):
        r = self.res.get(k)
        if r is None:
            r = [None, []]
            if isinstance(k, tuple) and k and k[0] in self.frontier:
                r[1] = list(self.frontier[k[0]])
            self.res[k] = r
            if isinstance(k, tuple) and k:
                self.groups.setdefault(k[0], set()).add(k)
        return r

    def fence(self, groups):
        if isinstance(groups, str):
            groups = (groups,)
        fr = {}
        for group in groups:
            for k in self.groups.get(group, ()):
                r = self.res.pop(k)
                if r[0] is not None:
                    fr[id(r[0])] = r[0]
                for o in r[1]:
                    fr[id(o)] = o
            for o in self.frontier.get(group, ()):
                fr[id(o)] = o
        best = {}
        for o in fr.values():
            sem, val = o.sig
            b = best.get(id(sem))
            if b is None or b.sig[1] < val:
                best[id(sem)] = o
        for group in groups:
            self.frontier[group] = list(best.values())
            self.groups[group] = set()

    def add(self, eng, emit, reads=(), writes=(), is_dma=False):
        op = Op()
        op.eng, op.emit, op.waits, op.is_dma = eng, emit, [], is_dma
        self.nops += 1
        if is_dma:
            pool = self.dstate[eng]
            i = self.drr[eng]
            self.drr[eng] = (i + 1) % len(pool)
            st = pool[i]
            if st[1] is not None:
                sem, val = st[1].sig
                w = self.waited[eng]
                if w.get(id(sem), -1) < val:
                    w[id(sem)] = val
                    op.waits.append((sem, val))
            st[0] += 16
            st[1] = op
            op.sig = (self.dsem[eng][i], st[0])
        else:
            self.ecount[eng] += 1
            op.sig = (self.esem[eng], self.ecount[eng])
        pr = [k for k in reads if isinstance(k, tuple) and k[0] == "ps"]
        if pr:
            reads = [k for k in reads if not (isinstance(k, tuple) and k[0] == "ps")]
            writes = list(writes) + pr
        for k in reads:
            r = self._get(k)
            self._need(op, r[0])
            r[1].append(op)
        for k in writes:
            r = self._get(k)
            self._need(op, r[0])
            for rd in r[1]:
                self._need(op, rd)
            r[0] = op
            r[1] = []
        self.ops[eng].append(op)
        return op

    def emit_all(self, final_eng="sp"):
        nc = self.nc
        finals = []
        for e in ENGS:
            if self.ecount[e] > 0 and e != final_eng:
                finals.append((self.esem[e], self.ecount[e]))
        for e in self.dsem:
            for i, st in enumerate(self.dstate[e]):
                if st[0] > 0:
                    finals.append((self.dsem[e][i], st[0]))
        with nc.Block() as block:
            def mk(e):
                def body(eng):
                    for o in self.ops[e]:
                        for (sem, val) in o.waits:
                            eng.wait_ge(sem, val)
                        ins = o.emit(eng)
                        ins.then_inc(o.sig[0], 16 if o.is_dma else 1)
                    if e == final_eng:
                        for (sem, val) in finals:
                            eng.wait_ge(sem, val)
                return body
            block.tensor(mk("pe"))
            block.scalar(mk("act"))
            block.vector(mk("dve"))
            block.gpsimd(mk("pool"))
            block.sync(mk("sp"))


def const_layout(NH):
    lay, off = {}, 0
    for name, w in (("ident", 128), ("ones", 128), ("mask2", 128), ("nml", 128), ("nmu", 128),
                    ("pmui", 128), ("sel", 8), ("onesel", 1024), ("rmask", NH), ("m47", 1),
                    ("reteb", 256), ("retenb", 256), ("retkd", 256), ("rswap", 128)):
        lay[name] = (off, w)
        off += w
    return lay, off


def make_consts(NH):
    lay, cw = const_layout(NH)
    c = np.zeros((128, cw), np.float32)
    p = np.arange(128)[:, None]
    f = np.arange(128)[None, :]
    same = (p // 64) == (f // 64)
    c[:, lay["ident"][0]:lay["ident"][0] + 128] = np.eye(128)
    c[:, lay["ones"][0]:lay["ones"][0] + 128] = 1.0
    c[:, lay["mask2"][0]:lay["mask2"][0] + 128] = (same & (p <= f)).astype(np.float32)
    BIG = 1.0e4
    c[:, lay["nml"][0]:lay["nml"][0] + 128] = BIG * (~(same & (f < p))).astype(np.float32)
    c[:, lay["nmu"][0]:lay["nmu"][0] + 128] = BIG * (~(same & (p < f))).astype(np.float32)
    c[:, lay["pmui"][0]:lay["pmui"][0] + 128] = BIG * (~(same & (p <= f))).astype(np.float32)
    so = lay["sel"][0]
    for h in range(4):
        c[4 + h, so + h] = 1.0
        c[h, so + 4 + h] = 1.0
    oo = lay["onesel"][0]
    for h in range(4):
        c[4 + h, oo + h * 128:oo + (h + 1) * 128] = 1.0
        c[h, oo + (4 + h) * 128:oo + (5 + h) * 128] = 1.0
    ro = lay["rmask"][0]
    rm = np.ones(NH, np.float32)
    rm[::64] = 0.0
    c[:, ro:ro + NH] = rm[None, :]
    c[4:8, lay["m47"][0]] = 1.0
    for m_ in range(128):
        c[(m_ + 64) % 128, lay["rswap"][0] + m_] = 1.0
    t = np.arange(64, dtype=np.float64)
    for h in range(4):
        lg = math.log(1.0 - 2.0 ** (-5.0 - h))
        c[:, lay["reteb"][0] + h * 64:lay["reteb"][0] + (h + 1) * 64] = np.exp(lg * (t + 1))[None, :]
        c[:, lay["retenb"][0] + h * 64:lay["retenb"][0] + (h + 1) * 64] = (np.exp(-lg * (t + 1)) * 128 ** -0.5)[None, :]
        c[:, lay["retkd"][0] + h * 64:lay["retkd"][0] + (h + 1) * 64] = (np.exp(lg * (63 - t)) * 128 ** -0.5)[None, :]
    return c


def pcol_layout():
    lay, off = {}, 0
    def add(name, w):
        nonlocal off
        lay[name] = (off, w)
        off += w
    for l in range(2):
        add("nmix%d" % l, 16)
        add("nffn%d" % l, 16)
        add("bgk%d" % l, 2)
        add("gla_ng%d" % l, 1)
        add("hg_ng%d" % l, 1)
        add("gdn_ng%d" % l, 1)
        add("ret_ng%d" % l, 1)
        add("ret_nb%d" % l, 1)
        add("lbz%d" % l, 4)
        add("gcw%d" % l, 48)
        add("fcw%d" % l, 129)
        add("fcb%d" % l, 43)
        add("dtb%d" % l, 1)
        add("alog%d" % l, 1)
    add("nfin", 16)
    return lay, off


def cols(v):
    return np.ascontiguousarray(np.asarray(v, np.float32).reshape(-1, 128).T)


def make_pcols(inp):
    lay, n = pcol_layout()
    t = np.zeros((128, n), np.float32)
    def put(name, arr):
        o, w = lay[name]
        assert arr.shape == (128, w), (name, arr.shape, w)
        t[:, o:o + w] = arr
    for l in range(2):
        put("nmix%d" % l, cols(inp["norm_mix_g"][l]))
        put("nffn%d" % l, cols(inp["norm_ffn_g"][l]))
        put("bgk%d" % l, cols(inp["gla_b_gk"][l]))
        put("gla_ng%d" % l, cols(inp["gla_norm_g"][l]))
        put("hg_ng%d" % l, cols(inp["hgrn_norm_g"][l]))
        put("gdn_ng%d" % l, cols(inp["gdn_norm_g"][l]))
        put("ret_ng%d" % l, cols(inp["ret_norm_g"][l]))
        put("ret_nb%d" % l, cols(inp["ret_norm_b"][l]))
        put("lbz%d" % l, cols(inp["hgrn_lb_logits"][l]))
        put("gcw%d" % l, np.concatenate([cols(inp["gdn_conv_w"][l][j]) for j in range(4)], axis=1))
        put("fcw%d" % l, np.concatenate([cols(inp["ffn_conv_w"][l][j]) for j in range(3)], axis=1))
        put("fcb%d" % l, cols(inp["ffn_conv_b"][l]))
        a = np.zeros((128, 1), np.float32)
        a[4:8, 0] = np.asarray(inp["gdn_dt_bias"][l], np.float32)
        put("dtb%d" % l, a)
        a = np.zeros((128, 1), np.float32)
        a[4:8, 0] = np.asarray(inp["gdn_a_log"][l], np.float32)
        put("alog%d" % l, a)
    put("nfin", cols(inp["norm_final_g"]))
    return t


def rot_tables(TP, past_len):
    pos = np.concatenate([np.arange(TP), past_len + np.arange(64)]).astype(np.float32)
    inv = (1.0 / (np.float32(10000.0) ** (np.arange(0, 128, 2, dtype=np.float32) / np.float32(128)))).astype(np.float32)
    ang = (pos[None, :] * inv[:, None]).astype(np.float32)
    cos, sin = np.cos(ang).astype(np.float32), np.sin(ang).astype(np.float32)
    rc = np.concatenate([cos, cos], axis=0)
    rs = np.concatenate([-sin, sin], axis=0)
    return np.ascontiguousarray(rc), np.ascontiguousarray(rs)


class Builder:
    def __init__(self, cfg):
        self.cfg = cfg
        self.TP, self.NS, self.NT = cfg["TP"], cfg["NS"], cfg["NT"]
        self.DEPTH = cfg.get("DEPTH", 2)
        self.NH = min(512, self.NT)
        self.NTOK = self.TP + self.NS * 64
        self.parts = cfg.get("parts", ("mix", "ffn"))
        self.nc = bass.Bass("TRN2", target_bir_lowering=False)
        self.dbg = {}

    def din(self, name, shape, dt=F32):
        return self.nc.dram_tensor(name, list(shape), dt, kind="ExternalInput").ap()

    def dout(self, name, shape):
        return self.nc.dram_tensor(name, list(shape), F32, kind="ExternalOutput").ap()

    def dscr(self, name, shape, dt):
        return self.nc.dram_tensor(name, list(shape), dt).ap()

    def sb(self, name, shape, dt=F32):
        return self.st.enter_context(self.nc.sbuf_tensor(name, list(shape), dt))

    def op(self, eng, fn, r=(), w=()):
        return self.P.add(eng, fn, r, w)

    def dma(self, eng, out, in_, r=(), w=(), **kw):
        return self.P.add(eng, lambda e: e.dma_start(out=out, in_=in_, **kw), r, w, is_dma=True)

    def mm(self, out, lhsT, rhs, start, stop, r, w):
        return self.op("pe", lambda e: e.matmul(out, lhsT=lhsT, rhs=rhs, start=start, stop=stop), r, w)

    def tr(self, out, in_, ident, r, w):
        return self.op("pe", lambda e: e.transpose(out, in_, ident), r, w)

    def act(self, out, in_, func, r, w, bias=None, scale=None):
        kw = {}
        if bias is not None:
            kw["bias"] = bias
        if scale is not None:
            kw["scale"] = scale
        return self.op("act", lambda e: e.activation(out=out, in_=in_, func=func, **kw), r, w)

    def tt(self, eng, out, in0, in1, alu, r, w):
        return self.op(eng, lambda e: e.tensor_tensor(out=out, in0=in0, in1=in1, op=alu), r, w)

    def ts(self, eng, out, in0, s1, s2, op0, op1, r, w):
        return self.op(eng, lambda e: e.tensor_scalar(out=out, in0=in0, scalar1=s1, scalar2=s2, op0=op0, op1=op1), r, w)

    def ts1(self, eng, out, in0, s1, op0, r, w):
        return self.op(eng, lambda e: e.tensor_single_scalar(out=out, in_=in0, scalar=s1, op=op0), r, w)

    def stt(self, out, in0, scalar, in1, op0, op1, r, w):
        return self.op("dve", lambda e: e.scalar_tensor_tensor(out=out, in0=in0, scalar=scalar, in1=in1, op0=op0, op1=op1), r, w)

    def cp(self, eng, out, in_, r, w):
        if eng == "act":
            return self.act(out, in_, AF.Copy, r, w)
        return self.op(eng, lambda e: e.tensor_copy(out=out, in_=in_), r, w)

    def memset(self, eng, ap, val, w):
        return self.op(eng, lambda e: e.memset(ap, val), (), w)

    def psk(self, bank, q0=0, q1=4):
        return [("ps", bank)]

    def rview(self, off, n, dt):
        assert off % 4 == 0
        if dt is F32:
            assert off + 4 * n <= self.RBYTES, (off, n)
            return self.R[:, off // 4: off // 4 + n]
        assert n % 2 == 0 and off + 2 * n <= self.RBYTES, (off, n)
        return self.R[:, off // 4: off // 4 + n // 2].bitcast(BF16)

    def declare(self):
        TP, NS, DEPTH, NTOK = self.TP, self.NS, self.DEPTH, self.NTOK
        self.clay, self.CW = const_layout(self.NH)
        self.play, self.NPC = pcol_layout()
        i = {}
        i["x_p"] = self.din("x_p", [TP, D])
        i["x_s"] = self.din("x_s", [NS * 64, D])
        i["st_gla"] = self.din("st_gla", [DEPTH, NS, 2, 128, 128])
        for n in ("st_hg", "st_gdn", "st_ret"):
            i[n] = self.din(n, [DEPTH, NS, 4, 128, 128])
        i["c_gdnT"] = self.din("c_gdnT", [DEPTH, NS, 128, 12, 3])
        i["c_ffnT"] = self.din("c_ffnT", [DEPTH, NS, 128, FKT, 2])
        i["pcols"] = self.din("pcols", [128, self.NPC])
        i["consts"] = self.din("consts", [128, self.CW])
        i["wgk"] = self.din("wgk", [DEPTH, 16, 256])
        i["rotc"] = self.din("rotc", [128, TP + 64])
        i["rots"] = self.din("rots", [128, TP + 64])
        i["w_in"] = self.din("w_in", [DEPTH, D, NIN])
        i["w_branch"] = self.din("w_branch", [DEPTH, 4 * 512, D])
        i["w_out"] = self.din("w_out", [DEPTH, D, D])
        i["w_ffn_in"] = self.din("w_ffn_in", [DEPTH, D, 2 * DFF])
        i["w_ffn_out"] = self.din("w_ffn_out", [DEPTH, DFF, D])
        self.i = i
        o = {}
        o["y_p"] = self.dout("y_p", [TP, D])
        o["y_s"] = self.dout("y_s", [NS * 64, D])
        o["o_gla_p"] = self.dout("o_gla_p", [DEPTH, 2, 128, 128])
        for n in ("hg", "gdn", "ret"):
            o["o_%s_p" % n] = self.dout("o_%s_p" % n, [DEPTH, 4, 128, 128])
        o["o_cg_p"] = self.dout("o_cg_p", [DEPTH, 128, 12, 3])
        o["o_cf_p"] = self.dout("o_cf_p", [DEPTH, 128, FKT, 2])
        o["o_gla_s"] = self.dout("o_gla_s", [DEPTH, NS, 2, 128, 128])
        for n in ("hg", "gdn", "ret"):
            o["o_%s_s" % n] = self.dout("o_%s_s" % n, [DEPTH, NS, 4, 128, 128])
        o["o_cg_s"] = self.dout("o_cg_s", [DEPTH, NS, 128, 12, 3])
        o["o_cf_s"] = self.dout("o_cf_s", [DEPTH, NS, 128, FKT, 2])
        self.o = o
        self.xT = self.dscr("xT_scr", [KT, 128, NTOK], F32)
        self.wb = {n: self.dscr("wb_" + n, list(i[n].shape), BF16) for n in ("w_in", "w_branch", "w_out", "w_ffn_in", "w_ffn_out")}

    def dbg_out(self, name, sb_ap, shape, r):
        t = self.dout("dbg_" + name, shape)
        self.dbg[name] = t
        self.dma("pool", t, sb_ap, r=r)

    def build(self):
        nc = self.nc
        self.declare()
        with ExitStack() as st:
            self.st = st
            self.P = Prog(nc, st)
            self.alloc()
            self.setup()
            self.stage0()
            tiles = self.make_tiles()
            for l in range(self.DEPTH):
                for ti, tile in enumerate(tiles):
                    if "mix" in self.parts:
                        self.mixer_stage(l, ti, tile)
                    if "ffn" in self.parts:
                        self.ffn_stage(l, ti, tile)
            for ti, tile in enumerate(tiles):
                self.final_stage(ti, tile)
            self.P.emit_all()
        return nc

    def make_tiles(self):
        tiles = []
        npt = self.TP // self.NT
        for t in range(npt):
            nch = self.NT // 64
            tiles.append(dict(tok0=t * self.NT, N=self.NT, prompt=True,
                              chunks=[dict(stream=0, first=(t == 0 and c == 0), last=(t == npt - 1 and c == nch - 1),
                                           pos=t * self.NT + c * 64) for c in range(nch)],
                              nseg=1, seglen=self.NT, first_tile=(t == 0), last_tile=(t == npt - 1)))
        tiles.append(dict(tok0=self.TP, N=self.NS * 64, prompt=False,
                          chunks=[dict(stream=1 + s, first=True, last=True, pos=self.TP) for s in range(self.NS)],
                          nseg=self.NS, seglen=64, first_tile=True, last_tile=True))
        return tiles

    def alloc(self):
        nc, st = self.nc, self.st
        NT = self.NT
        self.C32 = self.sb("C32", [128, self.CW])
        self.PC = self.sb("PC", [128, self.NPC])
        self.identb = self.sb("identb", [128, 128], BF16)
        self.onesb = self.sb("onesb", [128, 128], BF16)
        self.rswapb = self.sb("rswapb", [128, 128], BF16)
        self.drv = self.sb("drv", [128, 32])
        self.hT = self.sb("hT", [128, KT, NT], BF16)
        NP = min(512, NT)
        n_a2 = 2 * KT * NT + max(76 * 1024 * self.NH // 512, 56 * 1024)
        n_a3 = 88 * NT + 16384
        n_b = 2 * FKT * NT + 2 * 4 * (NT + 2 * max(1, self.NS)) + 4 * NT + 2 * 2 * NT + 64
        n_norm = 4 * KT * NT + 8 * NP + 4 * NT + 16384
        self.RBYTES = max(n_a2, n_a3, n_b, n_norm, 32768)
        self.R = self.sb("R", [128, self.RBYTES // 4])
        self.wslot = [self.sb("wslot%d" % k, [128, WSLOT], BF16) for k in range(3)]
        self.wrr = 0
        self.S32 = self.sb("S32", [128, 14, 128])
        self.Sbf = self.sb("Sbf", [128, 14, 128], BF16)
        self.ghist = self.sb("ghist", [128, 12, 3])
        self.fhist = self.sb("fhist", [128, FKT, 2])
        self.wgk = self.sb("wgk_sb", [128, 2, 256], BF16)
        self.PS = [st.enter_context(nc.psum_tensor("psb%d" % k, [128, 512], F32)) for k in range(8)]

    def cst(self, name):
        o, w = self.clay[name]
        return self.C32[:, o:o + w]

    def pc(self, name, j0=0, j1=None):
        o, w = self.play[name]
        if j1 is None:
            j1 = w
        return self.PC[:, o + j0:o + j1]

    def setup(self):
        i = self.i
        self.dma("sp", self.C32[:], i["consts"], w=["C32"])
        self.dma("sp", self.PC[:], i["pcols"], w=["PC"])
        self.cp("dve", self.identb[:], self.cst("ident"), r=["C32"], w=["identb"])
        self.cp("dve", self.onesb[:], self.cst("ones"), r=["C32"], w=["onesb"])
        self.cp("dve", self.rswapb[:], self.cst("rswap"), r=["C32"], w=["rswapb"])
        for l in range(self.DEPTH):
            for name in ("w_in", "w_branch", "w_out", "w_ffn_in", "w_ffn_out"):
                src, dst = i[name], self.wb[name]
                rows = src.shape[1]
                step = 128
                cranges = [(0, 7680), (7680, 7704), (7704, NIN)] if name == "w_in" else [(0, src.shape[2])]
                for r0 in range(0, rows, step):
                    r1 = min(rows, r0 + step)
                    for (ca, cb) in cranges:
                        self.dma("pool", dst[l, r0:r1, ca:cb], src[l, r0:r1, ca:cb], w=[("wb", name, l, r0 // 128)],
                                 max_dma_last_dim=4096)
        self.P.fence(("R", "W"))
        wgk32 = self.rview(0, 512, F32).rearrange("p (l c) -> p l c", l=2)
        self.memset("pool", wgk32, 0.0, w=[("R", "wgk32")])
        for l in range(self.DEPTH):
            self.dma("sp", wgk32[112:128, l, :], i["wgk"][l], w=[("R", "wgk32")])
        self.cp("pool", self.wgk[:], wgk32, r=[("R", "wgk32")], w=["wgk"])
        dv = self.drv
        self.memset("dve", dv[:], 0.0, w=["drv"])
        for l in range(self.DEPTH):
            self.ts1("dve", dv[:, 2 * l:2 * l + 2], self.pc("bgk%d" % l), -1.0, ALU.mult, r=["PC"], w=["drv"])
        self.memset("dve", dv[:, 12:16], 1.0, w=["drv"])
        if self.DEPTH > 1:
            self.tt("dve", dv[:, 24:28], self.pc("lbz1"), self.pc("lbz0"), ALU.subtract, r=["PC"], w=["drv"])
            self.act(dv[:, 8:12], dv[:, 24:28], AF.Sigmoid, r=["drv"], w=["drv"])
            self.ts("dve", dv[:, 16:20], dv[:, 8:12], -1.0, 1.0, ALU.mult, ALU.add, r=["drv"], w=["drv"])
        for l in range(self.DEPTH):
            self.act(dv[:, 28:29], self.pc("alog%d" % l), AF.Exp, r=["PC", "drv"], w=["drv"])
            self.stt(dv[:, 20 + l:21 + l], dv[:, 28:29], -1.0, self.cst("m47"), ALU.mult, ALU.mult, r=["drv", "C32"], w=["drv"])

    def stage0(self):
        P = self.P
        P.fence(("R", "W"))
        nblk = self.NTOK // 128
        xin = [self.rview(k * 8192, 2048, F32) for k in range(2)]
        xo = [self.rview(16384 + k * 8192, 2048, F32) for k in range(2)]
        for b in range(nblk):
            t0 = b * 128
            k = b % 2
            src = self.i["x_p"][t0:t0 + 128, :] if t0 < self.TP else self.i["x_s"][t0 - self.TP:t0 - self.TP + 128, :]
            self.dma("sp", xin[k], src, w=[("R", "xin", k)])
            for g in range(4):
                bank = 4 * (b % 2) + g
                for j in range(4):
                    kt = g * 4 + j
                    self.tr(self.PS[bank][:, j * 128:(j + 1) * 128], xin[k][:, kt * 128:(kt + 1) * 128], self.cst("ident"),
                            r=[("R", "xin", k), "C32"], w=self.psk(bank, j, j + 1))
                eng = "act" if g % 2 == 0 else "dve"
                self.cp(eng, xo[k][:, g * 512:(g + 1) * 512], self.PS[bank][:, :], r=self.psk(bank), w=[("R", "xo", k)])
            self.dma("sp", self.xT[:, :, t0:t0 + 128].rearrange("kt p t -> p kt t"),
                     xo[k].rearrange("p (kt t) -> p kt t", kt=KT), r=[("R", "xo", k)], w=[("xT", (t0 // self.NT if t0 < self.TP else -1), j) for j in range(KT)])

    def xkey(self, tile, j):
        return ("xT", (tile["tok0"] // self.NT if tile["prompt"] else -1), j)

    def load_w(self, name, l, r0, nkt, c0, ncols, eng="sp"):
        assert nkt * ncols <= WSLOT
        s = self.wrr
        self.wrr = (self.wrr + 1) % len(self.wslot)
        view = self.wslot[s][:, 0:nkt * ncols].rearrange("p (k c) -> p k c", k=nkt)
        src = self.wb[name][l, r0:r0 + nkt * 128, c0:c0 + ncols].rearrange("(k p) c -> p k c", p=128)
        rk = [("wb", name, l, r0 // 128 + k) for k in range(nkt)]
        self.dma(eng, view, src, r=rk, w=[("w", s)])
        return view, ("w", s)

    def dense(self, wname, l, r0, nkt, blocks, act_fn, n0, N, epilogue, banks=(0, 1, 2, 3), wcols=None):
        pieces = [(a, min(N, a + 512)) for a in range(0, N, 512)]
        npc = len(pieces)
        nset = len(banks) // npc
        assert nset >= 1
        if wcols is None:
            wcols = max(128, min(512, (WSLOT // nkt) // 128 * 128))
        bi = 0
        cnt = getattr(self, "_dense_cnt", 0)
        while bi < len(blocks):
            grp = [blocks[bi]]
            while len(grp) * 128 < wcols and bi + len(grp) < len(blocks) and blocks[bi + len(grp)][0] == grp[-1][0] + 128:
                grp.append(blocks[bi + len(grp)])
            wv, wk = self.load_w(wname, l, r0, nkt, grp[0][0], 128 * len(grp))
            for gi, (c0, tag) in enumerate(grp):
                bset = [banks[(cnt % nset) * npc + p] for p in range(npc)]
                cnt += 1
                for kt in range(nkt):
                    for p, (a, b) in enumerate(pieces):
                        ap, ak = act_fn(kt, n0 + a, n0 + b)
                        self.mm(self.PS[bset[p]][:, 0:b - a], wv[:, kt, gi * 128:(gi + 1) * 128], ap,
                                kt == 0, kt == nkt - 1, r=[wk, ak], w=self.psk(bset[p]))
                epilogue(tag, [self.PS[bset[p]][:, 0:b - a] for p, (a, b) in enumerate(pieces)],
                         [self.psk(bset[p]) for p in range(npc)], pieces)
            bi += len(grp)
        self._dense_cnt = cnt

    def norm_stage(self, tile, gname, final=False):
        P = self.P
        P.fence(("R", "W"))
        N, tok0 = tile["N"], tile["tok0"]
        xall = self.rview(0, KT * N, F32).rearrange("p (k n) -> p k n", k=KT)
        sq = [self.rview(4 * KT * N + k * 4 * min(512, N), min(512, N), BF16) for k in range(2)]
        pieces = [(a, min(N, a + 512)) for a in range(0, N, 512)]
        for kt in range(KT):
            self.dma("sp", xall[:, kt, :], self.xT[kt, :, tok0:tok0 + N], r=[self.xkey(tile, kt)], w=[("R", "xall", kt)])
        c = 0
        for kt in range(KT):
            for p, (a, b) in enumerate(pieces):
                s = sq[c % 2]
                c += 1
                self.act(s[:, 0:b - a], xall[:, kt, a:b], AF.Square, r=[("R", "xall", kt)], w=[("R", "sq", (c - 1) % 2)])
                self.mm(self.PS[p][:, 0:b - a], self.onesb[:], s[:, 0:b - a], kt == 0, kt == KT - 1,
                        r=["onesb", ("R", "sq", (c - 1) % 2)], w=self.psk(p))
        rs = self.rview(4 * KT * N + 8 * min(512, N), N, F32)
        self.norm_end = 4 * KT * N + 8 * min(512, N) + 4 * N
        for p, (a, b) in enumerate(pieces):
            self.act(rs[:, a:b], self.PS[p][:, 0:b - a], AF.Ln, r=self.psk(p), w=[("R", "rstd")], bias=EPS, scale=1.0 / D)
            self.act(rs[:, a:b], rs[:, a:b], AF.Exp, r=[], w=[("R", "rstd")], scale=-0.5)
        for kt in range(KT):
            g = self.pc(gname, kt, kt + 1)
            if final:
                self.stt(xall[:, kt, :], xall[:, kt, :], g, rs[:, :], ALU.mult, ALU.mult,
                         r=["PC", ("R", "rstd"), ("R", "xall", kt)], w=[("R", "xall", kt)])
            else:
                self.stt(self.hT[:, kt, 0:N], xall[:, kt, :], g, rs[:, :], ALU.mult, ALU.mult,
                         r=["PC", ("R", "rstd"), ("R", "xall", kt)], w=[("hT", kt)])
        return xall

    def hT_fn(self, kt, a, b):
        return self.hT[:, kt, a:b], ("hT", kt)

    def resid_epilogue(self, tile, bufs, bkeys):
        N, tok0 = tile["N"], tile["tok0"]
        state = {"c": 0}
        def ep(tag, pss, pks, pieces):
            j = tag
            k = state["c"] % 2
            state["c"] += 1
            buf, bk = bufs[k], bkeys[k]
            xk = self.xkey(tile, j)
            self.dma("sp", buf, self.xT[j, :, tok0:tok0 + N], r=[xk], w=[bk])
            for p, (a, b) in enumerate(pieces):
                self.tt("dve", buf[:, a:b], buf[:, a:b], pss[p], ALU.add, r=pks[p] + [bk], w=[bk])
            self.dma("pool", self.xT[j, :, tok0:tok0 + N], buf, r=[bk], w=[xk])
        return ep

    def final_stage(self, ti, tile):
        N, tok0 = tile["N"], tile["tok0"]
        xall = self.norm_stage(tile, "nfin", final=True)
        yb = [self.rview(self.norm_end + k * 8192, 2048, F32) for k in range(2)]
        for tb in range(N // 128):
            k = tb % 2
            for g in range(4):
                bank = 4 * (tb % 2) + g
                for j in range(4):
                    kt = g * 4 + j
                    self.tr(self.PS[bank][:, j * 128:(j + 1) * 128], xall[:, kt, tb * 128:(tb + 1) * 128], self.cst("ident"),
                            r=[("R", "xall", kt), "C32"], w=self.psk(bank, j, j + 1))
                eng = "act" if g % 2 == 0 else "dve"
                self.cp(eng, yb[k][:, g * 512:(g + 1) * 512], self.PS[bank][:, :], r=self.psk(bank), w=[("R", "yb", k)])
            t0 = tok0 + tb * 128
            dst = self.o["y_p"][t0:t0 + 128, :] if t0 < self.TP else self.o["y_s"][t0 - self.TP:t0 - self.TP + 128, :]
            self.dma("pool", dst, yb[k], r=[("R", "yb", k)])

    def ffn_stage(self, l, ti, tile):
        N, tok0, nseg, L = tile["N"], tile["tok0"], tile["nseg"], tile["seglen"]
        self.norm_stage(tile, "nffn%d" % l)
        self.P.fence(("R", "W"))
        actT = self.rview(0, FKT * N, BF16).rearrange("p (k n) -> p k n", k=FKT)
        base = 2 * FKT * N
        W2 = nseg * (L + 2)
        W2a = (W2 + 1) // 2 * 2
        ae = [self.rview(base + k * 4 * W2a, W2, F32).rearrange("p (s t) -> p s t", s=nseg) for k in range(2)]
        base += 2 * 4 * W2a
        yv = self.rview(base, N, F32)
        base += 4 * N
        sv = [self.rview(base + k * 2 * N, N, BF16) for k in range(2)]
        fcw = lambda j, jf: self.pc("fcw%d" % l, j * FKT + jf, j * FKT + jf + 1)
        if tile["prompt"] and tile["first_tile"]:
            self.memset("pool", self.fhist[:], 0.0, w=["fhist"])

        def ep(tag, pss, pks, pieces):
            kind, jf = tag
            k = jf % 2
            if kind == "a":
                a_k, ak = ae[k], ("R", "ae", k)
                if tile["prompt"]:
                    self.cp("pool", a_k[:, 0, 0:2], self.fhist[:, jf, :], r=["fhist"], w=[ak])
                    for p, (a, b) in enumerate(pieces):
                        self.cp("act", a_k[:, 0, 2 + a:2 + b], pss[p], r=pks[p], w=[ak])
                else:
                    assert len(pieces) == 1
                    self.dma("sp", a_k[:, :, 0:2], self.i["c_ffnT"][l, :, :, jf, :].rearrange("s p j -> p s j"), w=[ak])
                    self.cp("act", a_k[:, :, 2:2 + L], pss[0].rearrange("p (s t) -> p s t", s=nseg), r=pks[0], w=[ak])
                y3 = yv.rearrange("p (s t) -> p s t", s=nseg)
                yk = ("R", "y")
                self.ts1("dve", y3, a_k[:, :, 0:L], fcw(0, jf), ALU.mult, r=[ak, "PC"], w=[yk])
                self.stt(y3, a_k[:, :, 1:L + 1], fcw(1, jf), y3, ALU.mult, ALU.add, r=[ak, "PC"], w=[yk])
                self.stt(y3, a_k[:, :, 2:L + 2], fcw(2, jf), y3, ALU.mult, ALU.add, r=[ak, "PC"], w=[yk])
                if tile["prompt"]:
                    self.cp("pool", self.fhist[:, jf, :], a_k[:, 0, L:L + 2], r=[ak], w=["fhist"])
                else:
                    self.dma("pool", self.o["o_cf_s"][l, :, :, jf, :].rearrange("s p j -> p s j"), a_k[:, :, L:L + 2], r=[ak])
                self.act(sv[k], yv, AF.Silu, r=[yk, "PC"], w=[("R", "s", k)], bias=self.pc("fcb%d" % l, jf, jf + 1))
            else:
                for p, (a, b) in enumerate(pieces):
                    self.tt("dve", actT[:, jf, a:b], pss[p], sv[k][:, a:b], ALU.mult, r=pks[p] + [("R", "s", k)], w=[("R", "actT", jf)])

        blocks = []
        for jp in range(0, FKT, 2):
            js = [j for j in (jp, jp + 1) if j < FKT]
            blocks += [(j * 128, ("a", j)) for j in js]
            blocks += [(DFF + j * 128, ("u", j)) for j in js]
        self.dense("w_ffn_in", l, 0, KT, blocks, self.hT_fn, 0, N, ep, wcols=256)
        if tile["prompt"] and tile["last_tile"]:
            self.dma("pool", self.o["o_cf_p"][l], self.fhist[:], r=["fhist"])
        rb = [self.rview(2 * FKT * N + k * 4 * N, N, F32) for k in range(2)]
        rkeys = [("R", "xres", 0), ("R", "xres", 1)]
        alias_r = [("R", "ae", 0), ("R", "ae", 1), ("R", "y"), ("R", "s", 0), ("R", "s", 1)]
        self.op("pool", lambda e: e.nop(), r=alias_r, w=rkeys)
        self.op("pool", lambda e: e.nop(), r=[], w=alias_r + rkeys)
        actfn = lambda kt, a, b: (actT[:, kt, a:b], ("R", "actT", kt))
        self.dense("w_ffn_out", l, 0, FKT, [(j * 128, j) for j in range(KT)], actfn, 0, N,
                   self.resid_epilogue(tile, rb, rkeys), wcols=128)

    def walloc(self, n, dt):
        nb = n * (4 if dt is F32 else 2)
        nb = (nb + 3) // 4 * 4
        v = self.rview(self.wptr, n if dt is F32 else (n + 1) // 2 * 2, dt)
        self.wptr += nb
        return v

    def dense_T(self, l, c0, ncols, h0, nh, vtok, vkey):
        cw = 256
        cnt = 0
        for cc in range(0, ncols, cw):
            wv, wk = self.load_w("w_in", l, 0, KT, c0 + cc, cw)
            for tb in range(nh // 128):
                bank = cnt % 2
                cnt += 1
                for kt in range(KT):
                    self.mm(self.PS[bank][:, 0:cw], self.hT[:, kt, h0 + tb * 128:h0 + (tb + 1) * 128], wv[:, kt, :],
                            kt == 0, kt == KT - 1, r=[wk, ("hT", kt)], w=self.psk(bank))
                self.cp("act", vtok[:, tb, cc:cc + cw], self.PS[bank][:, 0:cw], r=self.psk(bank), w=[vkey])

    def mixer_stage(self, l, ti, tile):
        N, tok0 = tile["N"], tile["tok0"]
        self.norm_stage(tile, "nmix%d" % l)
        self.P.fence(("R", "W"))
        oT = self.rview(0, KT * N, BF16).rearrange("p (k n) -> p k n", k=KT)
        self.WOFF = 2 * KT * N
        sel = self.cfg.get("mixers", ("gla", "hg", "ret", "gdn"))
        for kt in range(KT):
            if ("gla", "hg", "gdn", "ret")[kt // 4] not in sel:
                self.memset("pool", oT[:, kt, :], 0.0, w=[("R", "oT", kt)])
        for h0 in range(0, N, self.NH):
            nh = min(self.NH, N - h0)
            chunks = tile["chunks"][h0 // 64:(h0 + nh) // 64]
            if "gla" in sel:
                self.a2_gla(l, tile, h0, nh, chunks, oT)
            if "hg" in sel:
                self.a2_hg(l, tile, h0, nh, chunks, oT)
            if "ret" in sel:
                self.a2_ret(l, tile, h0, nh, chunks, oT)
            if "gdn" in sel:
                self.a2_gdn(l, tile, h0, nh, chunks, oT)
        self.P.fence("W")
        self.a3_merge(l, tile, oT)

    def state_io(self, l, mname, chunk, sidx, uidx, when):
        src = {"gla": "st_gla", "hg": "st_hg", "gdn": "st_gdn", "ret": "st_ret"}[mname]
        if when == "load":
            if chunk["stream"] == 0:
                self.memset("pool", self.S32[:, sidx, :], 0.0, w=[("S", sidx)])
                self.memset("pool", self.Sbf[:, sidx, :], 0.0, w=[("Sb", sidx)])
            else:
                s = chunk["stream"] - 1
                self.dma("sp", self.S32[:, sidx, :], self.i[src][l, s, uidx], w=[("S", sidx)])
                self.cp("act", self.Sbf[:, sidx, :], self.S32[:, sidx, :], r=[("S", sidx)], w=[("Sb", sidx)])
        else:
            if chunk["stream"] == 0:
                self.dma("pool", self.o["o_%s_p" % mname][l, uidx], self.S32[:, sidx, :], r=[("S", sidx)])
            else:
                s = chunk["stream"] - 1
                self.dma("pool", self.o["o_%s_s" % mname][l, s, uidx], self.S32[:, sidx, :], r=[("S", sidx)])

    def gla_scan(self, l, mname, mi, nh, chunks, units, vtok, vkey, oraw):
        nblk = nh // 128
        BO, BT = 5, 4
        BAq = lambda q: 2 + q % 2
        BSu = lambda u: 6 + u % 2
        att_sb = [self.walloc(128, BF16) for _ in range(4)]
        kdpA = [self.walloc(128, BF16) for _ in units]
        kdpB = [self.walloc(128, BF16) for _ in units]
        for u in range(len(units)):
            self.memset("pool", kdpA[u], 0.0, w=[("W", "kdpA", u)])
            self.memset("pool", kdpB[u], 0.0, w=[("W", "kdpB", u)])
        psT = self.PS[BT][:, :].bitcast(BF16)
        mask2 = self.cst("mask2")
        for tb in range(nblk):
            bs = slice(tb * 128, (tb + 1) * 128)
            for ui, U in enumerate(units):
                for hd in U["heads"]:
                    q = hd["h"]
                    self.mm(self.PS[BAq(q)][:, q * 128:(q + 1) * 128], hd["ktz"][:, bs], hd["qtz"][:, bs], True, True,
                            r=[hd["kkey"], hd["qkey"]], w=self.psk(BAq(q)))
                    self.tt("dve", att_sb[q], self.PS[BAq(q)][:, q * 128:(q + 1) * 128], mask2, ALU.mult,
                            r=self.psk(BAq(q)) + ["C32"], w=[("W", "att", q)])
                self.tr(psT[:, ui * 128:(ui + 1) * 128], U["kd"][:, bs], self.identb[:], r=[U["kdkey"], "identb"],
                        w=self.psk(BT, ui // 2, ui // 2 + 1))
                self.cp("act", kdpA[ui][0:64, :], psT[0:64, ui * 128:(ui + 1) * 128], r=self.psk(BT, ui // 2, ui // 2 + 1), w=[("W", "kdpA", ui)])
                self.cp("act", kdpB[ui][64:128, :], psT[64:128, ui * 128:(ui + 1) * 128], r=self.psk(BT, ui // 2, ui // 2 + 1), w=[("W", "kdpB", ui)])
            for ci in range(2):
                cidx = 2 * tb + ci
                ch = chunks[cidx]
                cs_ = slice(cidx * 64, (cidx + 1) * 64)
                for ui, U in enumerate(units):
                    sidx = U["sidx"]
                    if ch["first"]:
                        self.state_io(l, mname, ch, sidx, U["uidx"], "load")
                    for hd in U["heads"]:
                        q = hd["h"]
                        osl = self.PS[BO][:, q * 128 + ci * 64:q * 128 + ci * 64 + 64]
                        self.mm(osl, vtok[:, tb, hd["vq"] * 128:(hd["vq"] + 1) * 128], att_sb[q][:, ci * 64:(ci + 1) * 64], True, False,
                                r=[vkey, ("W", "att", q)], w=self.psk(BO, q, q + 1))
                        self.mm(osl, self.Sbf[:, sidx, :], hd["qtz"][:, cs_], False, True,
                                r=[("Sb", sidx), hd["qkey"]], w=self.psk(BO, q, q + 1))
                    kp = (kdpA if ci == 0 else kdpB)[ui]
                    kpk = ("W", "kdpA" if ci == 0 else "kdpB", ui)
                    vw = U["vw"]
                    nq = vw // 128
                    sps = self.PS[BSu(ui)][:, 0:vw]
                    spk = self.psk(BSu(ui))
                    self.mm(sps, kp, vtok[:, tb, U["vc0"]:U["vc0"] + vw], True, True, r=[kpk, vkey], w=spk)
                    dsc = U["dS"][:, cidx:cidx + 1]
                    if nq == 1:
                        self.stt(self.S32[:, sidx, :], self.S32[:, sidx, :], dsc, sps, ALU.mult, ALU.add,
                                 r=spk + [U["dskey"]], w=[("S", sidx)])
                    else:
                        self.stt(self.S32[0:64, sidx, :], self.S32[0:64, sidx, :], U["dS"][0:64, cidx:cidx + 1], sps[0:64, 0:128],
                                 ALU.mult, ALU.add, r=spk + [U["dskey"]], w=[("S", sidx)])
                        self.stt(self.S32[64:128, sidx, :], self.S32[64:128, sidx, :], U["dS"][64:128, cidx:cidx + 1], sps[64:128, 128:256],
                                 ALU.mult, ALU.add, r=spk + [U["dskey"]], w=[("S", sidx)])
                    self.cp("act", self.Sbf[:, sidx, :], self.S32[:, sidx, :], r=[("S", sidx)], w=[("Sb", sidx)])
                    if ch["last"]:
                        self.state_io(l, mname, ch, sidx, U["uidx"], "store")
            for U in units:
                for hd in U["heads"]:
                    q = hd["h"]
                    self.cp("act", oraw[q][:, bs], self.PS[BO][:, q * 128:(q + 1) * 128], r=self.psk(BO, q, q + 1), w=[("W", "oraw", q)])

    def head_norm(self, l, mi, nh, h0, oraw, sg, oT, gname, bname=None):
        BN = 6
        sq = self.walloc(nh, BF16)
        rs = self.walloc(nh, F32)
        t = self.walloc(nh, F32)
        ones = self.cst("ones")
        for h in range(4):
            ok = ("W", "oraw", h)
            src = oraw[h]
            if bname is not None:
                self.mm(self.PS[BN][:, 0:nh], ones, oraw[h], True, True, r=["C32", ok], w=self.psk(BN))
                self.stt(oraw[h], self.PS[BN][:, 0:nh], -1.0 / 128, oraw[h], ALU.mult, ALU.add, r=self.psk(BN) + [ok], w=[ok])
            self.act(sq, src, AF.Square, r=[ok], w=[("W", "hn_sq")])
            self.mm(self.PS[BN][:, 0:nh], self.onesb[:], sq, True, True, r=["onesb", ("W", "hn_sq")], w=self.psk(BN))
            self.act(rs, self.PS[BN][:, 0:nh], AF.Ln, r=self.psk(BN), w=[("W", "hn_rs")], bias=EPS, scale=1.0 / 128)
            self.act(rs, rs, AF.Exp, r=[], w=[("W", "hn_rs")], scale=-0.5)
            self.stt(t, src, self.pc(gname), rs, ALU.mult, ALU.mult, r=[ok, "PC", ("W", "hn_rs")], w=[("W", "hn_t")])
            okey = ("R", "oT", mi * 4 + h)
            if bname is None:
                self.tt("dve", oT[:, mi * 4 + h, h0:h0 + nh], t, sg[h], ALU.mult, r=[("W", "hn_t"), ("W", "sg", h)], w=[okey])
            else:
                self.stt(oT[:, mi * 4 + h, h0:h0 + nh], t, self.pc(bname), sg[h], ALU.add, ALU.mult,
                         r=[("W", "hn_t"), ("W", "sg", h), "PC"], w=[okey])

    def decay_ops(self, cs, cskey, nh, sb, eb, enb, kdec, dS, keytag):
        nch = nh // 64
        cs3 = cs.rearrange("p (c t) -> p c t", t=64)
        kd3 = kdec.rearrange("p (c t) -> p c t", t=64)
        if eb is not None:
            self.act(eb, cs, AF.Exp, r=[cskey], w=[("W", keytag, "eb")], scale=sb)
        self.act(enb, cs, AF.Exp, r=[cskey], w=[("W", keytag, "enb")], scale=-sb)
        self.act(dS[:, 0:nch], cs3[:, :, 63], AF.Exp, r=[cskey], w=[("W", keytag, "dS")], scale=sb)
        self.tt("dve", kd3, cs3[:, :, 63:64].broadcast_to([128, nch, 64]), cs3, ALU.subtract, r=[cskey], w=[("W", keytag, "kdec")])
        self.act(kdec, kdec, AF.Exp, r=[], w=[("W", keytag, "kdec")], scale=sb)

    def a2_gla(self, l, tile, h0, nh, chunks, oT):
        P = self.P
        P.fence("W")
        self.wptr = self.WOFF
        nch, nblk = nh // 64, nh // 128
        vtok = self.walloc(nblk * 512, BF16).rearrange("p (b c) -> p b c", b=nblk)
        ktz = [self.walloc(nh, BF16) for _ in range(4)]
        qtz = [self.walloc(nh, BF16) for _ in range(4)]
        kd = [self.walloc(nh, BF16) for _ in range(2)]
        sg = [self.walloc(nh, BF16) for _ in range(4)]
        oraw = [self.walloc(nh, F32) for _ in range(4)]
        dS = [self.walloc(max(nch, 2), F32) for _ in range(2)]
        lrT = self.walloc(nh, BF16)
        cs = self.walloc(nh, F32)
        eb = self.walloc(nh, F32)
        enb = self.walloc(nh, F32)
        kdec = self.walloc(nh, F32)
        tmp = self.walloc(nh, F32)
        rmask = self.cst("rmask")
        for h in range(4):
            self.memset("pool", ktz[h], 0.0, w=[("W", "ktz", h)])
            self.memset("pool", qtz[h], 0.0, w=[("W", "qtz", h)])
        self.dense_T(l, C_GLA_V, 512, h0, nh, vtok, ("W", "vtok"))

        def ep_lr(tag, pss, pks, pieces):
            self.cp("act", lrT, pss[0], r=pks[0], w=[("W", "lrT")])
        self.dense("w_in", l, 0, KT, [(C_GLA_LR + 16 - 128, 0)], self.hT_fn, h0, nh, ep_lr, banks=(0, 1))

        def ep_k(tag, pss, pks, pieces):
            jb = tag
            ps = pss[0]
            for hh in range(2):
                rs_ = slice(hh * 64, hh * 64 + 64)
                self.tt("dve", ktz[2 * jb + hh][rs_, :], ps[rs_, :], enb[rs_, :], ALU.mult,
                        r=pks[0] + [("W", "g", "enb")], w=[("W", "ktz", 2 * jb + hh)])
            self.tt("dve", kd[jb], ps, kdec, ALU.mult, r=pks[0] + [("W", "g", "kdec")], w=[("W", "kd", jb)])

        def ep_q(tag, pss, pks, pieces):
            jb = tag
            ps = pss[0]
            for hh in range(2):
                rs_ = slice(hh * 64, hh * 64 + 64)
                self.stt(qtz[2 * jb + hh][rs_, :], ps[rs_, :], 0.125, eb[rs_, :], ALU.mult, ALU.mult,
                         r=pks[0] + [("W", "g", "eb")], w=[("W", "qtz", 2 * jb + hh)])

        for jb in range(2):
            BG = 7
            self.mm(self.PS[BG][:, 0:nh], self.wgk[:, l, jb * 128:(jb + 1) * 128], lrT, True, True,
                    r=["wgk", ("W", "lrT")], w=self.psk(BG))
            self.act(tmp, self.PS[BG][:, 0:nh], AF.Exp, r=self.psk(BG) + ["drv"], w=[("W", "g", "tmp")],
                     scale=-1.0, bias=self.drv[:, 2 * l + jb:2 * l + jb + 1])
            self.act(tmp, tmp, AF.Ln, r=[], w=[("W", "g", "tmp")], bias=1.0)
            self.op("dve", lambda e, cs=cs, tmp=tmp: e.tensor_tensor_scan(out=cs, data0=rmask[:, 0:nh], data1=tmp, initial=0.0,
                                                                         op0=ALU.mult, op1=ALU.add),
                    r=[("W", "g", "tmp"), "C32"], w=[("W", "g", "cs")])
            self.decay_ops(cs, ("W", "g", "cs"), nh, -1.0 / 16, eb, enb, kdec, dS[jb], "g")
            self.dense("w_in", l, 0, KT, [(C_GLA_K + jb * 128, jb)], self.hT_fn, h0, nh, ep_k, banks=(0, 1))
            self.dense("w_in", l, 0, KT, [(C_GLA_Q + jb * 128, jb)], self.hT_fn, h0, nh, ep_q, banks=(0, 1))

        def ep_g(tag, pss, pks, pieces):
            self.act(sg[tag], pss[0], AF.Silu, r=pks[0], w=[("W", "sg", tag)])
        self.dense("w_in", l, 0, KT, [(C_GLA_G + h * 128, h) for h in range(4)], self.hT_fn, h0, nh, ep_g, banks=(0, 1))
        units = []
        for jb in range(2):
            units.append(dict(kd=kd[jb], kdkey=("W", "kd", jb), dS=dS[jb], dskey=("W", "g", "dS"), sidx=jb, uidx=jb,
                              vc0=jb * 256, vw=256,
                              heads=[dict(h=2 * jb + hh, ktz=ktz[2 * jb + hh], qtz=qtz[2 * jb + hh], kkey=("W", "ktz", 2 * jb + hh),
                                          qkey=("W", "qtz", 2 * jb + hh), vq=2 * jb + hh) for hh in range(2)]))
        if self.cfg.get("dbg") and l == 0 and tile["tok0"] == 0 and h0 == 0:
            self.dbg_out("cs", cs, [128, nh], [("W", "g", "cs")])
            self.dbg_out("eb", eb, [128, nh], [("W", "g", "eb")])
            self.dbg_out("kdec", kdec, [128, nh], [("W", "g", "kdec")])
            t32 = self.walloc(nh, F32)
            for nm, ap, key in (("ktz1", ktz[1], ("W", "ktz", 1)), ("qtz0", qtz[0], ("W", "qtz", 0)), ("kd0", kd[0], ("W", "kd", 0)), ("lrT", lrT, ("W", "lrT"))):
                t32 = self.walloc(nh, F32)
                self.cp("dve", t32, ap, r=[key], w=[("W", "dbg", nm)])
                self.dbg_out(nm, t32, [128, nh], [("W", "dbg", nm)])
            t33 = self.walloc(nblk * 512, F32)
            self.cp("dve", t33, vtok.rearrange("p b c -> p (b c)"), r=[("W", "vtok")], w=[("W", "dbg", "vtok")])
            self.dbg_out("vtok", t33, [128, nblk * 512], [("W", "dbg", "vtok")])
            self.dbg_out("dS0", dS[0], [128, 2], [("W", "g", "dS")])
        if self.cfg.get("stop") == "prep":
            return
        self.gla_scan(l, "gla", 0, nh, chunks, units, vtok, ("W", "vtok"), oraw)
        if self.cfg.get("dbg") and l == 0 and tile["tok0"] == 0 and h0 == 0:
            self.dbg_out("oraw0", oraw[0], [128, nh], [("W", "oraw", 0)])
            self.dbg_out("S0", self.S32[:, 0, :], [128, 128], [("S", 0)])
        if self.cfg.get("stop") == "scan":
            return
        self.head_norm(l, 0, nh, h0, oraw, sg, oT, "gla_ng%d" % l)

    def a2_hg(self, l, tile, h0, nh, chunks, oT):
        self.P.fence("W")
        self.wptr = self.WOFF
        nch, nblk = nh // 64, nh // 128
        vtok = self.walloc(nblk * 512, BF16).rearrange("p (b c) -> p b c", b=nblk)
        ktz = [self.walloc(nh, BF16) for _ in range(4)]
        qtz = [self.walloc(nh, BF16) for _ in range(4)]
        kd = [self.walloc(nh, BF16) for _ in range(4)]
        sg = [self.walloc(nh, BF16) for _ in range(4)]
        oraw = [self.walloc(nh, F32) for _ in range(4)]
        dS = [self.walloc(max(nch, 2), F32) for _ in range(4)]
        cs = self.walloc(nh, F32)
        eb = [self.walloc(nh, F32) for _ in range(2)]
        enb = self.walloc(nh, F32)
        kdec = self.walloc(nh, F32)
        tA = self.walloc(nh, F32)
        tB = self.walloc(nh, F32)
        rmask = self.cst("rmask")
        self.dense_T(l, C_HG_I, 512, h0, nh, vtok, ("W", "vtok"))

        def ep(tag, pss, pks, pieces):
            kind, h = tag
            ps = pss[0]
            if kind == "f":
                lb = self.drv[:, 4 + 4 * l + h:5 + 4 * l + h]
                oml = self.drv[:, 12 + 4 * l + h:13 + 4 * l + h]
                self.act(tA, ps, AF.Sigmoid, r=pks[0], w=[("W", "h", "tA")])
                self.ts("dve", tA, tA, oml, lb, ALU.mult, ALU.add, r=["drv"], w=[("W", "h", "tA")])
                self.act(tB, tA, AF.Ln, r=[("W", "h", "tA")], w=[("W", "h", "tB")])
                self.op("dve", lambda e: e.tensor_tensor_scan(out=cs, data0=rmask[:, 0:nh], data1=tB, initial=0.0,
                                                              op0=ALU.mult, op1=ALU.add),
                        r=[("W", "h", "tB"), "C32"], w=[("W", "h", "cs")])
                self.ts("dve", tA, tA, -1.0, 1.0, ALU.mult, ALU.add, r=[], w=[("W", "h", "tA")])
                self.decay_ops_h(cs, nh, eb[h % 2], enb, kdec, dS[h], h)
                self.tt("pool", ktz[h], tA, enb, ALU.mult, r=[("W", "h", "tA"), ("W", "h", "enb")], w=[("W", "ktz", h)])
                self.tt("pool", kd[h], tA, kdec, ALU.mult, r=[("W", "h", "tA"), ("W", "h", "kdec")], w=[("W", "kd", h)])
            elif kind == "q":
                self.act(tB, ps, AF.Silu, r=pks[0], w=[("W", "h", "tB")])
                self.tt("dve", qtz[h], tB, eb[h % 2], ALU.mult, r=[("W", "h", "tB"), ("W", "h", "eb", h % 2)], w=[("W", "qtz", h)])
            else:
                self.act(sg[h], ps, AF.Silu, r=pks[0], w=[("W", "sg", h)])
        blocks = []
        for h in range(4):
            blocks += [(C_HG_F + h * 128, ("f", h)), (C_HG_Q + h * 128, ("q", h))]
        blocks += [(C_HG_G + h * 128, ("g", h)) for h in range(4)]
        self.dense("w_in", l, 0, KT, blocks, self.hT_fn, h0, nh, ep, banks=(0, 1))
        units = [dict(kd=kd[h], kdkey=("W", "kd", h), dS=dS[h], dskey=("W", "h", "dS", h), sidx=2 + h, uidx=h, vc0=h * 128, vw=128,
                      heads=[dict(h=h, ktz=ktz[h], qtz=qtz[h], kkey=("W", "ktz", h), qkey=("W", "qtz", h), vq=h)]) for h in range(4)]
        self.gla_scan(l, "hg", 1, nh, chunks, units, vtok, ("W", "vtok"), oraw)
        self.head_norm(l, 1, nh, h0, oraw, sg, oT, "hg_ng%d" % l)

    def decay_ops_h(self, cs, nh, eb, enb, kdec, dS, h):
        nch = nh // 64
        cs3 = cs.rearrange("p (c t) -> p c t", t=64)
        kd3 = kdec.rearrange("p (c t) -> p c t", t=64)
        ck = ("W", "h", "cs")
        self.act(eb, cs, AF.Exp, r=[ck], w=[("W", "h", "eb", h % 2)])
        self.act(enb, cs, AF.Exp, r=[ck], w=[("W", "h", "enb")], scale=-1.0)
        self.act(dS[:, 0:nch], cs3[:, :, 63], AF.Exp, r=[ck], w=[("W", "h", "dS", h)])
        self.tt("dve", kd3, cs3[:, :, 63:64].broadcast_to([128, nch, 64]), cs3, ALU.subtract, r=[ck], w=[("W", "h", "kdec")])
        self.act(kdec, kdec, AF.Exp, r=[], w=[("W", "h", "kdec")])

    def load_w_swapped(self, l, c0):
        s = self.wrr
        self.wrr = (self.wrr + 1) % len(self.wslot)
        view = self.wslot[s][:, 0:KT * 128].rearrange("p (k c) -> p k c", k=KT)
        rk = [("wb", "w_in", l, k) for k in range(KT)]
        for (d0, s0) in ((0, c0 + 64), (64, c0)):
            src = self.wb["w_in"][l, :, s0:s0 + 64].rearrange("(k p) c -> p k c", p=128)
            self.dma("sp", view[:, :, d0:d0 + 64], src, r=rk, w=[("w", s)])
        return view, ("w", s)

    def a2_ret(self, l, tile, h0, nh, chunks, oT):
        self.P.fence("W")
        self.wptr = self.WOFF
        nch, nblk = nh // 64, nh // 128
        vtok = self.walloc(nblk * 512, BF16).rearrange("p (b c) -> p b c", b=nblk)
        ktz = [self.walloc(nh, BF16) for _ in range(4)]
        qtz = [self.walloc(nh, BF16) for _ in range(4)]
        kd = [self.walloc(nh, BF16) for _ in range(4)]
        sg = [self.walloc(nh, BF16) for _ in range(4)]
        oraw = [self.walloc(nh, F32) for _ in range(4)]
        dS = [self.walloc(max(nch, 2), F32) for _ in range(4)]
        rc = self.walloc(nh, F32)
        rsn = self.walloc(nh, F32)
        t1 = [self.walloc(nh, F32) for _ in range(2)]
        t2 = self.walloc(nh, F32)
        xb = [self.walloc(nh, BF16) for _ in range(2)]
        if tile["prompt"]:
            p0 = tile["tok0"] + h0
            self.dma("sp", rc, self.i["rotc"][:, p0:p0 + nh], w=[("W", "rc")])
            self.dma("sp", rsn, self.i["rots"][:, p0:p0 + nh], w=[("W", "rsn")])
        else:
            for c in range(nch):
                self.dma("sp", rc[:, c * 64:(c + 1) * 64], self.i["rotc"][:, self.TP:self.TP + 64], w=[("W", "rc")])
                self.dma("sp", rsn[:, c * 64:(c + 1) * 64], self.i["rots"][:, self.TP:self.TP + 64], w=[("W", "rsn")])
        for h in range(4):
            self.memset("pool", dS[h], math.exp(64 * math.log(1.0 - 2.0 ** (-5.0 - h))), w=[("W", "r", "dS", h)])
        self.dense_T(l, C_RET_V, 512, h0, nh, vtok, ("W", "vtok"))

        def tab(name, h):
            o, w = self.clay[name]
            return self.C32[:, o + h * 64:o + (h + 1) * 64].unsqueeze(1).broadcast_to([128, nch, 64])
        v3 = lambda ap: ap.rearrange("p (c t) -> p c t", t=64)

        def ep(tag, pss, pks, pieces):
            kind, h = tag
            ps = pss[0]
            i_ = 0 if kind[0] == "q" else 1
            if kind in ("q", "k"):
                self.cp("act", xb[i_], ps, r=pks[0], w=[("W", "r", "xb", i_)])
                self.tt("dve", t1[i_], ps, rc, ALU.mult, r=pks[0] + [("W", "rc")], w=[("W", "r", "t1", i_)])
            elif kind in ("qs", "ks"):
                self.tt("dve", t2, ps, rsn, ALU.mult, r=pks[0] + [("W", "rsn")], w=[("W", "r", "t2")])
                self.tt("pool", t1[i_], t1[i_], t2, ALU.add, r=[("W", "r", "t2")], w=[("W", "r", "t1", i_)])
                if kind == "qs":
                    self.tt("pool", v3(qtz[h]), v3(t1[i_]), tab("reteb", h), ALU.mult, r=["C32"], w=[("W", "qtz", h)])
                else:
                    self.tt("pool", v3(ktz[h]), v3(t1[i_]), tab("retenb", h), ALU.mult, r=["C32"], w=[("W", "ktz", h)])
                    self.tt("pool", v3(kd[h]), v3(t1[i_]), tab("retkd", h), ALU.mult, r=["C32"], w=[("W", "kd", h)])
            else:
                self.act(sg[h], ps, AF.Silu, r=pks[0], w=[("W", "sg", h)])

        for h in range(4):
            for (cbase, kind) in ((C_RET_Q, "q"), (C_RET_K, "k")):
                self.dense("w_in", l, 0, KT, [(cbase + h * 128, (kind, h))], self.hT_fn, h0, nh, ep, banks=(0, 1))
                bank = 6 + (h % 2)
                self.mm(self.PS[bank][:, 0:nh], self.rswapb[:], xb[0 if kind == "q" else 1], True, True,
                        r=["rswapb", ("W", "r", "xb", 0 if kind == "q" else 1)], w=self.psk(bank))
                ep((kind + "s", h), [self.PS[bank][:, 0:nh]], [self.psk(bank)], [(0, nh)])
        self.dense("w_in", l, 0, KT, [(C_RET_G + h * 128, ("g", h)) for h in range(4)], self.hT_fn, h0, nh, ep, banks=(0, 1))
        units = [dict(kd=kd[h], kdkey=("W", "kd", h), dS=dS[h], dskey=("W", "r", "dS", h), sidx=10 + h, uidx=h, vc0=h * 128, vw=128,
                      heads=[dict(h=h, ktz=ktz[h], qtz=qtz[h], kkey=("W", "ktz", h), qkey=("W", "qtz", h), vq=h)]) for h in range(4)]
        self.gla_scan(l, "ret", 3, nh, chunks, units, vtok, ("W", "vtok"), oraw)
        self.head_norm(l, 3, nh, h0, oraw, sg, oT, "ret_ng%d" % l, "ret_nb%d" % l)

    def a2_gdn(self, l, tile, h0, nh, chunks, oT):
        self.P.fence("W")
        self.wptr = self.WOFF
        nch, nblk = nh // 64, nh // 128
        if tile["nseg"] == 1:
            nsg, L = 1, nh
        else:
            nsg, L = nch, 64
        v4 = lambda ap: ap.rearrange("p (h n) -> p h n", h=4)
        qhat = v4(self.walloc(4 * nh, BF16))
        khat = v4(self.walloc(4 * nh, BF16))
        vT = v4(self.walloc(4 * nh, BF16))
        sg = [self.walloc(nh, BF16) for _ in range(4)]
        oraw = [self.walloc(nh, F32) for _ in range(4)]
        XW = nsg * (L + 3)
        xe = [self.walloc(XW + (XW % 2), F32)[:, 0:XW].rearrange("p (s t) -> p s t", s=nsg) for _ in range(2)]
        yv = self.walloc(nh, F32)
        sqv = self.walloc(nh, F32)
        sqb = self.walloc(nh, BF16)
        BTt = self.walloc(nh, F32)
        CB = self.walloc(nh, F32)
        RS = self.walloc(nh, F32)
        EBT = self.walloc(nh, F32)
        KDT = self.walloc(nh, F32)
        dSg = self.walloc(4 * max(nch, 2), F32)
        rmask = self.cst("rmask")
        ones = self.cst("ones")
        for t_, nm in ((BTt, "BT"), (CB, "CB"), (RS, "RS"), (EBT, "EBT"), (KDT, "KDT")):
            self.memset("pool", t_, 0.0, w=[("W", nm)])
        gcw = lambda j, blk: self.pc("gcw%d" % l, j * 12 + blk, j * 12 + blk + 1)
        if tile["prompt"] and tile["first_tile"] and h0 == 0:
            self.memset("pool", self.ghist[:], 0.0, w=["ghist"])
        s_first = chunks[0]["stream"] - 1
        BN = 7

        def ep(tag, pss, pks, pieces):
            kind, idx = tag
            ps = pss[0]
            if kind == "x":
                blk = idx
                k = blk % 2
                x_k, xk = xe[k], ("W", "xe", k)
                if tile["prompt"]:
                    self.cp("pool", x_k[:, 0, 0:3], self.ghist[:, blk, :], r=["ghist"], w=[xk])
                else:
                    self.dma("sp", x_k[:, :, 0:3], self.i["c_gdnT"][l, s_first:s_first + nsg, :, blk, :].rearrange("s p j -> p s j"), w=[xk])
                self.cp("act", x_k[:, :, 3:3 + L], ps.rearrange("p (s t) -> p s t", s=nsg), r=pks[0], w=[xk])
                y3 = yv.rearrange("p (s t) -> p s t", s=nsg)
                yk = ("W", "yv")
                self.ts1("dve", y3, x_k[:, :, 0:L], gcw(0, blk), ALU.mult, r=[xk, "PC"], w=[yk])
                for j in range(1, 4):
                    self.stt(y3, x_k[:, :, j:j + L], gcw(j, blk), y3, ALU.mult, ALU.add, r=[xk, "PC"], w=[yk])
                if tile["prompt"]:
                    self.cp("pool", self.ghist[:, blk, :], x_k[:, 0, L:L + 3], r=[xk], w=["ghist"])
                else:
                    self.dma("pool", self.o["o_cg_s"][l, s_first:s_first + nsg, :, blk, :].rearrange("s p j -> p s j"), x_k[:, :, L:L + 3], r=[xk])
                h = blk % 4
                if blk < 8:
                    dst = qhat if blk < 4 else khat
                    self.act(dst[:, h, :], yv, AF.Silu, r=[yk], w=[("W", "qhat" if blk < 4 else "khat")])
                else:
                    self.act(vT[:, h, :], yv, AF.Silu, r=[yk], w=[("W", "vT")])
            elif kind == "z":
                self.act(sg[idx], ps, AF.Silu, r=pks[0], w=[("W", "sg", idx)])
            else:
                R32 = slice(0, 32)
                self.act(BTt[R32, :], ps[R32, :], AF.Sigmoid, r=pks[0], w=[("W", "BT")])
                self.act(RS[R32, :], ps[R32, :], AF.Exp, r=pks[0] + ["PC"], w=[("W", "RS")], bias=self.pc("dtb%d" % l)[R32, :])
                self.act(RS[R32, :], RS[R32, :], AF.Ln, r=[], w=[("W", "RS")], bias=1.0)
                self.ts1("dve", RS[R32, :], RS[R32, :], self.drv[R32, 20 + l:21 + l], ALU.mult, r=["drv"], w=[("W", "RS")])
                self.op("dve", lambda e: e.tensor_tensor_scan(out=CB[R32, :], data0=rmask[R32, 0:nh], data1=RS[R32, :], initial=0.0,
                                                              op0=ALU.mult, op1=ALU.add), r=["C32"], w=[("W", "CB")])
                c3 = lambda ap: ap.rearrange("p (c t) -> p c t", t=64)
                self.tt("dve", c3(RS)[R32], c3(CB)[R32, :, 63:64].broadcast_to([32, nch, 64]), c3(CB)[R32], ALU.subtract,
                        r=[("W", "CB")], w=[("W", "RS")])
                self.act(EBT[R32, :], CB[R32, :], AF.Exp, r=[("W", "CB")], w=[("W", "EBT")])
                self.act(KDT[R32, :], RS[R32, :], AF.Exp, r=[("W", "RS")], w=[("W", "KDT")])

        blocks = [(C_GDN_B, ("ba", 0))] + [(C_GDN_QKV + b * 128, ("x", b)) for b in (4, 5, 6, 7, 0, 1, 2, 3, 8, 9, 10, 11)]
        blocks += [(C_GDN_Z + h * 128, ("z", h)) for h in range(4)]
        self.dense("w_in", l, 0, KT, blocks, self.hT_fn, h0, nh, ep, banks=(0, 1))
        if tile["prompt"] and tile["last_tile"] and h0 + nh == tile["N"]:
            self.dma("pool", self.o["o_cg_p"][l], self.ghist[:], r=["ghist"])
        for blk in range(8):
            h = blk % 4
            dst, dk = (qhat, ("W", "qhat")) if blk < 4 else (khat, ("W", "khat"))
            self.act(sqb, dst[:, h, :], AF.Square, r=[dk], w=[("W", "sqb")])
            self.mm(self.PS[BN][:, 0:nh], self.onesb[:], sqb, True, True, r=["onesb", ("W", "sqb")], w=self.psk(BN))
            self.act(sqv, self.PS[BN][:, 0:nh], AF.Ln, r=self.psk(BN), w=[("W", "sqv")], bias=EPS)
            self.act(sqv, sqv, AF.Exp, r=[], w=[("W", "sqv")], scale=-0.5)
            if blk < 4:
                self.stt(dst[:, h, :], dst[:, h, :], 128 ** -0.5, sqv, ALU.mult, ALU.mult, r=[("W", "sqv")], w=[dk])
            else:
                self.tt("dve", dst[:, h, :], dst[:, h, :], sqv, ALU.mult, r=[("W", "sqv")], w=[dk])
        osel = self.clay["onesel"][0]
        onesel = lambda q: self.C32[:, osel + q * 128:osel + (q + 1) * 128]
        sel = self.cst("sel")
        c3 = lambda ap: ap.rearrange("p (c t) -> p c t", t=64)
        for h in range(4):
            self.mm(self.PS[BN][:, h * nch:(h + 1) * nch], onesel(h), c3(EBT)[:, :, 63], True, True, r=["C32", ("W", "EBT")], w=self.psk(BN, 0, 1))
        self.cp("act", dSg[:, 0:4 * nch], self.PS[BN][:, 0:4 * nch], r=self.psk(BN, 0, 1), w=[("W", "dSg")])
        f4 = lambda: self.walloc(512, F32).rearrange("p (h n) -> p h n", h=4)
        b4 = lambda: self.walloc(512, BF16).rearrange("p (h n) -> p h n", h=4)
        colsb = self.walloc(20, F32)
        nbe = self.walloc(4, F32)
        g1, gA, gBs, gBi, X32, bv, tA_ = f4(), f4(), f4(), f4(), f4(), f4(), f4()
        tB_ = tA_
        attd, kdpA, kdpB, qe, u_sb, Am, Bm, Xm, A2, B2, rhs_sb = b4(), b4(), b4(), b4(), b4(), b4(), b4(), b4(), b4(), b4(), b4()
        self.memset("pool", kdpA, 0.0, w=[("W", "kdpA")])
        self.memset("pool", kdpB, 0.0, w=[("W", "kdpB")])
        bc4 = lambda ap: ap.unsqueeze(1).broadcast_to([128, 4, 128])
        colb = lambda q: colsb[:, 4 * q:4 * q + 4].unsqueeze(2).broadcast_to([128, 4, 128])
        psbank = lambda b: self.PS[b][:, :].rearrange("p (h n) -> p h n", h=4)
        psT = self.PS[2][:, :].bitcast(BF16)
        psTk = psT[:, 0:512].rearrange("p (h n) -> p h n", h=4)
        psTv = psT[:, 512:1024].rearrange("p (h n) -> p h n", h=4)
        for tb in range(nblk):
            bs = slice(tb * 128, (tb + 1) * 128)
            for q, (X, xn, sl) in enumerate(((CB, "CB", 0), (BTt, "BT", 4), (EBT, "EBT", 0), (KDT, "KDT", 0))):
                self.mm(self.PS[2][:, 4 * q:4 * q + 4], X[:, bs], sel[:, sl:sl + 4], True, True, r=[("W", xn), "C32"], w=self.psk(2))
            self.cp("act", colsb[:, 0:16], self.PS[2][:, 0:16], r=self.psk(2), w=[("W", "colsb")])
            self.stt(nbe, colsb[:, 4:8], -1.0, colsb[:, 8:12], ALU.mult, ALU.mult, r=[("W", "colsb")], w=[("W", "nbe")])
            self.ts1("dve", colsb[:, 16:20], colsb[:, 4:8], -1.0, ALU.mult, r=[], w=[("W", "colsb")])
            for h in range(4):
                self.mm(self.PS[3][:, h * 128:(h + 1) * 128], onesel(h), CB[:, bs], True, True, r=["C32", ("W", "CB")], w=self.psk(3))
                self.mm(self.PS[4][:, h * 128:(h + 1) * 128], onesel(4 + h), BTt[:, bs], True, True, r=["C32", ("W", "BT")], w=self.psk(4))
                self.mm(self.PS[5][:, h * 128:(h + 1) * 128], onesel(h), EBT[:, bs], True, True, r=["C32", ("W", "EBT")], w=self.psk(5))
            self.tt("dve", g1, psbank(3), colb(0), ALU.subtract, r=self.psk(3) + [("W", "colsb")], w=[("W", "g1")])
            self.stt(gA, g1, 0.0, bc4(self.cst("nml")), ALU.max, ALU.add, r=[("W", "g1"), "C32"], w=[("W", "gA")])
            self.stt(gBs, g1, 0.0, bc4(self.cst("nmu")), ALU.min, ALU.subtract, r=[("W", "g1"), "C32"], w=[("W", "gBs")])
            self.stt(gBi, g1, 0.0, bc4(self.cst("pmui")), ALU.min, ALU.subtract, r=[("W", "g1"), "C32"], w=[("W", "gBi")])
            self.act(gA, gA, AF.Exp, r=[], w=[("W", "gA")], scale=-1.0)
            self.act(gBs, gBs, AF.Exp, r=[], w=[("W", "gBs")])
            self.act(gBi, gBi, AF.Exp, r=[], w=[("W", "gBi")])
            for h in range(4):
                self.mm(self.PS[6][:, h * 128:(h + 1) * 128], khat[:, h, bs], khat[:, h, bs], True, True, r=[("W", "khat")], w=self.psk(6))
                self.mm(self.PS[7][:, h * 128:(h + 1) * 128], khat[:, h, bs], qhat[:, h, bs], True, True, r=[("W", "khat"), ("W", "qhat")], w=self.psk(7))
            self.tt("dve", tA_, psbank(6), gA, ALU.mult, r=self.psk(6) + [("W", "gA")], w=[("W", "tA_")])
            self.tt("dve", Am, tA_, colb(4), ALU.mult, r=[("W", "colsb")], w=[("W", "Am")])
            self.tt("dve", tB_, psbank(6), gBs, ALU.mult, r=self.psk(6) + [("W", "gBs")], w=[("W", "tB_")])
            self.stt(Bm, tB_, -1.0, psbank(4), ALU.mult, ALU.mult, r=self.psk(4), w=[("W", "Bm")])
            self.tt("dve", Xm, Bm, bc4(self.cst("ident")), ALU.add, r=["C32"], w=[("W", "Xm")])
            self.tt("dve", attd, psbank(7), gBi, ALU.mult, r=self.psk(7) + [("W", "gBi")], w=[("W", "attd")])
            self.tt("dve", qe, qhat[:, :, bs], psbank(5), ALU.mult, r=self.psk(5) + [("W", "qhat")], w=[("W", "qe")])
            for j in range(5):
                for h in range(4):
                    self.mm(self.PS[3][:, h * 128:(h + 1) * 128], Bm[:, h, :], Am[:, h, :], True, True, r=[("W", "Am"), ("W", "Bm")], w=self.psk(3))
                if j < 4:
                    for h in range(4):
                        self.mm(self.PS[4][:, h * 128:(h + 1) * 128], Am[:, h, :], Bm[:, h, :], True, True, r=[("W", "Am"), ("W", "Bm")], w=self.psk(4))
                self.cp("act", A2, psbank(3), r=self.psk(3), w=[("W", "A2")])
                if j < 4:
                    self.cp("dve", B2, psbank(4), r=self.psk(4), w=[("W", "B2")])
                for h in range(4):
                    self.mm(self.PS[5][:, h * 128:(h + 1) * 128], A2[:, h, :], Xm[:, h, :], True, True, r=[("W", "A2"), ("W", "Xm")], w=self.psk(5))
                self.tt("dve", Xm, Xm, psbank(5), ALU.add, r=self.psk(5), w=[("W", "Xm")])
                Am, A2 = A2, Am
                if j < 4:
                    Bm, B2 = B2, Bm
                self.op("pool", lambda e: e.nop(), r=[("W", "Am"), ("W", "A2"), ("W", "Bm"), ("W", "B2")],
                        w=[("W", "Am"), ("W", "A2"), ("W", "Bm"), ("W", "B2")])
            for h in range(4):
                self.tr(psTk[:, h, :], khat[:, h, bs], self.identb[:], r=[("W", "khat"), "identb"], w=self.psk(2, 0, 2))
                self.tr(psTv[:, h, :], vT[:, h, bs], self.identb[:], r=[("W", "vT"), "identb"], w=self.psk(2, 2, 4))
            kdc = colsb[:, 12:16].unsqueeze(2).broadcast_to([128, 4, 128])
            self.tt("dve", kdpA[0:64], psTk[0:64], kdc[0:64], ALU.mult, r=self.psk(2, 0, 2) + [("W", "colsb")], w=[("W", "kdpA")])
            self.tt("dve", kdpB[64:128], psTk[64:128], kdc[64:128], ALU.mult, r=self.psk(2, 0, 2) + [("W", "colsb")], w=[("W", "kdpB")])
            self.tt("dve", bv, psTv, colb(1), ALU.mult, r=self.psk(2, 2, 4) + [("W", "colsb")], w=[("W", "bv")])
            for ci in range(2):
                cidx = 2 * tb + ci
                ch = chunks[cidx]
                for h in range(4):
                    sidx = 6 + h
                    if ch["first"]:
                        self.state_io(l, "gdn", ch, sidx, h, "load")
                    hs = slice(h * 128, (h + 1) * 128)
                    self.mm(self.PS[6][:, hs], khat[:, h, bs], self.Sbf[:, sidx, :], True, True, r=[("W", "khat"), ("Sb", sidx)], w=self.psk(6, h, h + 1))
                    self.stt(rhs_sb[:, h, :], self.PS[6][:, hs], nbe[:, h:h + 1], bv[:, h, :], ALU.mult, ALU.add,
                             r=self.psk(6, h, h + 1) + [("W", "nbe"), ("W", "bv")], w=[("W", "rhs", h)])
                    self.mm(self.PS[7][:, hs], Xm[:, h, :], rhs_sb[:, h, :], True, True, r=[("W", "Xm"), ("W", "rhs", h)], w=self.psk(7, h, h + 1))
                    self.cp("act", u_sb[:, h, :], self.PS[7][:, hs], r=self.psk(7, h, h + 1), w=[("W", "u", h)])
                    osl = self.PS[3][:, h * 128 + ci * 64:h * 128 + ci * 64 + 64]
                    self.mm(osl, u_sb[:, h, :], attd[:, h, ci * 64:(ci + 1) * 64], True, False, r=[("W", "u", h), ("W", "attd")], w=self.psk(3, h, h + 1))
                    self.mm(osl, self.Sbf[:, sidx, :], qe[:, h, ci * 64:(ci + 1) * 64], False, True, r=[("Sb", sidx), ("W", "qe")], w=self.psk(3, h, h + 1))
                    kp = kdpA if ci == 0 else kdpB
                    self.mm(self.PS[4][:, hs], kp[:, h, :], u_sb[:, h, :], True, True, r=[("W", "kdpA" if ci == 0 else "kdpB"), ("W", "u", h)], w=self.psk(4, h, h + 1))
                    self.stt(self.S32[:, sidx, :], self.S32[:, sidx, :], dSg[:, h * nch + cidx:h * nch + cidx + 1], self.PS[4][:, hs],
                             ALU.mult, ALU.add, r=self.psk(4, h, h + 1) + [("W", "dSg")], w=[("S", sidx)])
                    self.cp("act", self.Sbf[:, sidx, :], self.S32[:, sidx, :], r=[("S", sidx)], w=[("Sb", sidx)])
                    if ch["last"]:
                        self.state_io(l, "gdn", ch, sidx, h, "store")
            for h in range(4):
                self.cp("act", oraw[h][:, bs], self.PS[3][:, h * 128:(h + 1) * 128], r=self.psk(3, h, h + 1), w=[("W", "oraw", h)])
        self.head_norm(l, 2, nh, h0, oraw, sg, oT, "gdn_ng%d" % l)

    def a3_merge(self, l, tile, oT):
        N, tok0 = tile["N"], tile["tok0"]
        mixT = self.rview(2 * KT * N, KT * N, BF16).rearrange("p (k n) -> p k n", k=KT)
        base = 4 * KT * N
        acc = [self.rview(base + j * 4 * N, N, F32) for j in range(4)]
        base += 16 * N
        sgt = [self.rview(base + k * 4 * N, N, F32) for k in range(2)]
        base += 8 * N
        aux = self.rview(base, 16 * 512, BF16).rearrange("p (k c) -> p k c", k=16)
        ofn = lambda kt, a, b: (oT[:, kt, a:b], ("R", "oT", kt))
        pieces = [(a, min(N, a + 512)) for a in range(0, N, 512)]
        cnt = 0
        for jg in range(4):
            for n in range(4):
                self.dma("sp", aux[:, n * 4:(n + 1) * 4, :],
                         self.wb["w_branch"][l, n * 512:(n + 1) * 512, jg * 512:(jg + 1) * 512].rearrange("(k p) c -> p k c", p=128),
                         r=[("wb", "w_branch", l, n * 4 + k) for k in range(4)], w=[("W", "aux")])
            for n in range(4):
                for half in range(2):
                    wv, wk = self.load_w("w_in", l, 0, KT, C_MERGE + n * D + jg * 512 + half * 256, 256)
                    for jj in range(2):
                        jl = half * 2 + jj
                        j = jg * 4 + jl
                        gb = [(2 * (cnt % 2) + p) for p in range(len(pieces))]
                        bb = [4 + (2 * (cnt % 2) + p) for p in range(len(pieces))]
                        cnt += 1
                        for kt in range(KT):
                            for p, (a, b) in enumerate(pieces):
                                self.mm(self.PS[gb[p]][:, 0:b - a], wv[:, kt, jj * 128:(jj + 1) * 128], self.hT[:, kt, a:b],
                                        kt == 0, kt == KT - 1, r=[wk, ("hT", kt)], w=self.psk(gb[p]))
                        for kk in range(4):
                            for p, (a, b) in enumerate(pieces):
                                self.mm(self.PS[bb[p]][:, 0:b - a], aux[:, n * 4 + kk, jl * 128:(jl + 1) * 128], oT[:, n * 4 + kk, a:b],
                                        kk == 0, kk == 3, r=[("W", "aux"), ("R", "oT", n * 4 + kk)], w=self.psk(bb[p]))
                        sk = cnt % 2
                        for p, (a, b) in enumerate(pieces):
                            self.act(sgt[sk][:, a:b], self.PS[gb[p]][:, 0:b - a], AF.Sigmoid, r=self.psk(gb[p]), w=[("W", "sgt", sk)])
                            if n == 0:
                                self.tt("dve", acc[jl][:, a:b], self.PS[bb[p]][:, 0:b - a], sgt[sk][:, a:b], ALU.mult,
                                        r=self.psk(bb[p]) + [("W", "sgt", sk)], w=[("W", "acc", jl)])
                            else:
                                self.tt("dve", sgt[sk][:, a:b], self.PS[bb[p]][:, 0:b - a], sgt[sk][:, a:b], ALU.mult,
                                        r=self.psk(bb[p]), w=[("W", "sgt", sk)])
                                if n < 3:
                                    self.tt("pool", acc[jl][:, a:b], acc[jl][:, a:b], sgt[sk][:, a:b], ALU.add,
                                            r=[("W", "sgt", sk)], w=[("W", "acc", jl)])
                                else:
                                    self.tt("pool", mixT[:, j, a:b], acc[jl][:, a:b], sgt[sk][:, a:b], ALU.add,
                                            r=[("W", "sgt", sk), ("W", "acc", jl)], w=[("W", "mixT", j)])
        rb = sgt
        rkeys = [("W", "sgt", 0), ("W", "sgt", 1)]
        mfn = lambda kt, a, b: (mixT[:, kt, a:b], ("W", "mixT", kt))
        self.dense("w_out", l, 0, KT, [(j * 128, j) for j in range(KT)], mfn, 0, N, self.resid_epilogue(tile, rb, rkeys),
                   banks=(0, 1, 2, 3))


PAST_LEN = 4096


def core_inputs(inp, c, cfg, shared):
    TP, NS = cfg["TP"], cfg["NS"]
    f = lambda a: np.ascontiguousarray(np.asarray(a, np.float32))
    m = dict(shared)
    xp = np.asarray(inp["x_prompt"])
    if c < xp.shape[0]:
        m["x_p"] = f(xp[c, :TP])
    else:
        m["x_p"] = np.zeros((TP, D), np.float32)
    s0, s1 = c * NS, (c + 1) * NS
    m["x_s"] = f(np.asarray(inp["x_sample"])[s0:s1].reshape(NS * 64, D))
    m["st_gla"] = f(np.asarray(inp["state_gla"])[:, s0:s1].reshape(-1, NS, 2, 128, 128))
    m["st_hg"] = f(np.asarray(inp["state_hgrn"])[:, s0:s1])
    m["st_gdn"] = f(np.asarray(inp["state_gdn"])[:, s0:s1])
    m["st_ret"] = f(np.asarray(inp["state_ret"])[:, s0:s1])
    cg = np.asarray(inp["cache_gdn_conv"])[:, s0:s1]
    m["c_gdnT"] = f(cg.reshape(cg.shape[0], NS, 3, 12, 128).transpose(0, 1, 4, 3, 2))
    cf = np.asarray(inp["cache_ffn_conv"])[:, s0:s1]
    m["c_ffnT"] = f(cf.reshape(cf.shape[0], NS, 2, FKT, 128).transpose(0, 1, 4, 3, 2))
    return m


def shared_inputs(inp, cfg):
    TP = cfg["TP"]
    NH = min(512, cfg["NT"])
    f = lambda a: np.ascontiguousarray(np.asarray(a, np.float32))
    sh = {}
    sh["pcols"] = make_pcols(inp)
    sh["consts"] = make_consts(NH)
    sh["wgk"] = f(inp["gla_w_gk"])
    rc, rs = rot_tables(TP, PAST_LEN)
    sh["rotc"], sh["rots"] = rc, rs
    sh["w_in"] = f(inp["w_in"])
    wbr = np.asarray(inp["w_branch"], np.float32)
    sh["w_branch"] = np.ascontiguousarray(wbr.reshape(wbr.shape[0], 4 * 512, D))
    sh["w_out"] = f(inp["w_out"])
    sh["w_ffn_in"] = f(inp["w_ffn_in"])
    sh["w_ffn_out"] = f(inp["w_ffn_out"])
    return sh


_NC_CACHE = {}


def get_nc(cfg):
    key = repr(sorted(cfg.items()))
    if key not in _NC_CACHE:
        b = Builder(cfg)
        nc = b.build()
        _NC_CACHE[key] = (nc, b)
    return _NC_CACHE[key]


def kernel(**inputs):
    cfg = dict(TP=8192, NS=4, NT=1024, DEPTH=2)
    nc, b = get_nc(cfg)
    sh = shared_inputs(inputs, cfg)
    in_maps = [core_inputs(inputs, c, cfg, sh) for c in range(8)]
    res = run_bass_kernel_spmd(nc, in_maps, core_ids=list(range(8)))
    R = res.results
    L = cfg["DEPTH"]
    y_p = np.stack([R[c]["y_p"] for c in range(2)])
    y_s = np.concatenate([R[c]["y_s"].reshape(4, 64, D) for c in range(8)])
    outs = [y_p, y_s]
    outs.append(np.stack([R[c]["o_gla_p"].reshape(L, 4, 64, 128) for c in range(2)], axis=1))
    for n in ("hg", "gdn", "ret"):
        outs.append(np.stack([R[c]["o_%s_p" % n] for c in range(2)], axis=1))
    outs.append(np.stack([R[c]["o_cg_p"].transpose(0, 3, 2, 1).reshape(L, 3, 1536) for c in range(2)], axis=1))
    outs.append(np.stack([R[c]["o_cf_p"].transpose(0, 3, 2, 1).reshape(L, 2, DFF) for c in range(2)], axis=1))
    outs.append(np.concatenate([R[c]["o_gla_s"].reshape(L, 4, 4, 64, 128) for c in range(8)], axis=1))
    for n in ("hg", "gdn", "ret"):
        outs.append(np.concatenate([R[c]["o_%s_s" % n] for c in range(8)], axis=1))
    outs.append(np.concatenate([R[c]["o_cg_s"].transpose(0, 1, 4, 3, 2).reshape(L, 4, 3, 1536) for c in range(8)], axis=1))
    outs.append(np.concatenate([R[c]["o_cf_s"].transpose(0, 1, 4, 3, 2).reshape(L, 4, 2, DFF) for c in range(8)], axis=1))
    return tuple(np.ascontiguousarray(o, dtype=np.float32) for o in outs)
```
